# Optimizing a Trainium2 kernel written in Bass

```python
import math
import jax
import jax.numpy as jnp
from jax import lax
import numpy as np

D_MODEL = 1024
BATCH = 2
SEQ = 8192
DEPTH = 2

GRID_W = 64
CTX_LEN = 256
N_EVEN = (DEPTH + 1) // 2
N_ODD = DEPTH // 2
NORM_EPS = 1e-6

SSD_WIDTH = D_MODEL
SSD_HEAD_DIM = 64
SSD_HEADS = SSD_WIDTH // SSD_HEAD_DIM
SSD_GROUPS = 4
SSD_STATE = 128
SSD_CONV = 5
SSD_CHUNK = 128
CONV_CH = SSD_WIDTH + 2 * SSD_GROUPS * SSD_STATE

S5_WIDTH = D_MODEL // 2
S5_GROUP = 16
S5_GROUPS = S5_WIDTH // S5_GROUP
S5_STATE = 64
DT_MIN = 1e-3
DT_MAX = 1e-1

EVEN_CUTS = [SSD_WIDTH, SSD_WIDTH + CONV_CH, SSD_WIDTH + CONV_CH + 2 * SSD_HEADS,
             SSD_WIDTH + CONV_CH + 2 * SSD_HEADS + S5_WIDTH]
EVEN_IN = EVEN_CUTS[-1] + S5_WIDTH
EVEN_OUT = SSD_WIDTH + S5_WIDTH

ATTN_HEAD_DIM = 64
ATTN_Q_HEADS = D_MODEL // ATTN_HEAD_DIM
ATTN_KV_HEADS = 4
ATTN_Q_BLOCK = 128
Q_W = ATTN_Q_HEADS * ATTN_HEAD_DIM
KV_W = ATTN_KV_HEADS * ATTN_HEAD_DIM
ODD_CUTS = [Q_W, Q_W + KV_W, Q_W + 2 * KV_W]
ODD_IN = 2 * Q_W + 2 * KV_W
ROPE_PAIRS = ATTN_HEAD_DIM // 4
ROPE_THETA = 10000.0

kernel_name = 'hybrid_ssd_s5_gqa_prefix_dit'


def rms_norm(x):
    xf = x.astype(jnp.float32)
    return (xf * lax.rsqrt(jnp.mean(xf * xf, axis=-1, keepdims=True) + NORM_EPS)).astype(x.dtype)


def centred_depthwise_conv(x, w, b):
    ch = x.shape[-1]
    kern = w.T[:, None, :].astype(x.dtype)
    y = lax.conv_general_dilated(x, kern, window_strides=(1,),
                                 padding=[(SSD_CONV // 2, SSD_CONV // 2)],
                                 dimension_numbers=('NWC', 'WIO', 'NWC'),
                                 feature_group_count=ch)
    return y + b


def bidirectional(fn, ctx_parts, lat_parts):
    n_ctx = ctx_parts[0].shape[1]
    rev = lambda a: jnp.flip(a, axis=1)
    y_f = fn(0, *[jnp.concatenate([pc, pl], axis=1) for pc, pl in zip(ctx_parts, lat_parts)])
    y_b = fn(1, *[jnp.concatenate([rev(pc), rev(pl)], axis=1) for pc, pl in zip(ctx_parts, lat_parts)])
    y_ctx = y_f[:, :n_ctx] + rev(y_b[:, :n_ctx])
    y_lat = y_f[:, n_ctx:] + rev(y_b[:, n_ctx:])
    return y_ctx, y_lat


def ssd_chunked(xs, dt, a, bm, cm):
    b, t, h, p = xs.shape
    nc = t // SSD_CHUNK
    r = lambda z: z.reshape(b, nc, SSD_CHUNK, *z.shape[2:])
    loga = r(dt * a)
    xd = r(xs * dt[..., None])
    bm, cm = r(bm), r(cm)
    a_cs = jnp.cumsum(loga, axis=2)
    seg = a_cs[:, :, :, None, :] - a_cs[:, :, None, :, :]
    causal = jnp.tril(jnp.ones((SSD_CHUNK, SSD_CHUNK), dtype=bool))[None, None, :, :, None]
    lmat = jnp.exp(jnp.where(causal, seg, -jnp.inf))
    scores = jnp.einsum('bclhn,bcshn->bclsh', cm, bm) * lmat
    y_diag = jnp.einsum('bclsh,bcshp->bclhp', scores, xd)
    decay_to_end = jnp.exp(a_cs[:, :, -1:, :] - a_cs)
    chunk_states = jnp.einsum('bclhn,bclh,bclhp->bchpn', bm, decay_to_end, xd)
    chunk_decay = jnp.exp(a_cs[:, :, -1, :])

    def step(carry, inp):
        st, dec = inp
        return carry * dec[..., None, None] + st, carry

    init = jnp.zeros_like(chunk_states[:, 0])
    _, prev = lax.scan(step, init, (jnp.moveaxis(chunk_states, 1, 0), jnp.moveaxis(chunk_decay, 1, 0)))
    prev = jnp.moveaxis(prev, 0, 1)
    y_off = jnp.einsum('bclhn,bchpn,bclh->bclhp', cm, prev, jnp.exp(a_cs))
    return (y_diag + y_off).reshape(b, t, h, p)


def _complex_affine_combine(e1, e2):
    a1r, a1i, b1r, b1i = e1
    a2r, a2i, b2r, b2i = e2
    ar = a2r * a1r - a2i * a1i
    ai = a2r * a1i + a2i * a1r
    br = a2r * b1r - a2i * b1i + b2r
    bi = a2r * b1i + a2i * b1r + b2i
    return ar, ai, br, bi


def s5_scan(u, lam_re, lam_im, log_step, b_re, b_im, c_re, c_im):
    f32 = jnp.float32
    lr, li = lam_re.astype(f32), lam_im.astype(f32)
    step = jnp.exp(log_step.astype(f32))[:, None]
    mag = jnp.exp(lr * step)
    ab_re, ab_im = mag * jnp.cos(li * step), mag * jnp.sin(li * step)
    den = lr * lr + li * li
    f_re = ((ab_re - 1.0) * lr + ab_im * li) / den
    f_im = (ab_im * lr - (ab_re - 1.0) * li) / den
    br, bi = b_re.astype(f32), b_im.astype(f32)
    bb_re = f_re[..., None] * br - f_im[..., None] * bi
    bb_im = f_re[..., None] * bi + f_im[..., None] * br
    uf = u.astype(f32)
    bu_re = jnp.einsum('btgi,gpi->btgp', uf, bb_re)
    bu_im = jnp.einsum('btgi,gpi->btgp', uf, bb_im)
    a_re = jnp.broadcast_to(ab_re, bu_re.shape)
    a_im = jnp.broadcast_to(ab_im, bu_im.shape)
    _, _, h_re, h_im = lax.associative_scan(_complex_affine_combine, (a_re, a_im, bu_re, bu_im), axis=1)
    return (jnp.einsum('btgp,gip->btgi', h_re, c_re.astype(f32))
            - jnp.einsum('btgp,gip->btgi', h_im, c_im.astype(f32)))


def ssm_mixer(a_ctx, a_lat, w_in, conv_w, conv_b, dt_bias, a_log, d_ssd, ssd_norm,
              lam_re, lam_im, log_step, b_re, b_im, c_re, c_im, d_s5, glu_w, glu_b, w_out):
    def project(h):
        z, xbc, dt_raw, u, g = jnp.split(h @ w_in, EVEN_CUTS, axis=-1)
        xbc = jax.nn.silu(centred_depthwise_conv(xbc, conv_w, conv_b))
        return z, xbc, dt_raw, u, g

    zc, xbc_c, dtc, uc, gc = project(a_ctx)
    zl, xbc_l, dtl, ul, gl = project(a_lat)
    rep = SSD_HEADS // SSD_GROUPS

    def ssd_dir(d, xbc, dt_raw):
        b, t = xbc.shape[:2]
        xs, bm, cm = jnp.split(xbc, [SSD_WIDTH, SSD_WIDTH + SSD_GROUPS * SSD_STATE], axis=-1)
        xs = xs.reshape(b, t, SSD_HEADS, SSD_HEAD_DIM)
        bm = jnp.repeat(bm.reshape(b, t, SSD_GROUPS, SSD_STATE), rep, axis=2)
        cm = jnp.repeat(cm.reshape(b, t, SSD_GROUPS, SSD_STATE), rep, axis=2)
        dt = jax.nn.softplus(dt_raw.reshape(b, t, 2, SSD_HEADS)[:, :, d].astype(jnp.float32)
                             + dt_bias[d].astype(jnp.float32))
        a = -jnp.exp(a_log[d].astype(jnp.float32))
        return ssd_chunked(xs, dt, a, bm, cm).reshape(b, t, SSD_WIDTH)

    ys_c, ys_l = bidirectional(ssd_dir, (xbc_c, dtc), (xbc_l, dtl))
    d_full = jnp.repeat(d_ssd, SSD_HEAD_DIM)

    def ssd_out(y, xbc, z):
        y = y + d_full * xbc[..., :SSD_WIDTH]
        return rms_norm(y * jax.nn.silu(z)) * ssd_norm

    def s5_dir(d, u):
        b, t = u.shape[:2]
        y = s5_scan(u.reshape(b, t, S5_GROUPS, S5_GROUP), lam_re[d], lam_im[d], log_step[d],
                    b_re, b_im, c_re[d], c_im[d])
        return y.reshape(b, t, S5_WIDTH)

    y5_c, y5_l = bidirectional(s5_dir, (uc,), (ul,))

    def s5_out(y, u, g):
        y = jax.nn.gelu(y + d_s5 * u)
        y = y * jax.nn.sigmoid(y @ glu_w + glu_b)
        return y * jax.nn.silu(g)

    out_c = jnp.concatenate([ssd_out(ys_c, xbc_c, zc), s5_out(y5_c, uc, gc)], axis=-1) @ w_out
    out_l = jnp.concatenate([ssd_out(ys_l, xbc_l, zl), s5_out(y5_l, ul, gl)], axis=-1) @ w_out
    return out_c, out_l


def axial_rope_tables(n_tokens):
    rows = n_tokens // GRID_W
    row = jnp.repeat(jnp.arange(rows), GRID_W).astype(jnp.float32)
    col = jnp.tile(jnp.arange(GRID_W), rows).astype(jnp.float32)
    inv = ROPE_THETA ** (-jnp.arange(ROPE_PAIRS, dtype=jnp.float32) / ROPE_PAIRS)
    ang = jnp.concatenate([row[:, None] * inv, col[:, None] * inv], axis=-1)
    return jnp.cos(ang), jnp.sin(ang)


def apply_axial_rope(x, cos, sin):
    b, n, h, _ = x.shape
    xr = x.reshape(b, n, h, 2, 2, ROPE_PAIRS)
    x1, x2 = xr[..., 0, :], xr[..., 1, :]
    cs = cos.reshape(n, 2, ROPE_PAIRS)[None, :, None]
    sn = sin.reshape(n, 2, ROPE_PAIRS)[None, :, None]
    out = jnp.stack([x1 * cs - x2 * sn, x2 * cs + x1 * sn], axis=-2)
    return out.reshape(x.shape).astype(x.dtype)


def blocked_attention(q, k, v):
    b, t = q.shape[:2]
    nb = t // ATTN_Q_BLOCK
    grp = ATTN_Q_HEADS // ATTN_KV_HEADS
    qb = q.reshape(b, nb, ATTN_Q_BLOCK, ATTN_KV_HEADS, grp, ATTN_HEAD_DIM).swapaxes(0, 1)
    scale = ATTN_HEAD_DIM ** -0.5

    def block(qi):
        s = jnp.einsum('bqkgd,bskd->bkgqs', qi, k).astype(jnp.float32) * scale
        p = jax.nn.softmax(s, axis=-1).astype(v.dtype)
        return jnp.einsum('bkgqs,bskd->bqkgd', p, v)

    o = lax.map(block, qb)
    return o.swapaxes(0, 1).reshape(b, t, Q_W)


def attention_mixer(a_ctx, a_lat, w_in, q_gain, k_gain, w_out, cos, sin, need_ctx):
    b, n_lat = a_lat.shape[:2]
    n_ctx = a_ctx.shape[1]
    heads = lambda z, n, h: z.reshape(b, n, h, ATTN_HEAD_DIM)
    q_l, k_l, v_l, g_l = jnp.split(a_lat @ w_in, ODD_CUTS, axis=-1)
    q_l = apply_axial_rope(rms_norm(heads(q_l, n_lat, ATTN_Q_HEADS)) * q_gain, cos, sin)
    k_l = apply_axial_rope(rms_norm(heads(k_l, n_lat, ATTN_KV_HEADS)) * k_gain, cos, sin)
    v_l = heads(v_l, n_lat, ATTN_KV_HEADS)
    if need_ctx:
        q_c, k_c, v_c, g_c = jnp.split(a_ctx @ w_in, ODD_CUTS, axis=-1)
    else:
        k_c, v_c = jnp.split(a_ctx @ w_in[:, Q_W:Q_W + 2 * KV_W], [KV_W], axis=-1)
    k_c = rms_norm(heads(k_c, n_ctx, ATTN_KV_HEADS)) * k_gain
    v_c = heads(v_c, n_ctx, ATTN_KV_HEADS)
    k_all = jnp.concatenate([k_c, k_l], axis=1)
    v_all = jnp.concatenate([v_c, v_l], axis=1)
    o_l = blocked_attention(q_l, k_all, v_all)
    out_l = (o_l * jax.nn.silu(g_l)) @ w_out
    out_c = None
    if need_ctx:
        q_c = rms_norm(heads(q_c, n_ctx, ATTN_Q_HEADS)) * q_gain
        o_c = blocked_attention(q_c, k_c, v_c)
        out_c = (o_c * jax.nn.silu(g_c)) @ w_out
    return out_c, out_l


def setup_inputs(seed: int = 0) -> dict:
    key = jax.random.key(seed)
    k = jax.random.split(key, 32)
    f32 = jnp.float32
    nrm = lambda i, shape, s: jax.random.normal(k[i], shape, f32) * s
    log_u = lambda i, shape, lo, hi: jax.random.uniform(k[i], shape, f32, math.log(lo), math.log(hi))
    dt0 = jnp.exp(log_u(9, (N_EVEN, 2, SSD_HEADS), DT_MIN, DT_MAX))
    lam_im0 = jnp.pi * jnp.arange(S5_STATE, dtype=f32)
    return {
        'x': nrm(0, (BATCH, SEQ, D_MODEL), 1.0),
        'c': nrm(1, (BATCH, D_MODEL), 1.0),
        'ctx': nrm(2, (BATCH, CTX_LEN, D_MODEL), 1.0),
        'c_ctx': nrm(3, (D_MODEL,), 1.0),
        'ada_w': nrm(4, (DEPTH, D_MODEL, 3 * D_MODEL), 0.5 * D_MODEL ** -0.5),
        'ada_b': nrm(5, (DEPTH, 3 * D_MODEL), 0.01),
        'ev_w_in': nrm(6, (N_EVEN, D_MODEL, EVEN_IN), D_MODEL ** -0.5),
        'ev_conv_w': nrm(7, (N_EVEN, CONV_CH, SSD_CONV), SSD_CONV ** -0.5),
        'ev_conv_b': nrm(8, (N_EVEN, CONV_CH), 0.01),
        'ev_dt_bias': dt0 + jnp.log(-jnp.expm1(-dt0)),
        'ev_a_log': jnp.log(jax.random.uniform(k[10], (N_EVEN, 2, SSD_HEADS), f32, 1.0, 16.0)),
        'ev_d_ssd': 1.0 + nrm(11, (N_EVEN, SSD_HEADS), 0.01),
        'ev_ssd_norm': 1.0 + nrm(12, (N_EVEN, SSD_WIDTH), 0.01),
        'ev_lam_re': -0.5 + nrm(13, (N_EVEN, 2, S5_GROUPS, S5_STATE), 0.01),
        'ev_lam_im': lam_im0 + nrm(14, (N_EVEN, 2, S5_GROUPS, S5_STATE), 0.01),
        'ev_log_step': log_u(15, (N_EVEN, 2, S5_GROUPS), DT_MIN, DT_MAX),
        'ev_b_re': nrm(16, (N_EVEN, S5_GROUPS, S5_STATE, S5_GROUP), (2 * S5_GROUP) ** -0.5),
        'ev_b_im': nrm(17, (N_EVEN, S5_GROUPS, S5_STATE, S5_GROUP), (2 * S5_GROUP) ** -0.5),
        'ev_c_re': nrm(18, (N_EVEN, 2, S5_GROUPS, S5_GROUP, S5_STATE), (2 * S5_STATE) ** -0.5),
        'ev_c_im': nrm(19, (N_EVEN, 2, S5_GROUPS, S5_GROUP, S5_STATE), (2 * S5_STATE) ** -0.5),
        'ev_d_s5': nrm(20, (N_EVEN, S5_WIDTH), 1.0),
        'ev_glu_w': nrm(21, (N_EVEN, S5_WIDTH, S5_WIDTH), S5_WIDTH ** -0.5),
        'ev_glu_b': nrm(22, (N_EVEN, S5_WIDTH), 0.01),
        'ev_w_out': nrm(23, (N_EVEN, EVEN_OUT, D_MODEL), EVEN_OUT ** -0.5),
        'od_w_in': nrm(24, (N_ODD, D_MODEL, ODD_IN), D_MODEL ** -0.5),
        'od_q_gain': 1.0 + nrm(25, (N_ODD, ATTN_HEAD_DIM), 0.01),
        'od_k_gain': 1.0 + nrm(26, (N_ODD, ATTN_HEAD_DIM), 0.01),
        'od_w_out': nrm(27, (N_ODD, Q_W, D_MODEL), Q_W ** -0.5),
        'final_gain': 1.0 + nrm(28, (D_MODEL,), 0.01),
    }


def reference(x, c, ctx, c_ctx, ada_w, ada_b, ev_w_in, ev_conv_w, ev_conv_b, ev_dt_bias, ev_a_log,
              ev_d_ssd, ev_ssd_norm, ev_lam_re, ev_lam_im, ev_log_step, ev_b_re, ev_b_im, ev_c_re,
              ev_c_im, ev_d_s5, ev_glu_w, ev_glu_b, ev_w_out, od_w_in, od_q_gain, od_k_gain,
              od_w_out, final_gain):
    cos, sin = axial_rope_tables(x.shape[1])
    sc = jax.nn.silu(c)
    scc = jax.nn.silu(c_ctx)
    h_lat, h_ctx = x, ctx
    for i in range(DEPTH):
        shift, scale, gate = jnp.split(sc @ ada_w[i] + ada_b[i], 3, axis=-1)
        shift_c, scale_c, gate_c = jnp.split(scc @ ada_w[i] + ada_b[i], 3, axis=-1)
        a_lat = rms_norm(h_lat) * (1.0 + scale[:, None]) + shift[:, None]
        a_ctx = rms_norm(h_ctx) * (1.0 + scale_c) + shift_c
        last = i == DEPTH - 1
        j = i // 2
        if i % 2 == 0:
            o_ctx, o_lat = ssm_mixer(a_ctx, a_lat, ev_w_in[j], ev_conv_w[j], ev_conv_b[j], ev_dt_bias[j],
                                     ev_a_log[j], ev_d_ssd[j], ev_ssd_norm[j], ev_lam_re[j], ev_lam_im[j],
                                     ev_log_step[j], ev_b_re[j], ev_b_im[j], ev_c_re[j], ev_c_im[j],
                                     ev_d_s5[j], ev_glu_w[j], ev_glu_b[j], ev_w_out[j])
        else:
            o_ctx, o_lat = attention_mixer(a_ctx, a_lat, od_w_in[j], od_q_gain[j], od_k_gain[j],
                                           od_w_out[j], cos, sin, not last)
        h_lat = h_lat + gate[:, None] * o_lat
        if not last:
            h_ctx = h_ctx + gate_c * o_ctx
    return rms_norm(h_lat) * final_gain
```

```python
import numpy as np
from contextlib import ExitStack
import concourse.bass as bass
import concourse.mybir as mybir
from concourse.bass_utils import run_bass_kernel_spmd
import ml_dtypes

BF = ml_dtypes.bfloat16

F32 = mybir.dt.float32
BF16 = mybir.dt.bfloat16
AF = mybir.ActivationFunctionType
ALU = mybir.AluOpType
AX = mybir.AxisListType

D = 1024
NCTX = 256
NLAT = 8192
T = NCTX + NLAT
NCH = T // 128
EPS = 1e-6
NEG = -30000.0
SKIP_SAME_ENGINE_WAITS_IN_SSD = False
PENGB = 'dve'
PENG4 = 'dve'


class Sched:
    def __init__(self, nc, es, same_engine_sync=True):
        self.nc, self.es = nc, es
        self.sem_es = es
        self.prefix = ''
        self.E = dict(pe=nc.tensor, dve=nc.vector, act=nc.scalar, pool=nc.gpsimd, sp=nc.sync)
        self.sems, self.cnt = {}, {}
        self.seen = {e: {} for e in self.E}
        self.lw, self.rd = {}, {}
        self.same = same_engine_sync
        self.nwait = 0
        self.nins = 0
        self.dma_sems = set()
        self.psum_keys = set()
        for e in ('pe', 'dve', 'act', 'pool'):
            self._mk(e)

    def _mk(self, name):
        self.sems[name] = self.sem_es.enter_context(self.nc.semaphore('s_' + name))
        self.cnt[name] = 0

    def sb(self, name, shape, dt=F32, es=None):
        return (es or self.es).enter_context(self.nc.sbuf_tensor('sb_' + self.prefix + name, list(shape), dt))

    def ps(self, name, shape, dt=F32, es=None):
        self.psum_keys.add(name)
        return (es or self.es).enter_context(self.nc.psum_tensor('ps_' + self.prefix + name, list(shape), dt))

    def _deps(self, eng, reads, writes):
        need = {}

        def add(tok, key=None):
            sn, v = tok
            if sn == eng and key is not None and self.bulk_on and self._bulk(key):
                return
            if sn in self.dma_sems:
                v = self.cnt[sn]
            if v > need.get(sn, 0):
                need[sn] = v
        for k in reads:
            if k in self.lw:
                add(self.lw[k], k)
            if k in self.psum_keys:
                for r in self.rd.get(k, ()):
                    if r[0] != eng:
                        add(r)
        for k in writes:
            if k in self.lw:
                add(self.lw[k], k)
            for r in self.rd.get(k, ()):
                add(r, k)
        E = self.E[eng]
        for sn, v in need.items():
            if sn == eng and (eng == 'pe' or not self.same):
                continue
            if self.seen[eng].get(sn, 0) >= v:
                continue
            E.wait_ge(self.sems[sn], v)
            self.nwait += 1
            self.seen[eng][sn] = v

    BULK = ('aT', 'pre', 'acc', 'xpT', 'BT', 'CT', 'xs', 'Btok', 'zsb', 'pcm', 'ptm', 'np_xt', 'np_xn', 'np_junk', 'np_pT', 'pX',
            'xdf', 'xdb', 'xde', 'dg', 'LT', 'MT', 'pSeg', 'pG', 'pY', 'pO', 'pS', 't1', 't2', 't3', 'ysb', 'stf', 'stb', 'prevf', 'prevb',
            'uT', 'U', 'pU', 'gsb', 'S_re', 'S_im', 'G_re', 'G_im', 'RT', 'r_ta', 'r_tb', 'Xin_', 'pSr', 'pSi', 'pYb', 'Yb', 'y5T', 'ytk', 'pTy',
            'tt', 'Fb', 'vv', 'vb', 'sg5', 'sgg', 'vT', 'FT', 'h1_', 'a1T', 'qf', 'qs', 'qr', 'qo', 'ko', 'vo', 'sgo', 'qzt', 'ktt', 'pTv', 'pF', 'pA',
            'PT', 'OTs', 'pOT', 'og', 'ogg', 'ogT', 'pTg', 'h2', 'osb', 'junkc', 'wb', 'wst', 'w1', 'wout', 'gluw', 'wo', 'aTs', 'q_tw', 'q_WT', 'q_Km',
            'q_Q', 'q_V', 'q_Qz', 'pW', 'Tbf', 'Wz', 'Vz')
    _bulk_cache = {}
    bulk_on = False

    def _bulk(self, key):
        r = self._bulk_cache.get(key)
        if r is None:
            r = any(key == b or (key.startswith(b) and (key[len(b):].isdigit() or key[len(b):].replace('_', '').isdigit() or b.endswith('_')))
                    for b in self.BULK)
            self._bulk_cache[key] = r
        return r

    def _done(self, tok, reads, writes):
        for k in reads:
            self.rd.setdefault(k, []).append(tok)
        for k in writes:
            self.lw[k] = tok
            self.rd[k] = []

    marked = False
    nmark = 0
    skip_after = 1 << 60
    force = False

    def mark(self):
        self.marked = True

    def _skip(self):
        if not self.marked or self.force:
            return False
        self.nmark += 1
        return self.nmark > self.skip_after

    def op(self, eng, fn, reads=(), writes=()):
        if self._skip():
            return
        self._deps(eng, reads, writes)
        ins = fn(self.E[eng])
        self.cnt[eng] += 1
        ins.then_inc(self.sems[eng], 1)
        self.nins += 1
        self._done((eng, self.cnt[eng]), reads, writes)

    def dma(self, eng, sem, out, in_, reads=(), writes=(), **kw):
        if self._skip():
            return
        if sem not in self.sems:
            self._mk(sem)
            self.dma_sems.add(sem)
        self._deps(eng, reads, writes)
        ins = self.E[eng].dma_start(out=out, in_=in_, **kw)
        self.cnt[sem] += 16
        ins.then_inc(self.sems[sem], 16)
        self.nins += 1
        self._done((sem, self.cnt[sem]), reads, writes)

    def wait_all(self, eng, keys):
        self._deps(eng, list(keys), [])

    def barrier(self):
        for eng, E in self.E.items():
            for sn, v in self.cnt.items():
                if v == 0 or self.seen[eng].get(sn, 0) >= v:
                    continue
                if sn == eng and eng == 'pe':
                    continue
                E.wait_ge(self.sems[sn], v)
                self.nwait += 1
                self.seen[eng][sn] = v


def make_identity(s, name, dt):
    idf = s.sb(name + '_f', [128, 128], F32)
    s.op('pool', lambda e: e.memset(idf[:], 1.0), writes=[name + '_f'])
    s.op('pool', lambda e: e.affine_select(out=idf[:], in_=idf[:], pattern=[[-1, 128]], compare_op=ALU.is_equal,
                                           fill=0.0, base=0, channel_multiplier=1), reads=[name + '_f'], writes=[name + '_f'])
    if dt == F32:
        return idf, name + '_f'
    idb = s.sb(name, [128, 128], dt)
    s.op('dve', lambda e: e.tensor_copy(out=idb[:], in_=idf[:]), reads=[name + '_f'], writes=[name])
    return idb, name


def phase0_mod(s, nc, cvec_d, adaw_d, adab_d, nj):
    modT = s.sb('modT', [128, nj, 2])
    with ExitStack() as tes:
        cv = s.sb('cv', [128, 8, 2], es=tes)
        scv = s.sb('scv', [128, 8, 2], es=tes)
        ab = s.sb('ab', [128, nj], es=tes)
        aw = s.sb('aw', [128, 8, nj * 128], es=tes)
        pm = s.ps('pm', [128, 512], es=tes)
        s.dma('sp', 'dm_c', cv[:], cvec_d[:, :, :], writes=['cv'])
        s.dma('sp', 'dm_c', ab[:], adab_d[:, :], writes=['ab'])
        for k in range(8):
            s.dma('sp' if k % 2 == 0 else 'act', 'dm_aw', aw[:, k, :], adaw_d[:, k, :], writes=['aw%d' % k])
        s.op('act', lambda e: e.activation(out=scv[:], in_=cv[:], func=AF.Silu), reads=['cv'], writes=['scv'])
        for j in range(nj):
            for k in range(8):
                s.op('pe', lambda e: e.matmul(pm[:, 2 * j:2 * j + 2], lhsT=aw[:, k, j * 128:(j + 1) * 128], rhs=scv[:, k, :],
                                              start=(k == 0), stop=(k == 7)), reads=['aw%d' % k, 'scv'], writes=['pm'])
        s.op('dve', lambda e: e.tensor_tensor(out=modT[:], in0=pm[:, 0:2 * nj].rearrange("p (j t) -> p j t", t=2),
                                              in1=ab[:].unsqueeze(2).to_broadcast([128, nj, 2]), op=ALU.add),
             reads=['pm', 'ab'], writes=['modT'])
        s.barrier()
    return modT


class NormProj:
    def __init__(self, s, nc, x_d, identb, identb_key, sc1, sh, es):
        self.s, self.nc, self.x_d = s, nc, x_d
        self.identb, self.idk = identb, identb_key
        self.sc1, self.sh = sc1, sh
        self.xt = [s.sb('np_xt%d' % i, [128, D], es=es) for i in range(2)] if x_d is not None else [None, None]
        self.junk = [s.sb('np_junk%d' % i, [128, D], BF16, es=es) for i in range(2)]
        self.xn = [s.sb('np_xn%d' % i, [128, D], BF16, es=es) for i in range(2)]
        self.st = [s.sb('np_st%d' % i, [128, 8], es=es) for i in range(2)]
        self.pT = [s.ps('np_pT%d' % i, [128, 8, 128], BF16, es=es) for i in range(2)]
        self.n = 0

    def sub(self, row0, which, aT, aT_key, col0, src=None):
        s = self.s
        i = self.n % 2
        self.n += 1
        xt, xn, pT = self.xt[i], self.xn[i], self.pT[i]
        kx, kn, kp, kst = 'np_xt%d' % i, 'np_xn%d' % i, 'np_pT%d' % i, 'np_st%d' % i
        if src is None:
            s.dma('sp' if i == 0 else 'act', 'dm_x%d' % i, xt[:], self.x_d[row0:row0 + 128, :], writes=[kx])
        else:
            xt, kx = src
        st = self.st[i]
        s.op('act', lambda e: e.activation(out=self.junk[i][:], in_=xt[:], func=AF.Square, accum_out=st[:, 0:1]),
             reads=[kx], writes=['np_junk%d' % i, kst])
        s.op('dve', lambda e: e.tensor_scalar(out=st[:, 1:2], in0=st[:, 0:1], scalar1=1.0 / D, scalar2=EPS, op0=ALU.mult, op1=ALU.add),
             reads=[kst], writes=[kst])
        s.op('act', lambda e: e.activation(out=st[:, 2:3], in_=st[:, 1:2], func=AF.Sqrt), reads=[kst], writes=[kst])
        s.op('dve', lambda e: e.reciprocal(out=st[:, 3:4], in_=st[:, 2:3]), reads=[kst], writes=[kst])
        s.op('dve', lambda e: e.tensor_scalar(out=xn[:], in0=xt[:], scalar1=st[:, 3:4], scalar2=None, op0=ALU.mult),
             reads=[kx, kst], writes=[kn])
        for k in range(8):
            s.op('pe', lambda e: e.transpose(out=pT[:, k, :], in_=xn[:, k * 128:(k + 1) * 128], identity=self.identb[:]),
                 reads=[kn, self.idk], writes=[kp])
        for k in range(8):
            dst = aT[:, k, col0:col0 + 128]
            if i == 0:
                s.op('act', lambda e: e.activation(out=dst, in_=pT[:, k, :], func=AF.Identity,
                                                   bias=self.sh[:, k, which:which + 1], scale=self.sc1[:, k, which:which + 1]),
                     reads=[kp, 'modv'], writes=[aT_key])
            else:
                s.op('dve', lambda e: e.tensor_scalar(out=dst, in0=pT[:, k, :], scalar1=self.sc1[:, k, which:which + 1],
                                                      scalar2=self.sh[:, k, which:which + 1], op0=ALU.mult, op1=ALU.add),
                     reads=[kp, 'modv'], writes=[aT_key])


def load_cast_weights(s, nc, w_d, wb, wb_key, ncols, es_tmp, tag):
    stg = [s.sb('wstg%s%d' % (tag, i), [128, ncols], es=es_tmp) for i in range(2)]
    for k in range(8):
        i = k % 2
        s.dma('sp' if i == 0 else 'act', 'dm_w%s%d' % (tag, i), stg[i][:], w_d[:, k, :], writes=['wstg%s%d' % (tag, i)])
        s.op('pool', lambda e: e.tensor_copy(out=wb[:, k, :], in_=stg[i][:]), reads=['wstg%s%d' % (tag, i)], writes=[wb_key])


def emit_norm0(nc, s, P, xin, cvec_d, adaw_d, adab_d, aT_all):
    with ExitStack() as es:
        s.es = es
        s.prefix = P
        identb, idk = make_identity(s, 'identb', BF16)
        modT = phase0_mod(s, nc, cvec_d, adaw_d, adab_d, 16)
        sc1 = s.sb('sc1', [128, 8, 2])
        s.op('dve', lambda e: e.tensor_scalar(out=sc1[:], in0=modT[:, 8:16, :], scalar1=1.0, scalar2=None, op0=ALU.add), reads=['modT'], writes=['modv'])
        sh = modT[:, 0:8, :]
        npj = NormProj(s, nc, xin, identb, idk, sc1, sh, es)
        aTs = [s.sb('aTs%d' % i, [128, 8, 128], BF16) for i in range(2)]
        for c in range(NCH):
            i = c % 2
            npj.sub(c * 128, 1 if c < NCTX // 128 else 0, aTs[i], 'aTs%d' % i, 0)
            s.dma('pool', 'dm_aTs%d' % i, aT_all[:, :, c * 128:(c + 1) * 128], aTs[i][:], reads=['aTs%d' % i], writes=['aT_all'])
        s.wait_all('sp', ['aT_all'])
        s.barrier()


SSD_NCM = 4
SSD_NTM = 264
SEGS = [(0, NCTX, 1), (NCTX, NLAT, 0)]


def _finish_dbg(s, dbg_d, srcs):
    s.force = True
    with ExitStack() as de:
        dd = s.sb('ddx', [128, 4096], es=de)
        s.op('dve', lambda e: e.memset(dd[:], 0.0), writes=['ddx'])
        o = 0
        for (ap, key, n) in srcs:
            s.op('dve', lambda e: e.tensor_copy(out=dd[:, o:o + n], in_=ap), reads=[key], writes=['ddx'])
            o += n
        s.dma('sp', 'dm_o', dbg_d[:, :], dd[:], reads=['ddx'], writes=['dbg'])
        s.wait_all('sp', ['dbg'])
        s.barrier()


def emit_l0a_ssd(nc, s, P, xin, cvec_d, adaw_d, adab_d, win_d, convw_d, convb_d, dtb_d, alog_d, dssd_d, cst_d, ztok_d, yssd_d, dbg_d, stage=99, cut=None, aT_all=None):
    NW = SSD_NCM * 128 + SSD_NTM
    with ExitStack() as es:
        s.es = es
        s.prefix = P
        s.bulk_on = SKIP_SAME_ENGINE_WAITS_IN_SSD
        identb, idk = make_identity(s, 'identb', BF16)
        identf, idfk = identb, idk
        identf = None
        cst = s.sb('cst', [128, 6, 128])
        s.dma('sp', 'dm_c', cst[:], cst_d[:, :, :], writes=['cst'])
        convw = s.sb('convw', [128, 4, 5]); convb = s.sb('convb', [128, 4])
        dtb = s.sb('dtb', [128, 8]); alog = s.sb('alog', [128, 8]); dssd = s.sb('dssd', [128, 4])
        s.dma('sp', 'dm_c', convw[:], convw_d[:, :, :], writes=['convw'])
        s.dma('sp', 'dm_c', convb[:], convb_d[:, :], writes=['convb'])
        s.dma('sp', 'dm_c', dtb[:], dtb_d[:, :], writes=['dtb'])
        s.dma('sp', 'dm_c', alog[:], alog_d[:, :], writes=['alog'])
        s.dma('sp', 'dm_c', dssd[:], dssd_d[:, :], writes=['dssd'])
        if aT_all is None:
            modT = phase0_mod(s, nc, cvec_d, adaw_d, adab_d, 16)
            sc1 = s.sb('sc1', [128, 8, 2])
            s.op('dve', lambda e: e.tensor_scalar(out=sc1[:], in0=modT[:, 8:16, :], scalar1=1.0, scalar2=None, op0=ALU.add),
                 reads=['modT'], writes=['modv'])
            sh = modT[:, 0:8, :]
        if stage == 0:
            with ExitStack() as de:
                dd = s.sb('dd', [128, 4096], es=de)
                s.op('dve', lambda e: e.memset(dd[:], 0.0), writes=['dd'])
                s.op('dve', lambda e: e.tensor_copy(out=dd[:, 0:32], in_=modT[:].rearrange("p j t -> p (j t)")), reads=['modT'], writes=['dd'])
                s.dma('sp', 'dm_o', dbg_d[:, :], dd[:], reads=['dd'], writes=['dbg'])
                s.wait_all('sp', ['dbg'])
                s.barrier()
            return nc

        xpT = s.sb('xpT', [128, 2, T], BF16)
        BT = s.sb('BT', [128, T], BF16)
        CT = s.sb('CT', [128, T], BF16)
        xs = s.sb('xs', [128, NCH, 256], BF16)
        Btok = s.sb('Btok', [128, NCH, 128], BF16)
        dtraw = s.sb('dtraw', [128, NCH, 8])

        with ExitStack() as p1:
            wb = s.sb('wb', [128, 8, NW], BF16, es=p1)
            with ExitStack() as wtmp:
                load_cast_weights(s, nc, win_d, wb, 'wb', NW, wtmp, 'a')
                s.barrier()
            if stage == 11:
                _finish_dbg(s, dbg_d, [(wb[:, 3, 0:512], 'wb', 512)])
                return nc
            if aT_all is None:
                npj = NormProj(s, nc, xin, identb, idk, sc1, sh, p1)
                aT = s.sb('aT', [128, 8, 512], BF16, es=p1); kaT = 'aT'
            else:
                aTb = [s.sb('aT%d' % i_, [128, 8, 512], BF16, es=p1) for i_ in range(2)]
            tcnt = 0
            pre = s.sb('pre', [128, 4, 520], es=p1)
            acc = s.sb('acc', [128, 4, 512], es=p1)
            zsb = [s.sb('zsb%d' % i, [128, SSD_NTM], es=p1) for i in range(2)]
            pcm = [s.ps('pcm%d' % i, [128, 512], es=p1) for i in range(2)]
            ptm = [s.ps('ptm%d' % i, [128, 512], es=p1) for i in range(2)]
            ncm = 0
            ntm = 0
            dests = [lambda a, b: xpT[:, 0, a:b], lambda a, b: xpT[:, 1, a:b], lambda a, b: BT[:, a:b], lambda a, b: CT[:, a:b]]
            dkeys = ['xpT', 'xpT', 'BT', 'CT']

            def conv(g0, j_lo, j_hi):
                n = j_hi - j_lo
                for m in range(4):
                    a = acc[:, m, 0:n]
                    s.op('dve', lambda e: e.tensor_scalar(out=a, in0=pre[:, m, j_lo + 2:j_lo + 2 + n], scalar1=convw[:, m, 0:1],
                                                          scalar2=convb[:, m:m + 1], op0=ALU.mult, op1=ALU.add),
                         reads=['pre', 'convw', 'convb'], writes=['acc%d' % m])
                    for k in range(1, 5):
                        s.op('dve', lambda e: e.scalar_tensor_tensor(out=a, in0=pre[:, m, j_lo + 2 + k:j_lo + 2 + k + n],
                                                                     scalar=convw[:, m, k:k + 1], in1=a, op0=ALU.mult, op1=ALU.add),
                             reads=['pre', 'acc%d' % m], writes=['acc%d' % m])
                    s.op('act', lambda e: e.activation(out=dests[m](g0 + j_lo, g0 + j_hi), in_=a, func=AF.Silu),
                         reads=['acc%d' % m], writes=[dkeys[m]])

            for (seg0, seglen, which) in SEGS:
                ntiles = (seglen + 511) // 512
                s.op('pool', lambda e: e.memset(pre[:, :, 0:4], 0.0), reads=['pre'], writes=['pre'])
                for ti in range(ntiles):
                    t0 = seg0 + ti * 512
                    nt = min(512, seg0 + seglen - t0)
                    if aT_all is None:
                        for sub in range(nt // 128):
                            npj.sub(t0 + sub * 128, which, aT, 'aT', sub * 128)
                    else:
                        aT = aTb[tcnt % 2]; kaT = 'aT%d' % (tcnt % 2)
                        s.dma('sp' if tcnt % 2 == 0 else 'act', 'dm_' + kaT, aT[:, :, 0:nt], aT_all[:, :, t0:t0 + nt], writes=[kaT])
                        tcnt += 1
                    if stage == 12:
                        _finish_dbg(s, dbg_d, [(aT[:, 2, 0:256], 'aT', 256)])
                        return nc
                    for m in range(SSD_NCM):
                        pc = pcm[ncm % 2]; kpc = 'pcm%d' % (ncm % 2); ncm += 1
                        for k in range(8):
                            s.op('pe', lambda e: e.matmul(pc[:, 0:nt], lhsT=wb[:, k, m * 128:(m + 1) * 128], rhs=aT[:, k, 0:nt],
                                                          start=(k == 0), stop=(k == 7)), reads=['wb', kaT], writes=[kpc])
                        s.op('act', lambda e: e.activation(out=pre[:, m, 4:4 + nt], in_=pc[:, 0:nt], func=AF.Copy),
                             reads=[kpc], writes=['pre'])
                    if stage == 13:
                        _finish_dbg(s, dbg_d, [(pre[:, 2, 4:260], 'pre', 256)])
                        return nc
                    for sub in range(nt // 128):
                        pt = ptm[ntm % 2]; kpt = 'ptm%d' % (ntm % 2)
                        zb = zsb[ntm % 2]; kzb = 'zsb%d' % (ntm % 2); ntm += 1
                        for k in range(8):
                            s.op('pe', lambda e: e.matmul(pt[:, 0:SSD_NTM], lhsT=aT[:, k, sub * 128:(sub + 1) * 128],
                                                          rhs=wb[:, k, SSD_NCM * 128:NW], start=(k == 0), stop=(k == 7)),
                                 reads=['wb', kaT], writes=[kpt])
                        s.op('act', lambda e: e.activation(out=zb[:], in_=pt[:, 0:SSD_NTM], func=AF.Copy), reads=[kpt], writes=[kzb])
                        c = (t0 + sub * 128) // 128
                        s.op('pool', lambda e: e.tensor_copy(out=dtraw[:, c, :], in_=zb[:, 256:264]), reads=[kzb], writes=['dtraw'])
                        s.dma('sp', 'dm_z%d' % ((ntm - 1) % 2), ztok_d[t0 + sub * 128:t0 + (sub + 1) * 128, :], zb[:], reads=[kzb], writes=['ztok'])
                    if stage == 14:
                        _finish_dbg(s, dbg_d, [(zsb[1][:, 0:264], 'zsb1', 264)])
                        s.wait_all('sp', ['ztok'])
                        return nc
                    conv(t0, 0 if ti == 0 else -2, nt - 2)
                    if stage == 15:
                        _finish_dbg(s, dbg_d, [(BT[:, 0:254], 'BT', 254)])
                        s.wait_all('sp', ['ztok'])
                        return nc
                    s.op('pool', lambda e: e.tensor_copy(out=pre[:, :, 0:4], in_=pre[:, :, nt:nt + 4]), reads=['pre'], writes=['pre'])
                    last_t0, last_nt = t0, nt
                s.op('pool', lambda e: e.memset(pre[:, :, 4:8], 0.0), reads=['pre'], writes=['pre'])
                conv(last_t0 + last_nt, -2, 0)
            if stage == 16:
                _finish_dbg(s, dbg_d, [(BT[:, 0:1024], 'BT', 1024)])
                s.wait_all('sp', ['ztok'])
                return nc
            pX = [s.ps('pX%d' % i, [128, 512], BF16, es=p1) for i in range(2)]
            for c in range(NCH if stage not in (18, 19, 20) else 4):
                px = pX[c % 2]; kpx = 'pX%d' % (c % 2)
                if stage != 20:
                    for hh in range(2):
                        s.op('pe', lambda e: e.transpose(out=px[:, hh * 128:(hh + 1) * 128], in_=xpT[:, hh, c * 128:(c + 1) * 128], identity=identb[:]),
                             reads=['xpT', idk], writes=[kpx])
                    s.op('act', lambda e: e.activation(out=xs[:, c, :], in_=px[:, 0:256], func=AF.Copy), reads=[kpx], writes=['xs'])
                if stage != 19:
                    s.op('pe', lambda e: e.transpose(out=px[:, 256:384], in_=BT[:, c * 128:(c + 1) * 128], identity=identb[:]),
                         reads=['BT', idk], writes=[kpx])
                    s.op('act', lambda e: e.activation(out=Btok[:, c, :], in_=px[:, 256:384], func=AF.Copy), reads=[kpx], writes=['Btok'])
            if stage in (17, 18, 19, 20):
                _finish_dbg(s, dbg_d, [(xs[:, 3, :], 'xs', 256), (Btok[:, 65, :], 'Btok', 128)])
                s.wait_all('sp', ['ztok'])
                return nc
            s.barrier()
        if stage == 1:
            with ExitStack() as de:
                dd = s.sb('dd', [128, 4096], es=de)
                s.op('dve', lambda e: e.tensor_copy(out=dd[:, 0:1024], in_=xpT[:, 0, 0:1024]), reads=['xpT'], writes=['dd'])
                s.op('dve', lambda e: e.tensor_copy(out=dd[:, 1024:2048], in_=BT[:, 0:1024]), reads=['BT'], writes=['dd'])
                s.op('dve', lambda e: e.tensor_copy(out=dd[:, 2048:3072], in_=CT[:, T - 1024:T]), reads=['CT'], writes=['dd'])
                s.op('dve', lambda e: e.tensor_copy(out=dd[:, 3072:3072 + 256], in_=xs[:, 3, :]), reads=['xs'], writes=['dd'])
                s.op('dve', lambda e: e.tensor_copy(out=dd[:, 3328:3328 + 128], in_=Btok[:, 65, :]), reads=['Btok'], writes=['dd'])
                s.op('dve', lambda e: e.tensor_copy(out=dd[:, 3456:3456 + 16], in_=modT[:, :, 0]), reads=['modT'], writes=['dd'])
                s.dma('sp', 'dm_o', dbg_d[:, :], dd[:], reads=['dd'], writes=['dbg'])
                s.wait_all('sp', ['dbg', 'ztok'])
                s.barrier()
            return nc
        if cut is not None:
            s.mark(); s.skip_after = cut
        ident_f = s.sb('identf2', [128, 128])
        s.op('pool', lambda e: e.memset(ident_f[:], 1.0), writes=['identf2'])
        s.op('pool', lambda e: e.affine_select(out=ident_f[:], in_=ident_f[:], pattern=[[-1, 128]], compare_op=ALU.is_equal,
                                               fill=0.0, base=0, channel_multiplier=1), reads=['identf2'], writes=['identf2'])
        ones_f = s.sb('ones_f', [128, 128])
        s.op('pool', lambda e: e.memset(ones_f[:], 1.0), writes=['ones_f'])
        mb4 = s.sb('mb4', [128, 2, 4, 128], BF16)
        for d in range(2):
            for h in range(4):
                s.op('act', lambda e: e.activation(out=mb4[:, d, h, :], in_=cst[:, 2 + d, :], func=AF.Copy), reads=['cst'], writes=['mb4'])
        NC8 = NCH * 8
        shp = [128, 2, NCH, 4]
        dt = s.sb('dt', shp); acs = s.sb('acs', shp); nacs = s.sb('nacs', shp)
        eacs = s.sb('eacs', shp); cdec = s.sb('cdec', shp); wde = s.sb('wde', shp)
        abc = s.sb('abc', [128, 8])
        fl = lambda t, d: t[:, d, :, :].rearrange("p c h -> p (c h)")
        with ExitStack() as p2:
            tmp = s.sb('p2tmp', shp, es=p2)
            pc2l = [s.ps('pc2_%d' % i, [128, 512], es=p2) for i in range(2)]
            pe2l = [s.ps('pe2_%d' % i, [128, 512], es=p2) for i in range(2)]
            for d in range(2):
                s.op('dve', lambda e: e.tensor_tensor(out=tmp[:, d, :, :], in0=dtraw[:, :, 4 * d:4 * d + 4],
                                                      in1=dtb[:, 4 * d:4 * d + 4].unsqueeze(1).to_broadcast([128, NCH, 4]), op=ALU.add),
                     reads=['dtraw', 'dtb'], writes=['p2tmp'])
            fa_ = lambda t: t[:].rearrange("p d c h -> p (d c h)")
            sp1 = s.sb('sp1', shp, es=p2); sp2 = s.sb('sp2', shp, es=p2)
            s.op('dve', lambda e: e.tensor_scalar(out=fa_(sp1), in0=fa_(tmp), scalar1=-1.0, scalar2=None, op0=ALU.mult), reads=['p2tmp'], writes=['sp1'])
            s.op('dve', lambda e: e.tensor_tensor(out=fa_(sp1), in0=fa_(sp1), in1=fa_(tmp), op=ALU.max), reads=['sp1', 'p2tmp'], writes=['sp1'])
            s.op('act', lambda e: e.activation(out=fa_(sp2), in_=fa_(sp1), func=AF.Exp, scale=-1.0), reads=['sp1'], writes=['sp2'])
            s.op('act', lambda e: e.activation(out=fa_(sp2), in_=fa_(sp2), func=AF.Ln, bias=1.0), reads=['sp2'], writes=['sp2'])
            s.op('dve', lambda e: e.tensor_scalar(out=fa_(sp1), in0=fa_(tmp), scalar1=0.0, scalar2=None, op0=ALU.max), reads=['p2tmp', 'sp1'], writes=['sp1'])
            s.op('dve', lambda e: e.tensor_tensor(out=fa_(dt), in0=fa_(sp1), in1=fa_(sp2), op=ALU.add), reads=['sp1', 'sp2'], writes=['dt'])
            s.op('act', lambda e: e.activation(out=abc[:], in_=alog[:], func=AF.Exp), reads=['alog'], writes=['abc'])
            s.op('dve', lambda e: e.tensor_scalar(out=abc[:], in0=abc[:], scalar1=-1.0, scalar2=None, op0=ALU.mult), reads=['abc'], writes=['abc'])
            NQ = NCH * 4
            for d in range(2):
                s.op('pe', lambda e: e.matmul(pc2l[d][:, 0:NQ], lhsT=cst[:, d, :], rhs=fl(dt, d), start=True, stop=True),
                     reads=['cst', 'dt'], writes=['pc2_%d' % d])
                s.op('dve', lambda e: e.tensor_tensor(out=acs[:, d, :, :], in0=pc2l[d][:, 0:NQ].rearrange("p (c h) -> p c h", h=4),
                                                      in1=abc[:, 4 * d:4 * d + 4].unsqueeze(1).to_broadcast([128, NCH, 4]), op=ALU.mult),
                     reads=['pc2_%d' % d, 'abc'], writes=['acs'])
            for d in range(2):
                s.op('pe', lambda e: e.matmul(pe2l[d][:, 0:NQ], lhsT=cst[:, 4 + d, :], rhs=fl(acs, d), start=True, stop=True),
                     reads=['cst', 'acs'], writes=['pe2_%d' % d])
                s.op('act', lambda e: e.activation(out=fl(cdec, d), in_=pe2l[d][:, 0:NQ], func=AF.Exp), reads=['pe2_%d' % d], writes=['cdec'])
                s.op('dve', lambda e: e.scalar_tensor_tensor(out=fl(tmp, d), in0=pe2l[d][:, 0:NQ], scalar=1.0, in1=fl(acs, d), op0=ALU.mult, op1=ALU.subtract),
                     reads=['pe2_%d' % d, 'acs'], writes=['p2tmp'])
            fa = lambda t: t[:].rearrange("p d c h -> p (d c h)")
            s.op('act', lambda e: e.activation(out=fa(wde), in_=fa(tmp), func=AF.Exp), reads=['p2tmp'], writes=['wde'])
            s.op('dve', lambda e: e.tensor_tensor(out=fa(wde), in0=fa(wde), in1=fa(dt), op=ALU.mult), reads=['wde', 'dt'], writes=['wde'])
            s.op('act', lambda e: e.activation(out=fa(eacs), in_=fa(acs), func=AF.Exp), reads=['acs'], writes=['eacs'])
            s.op('dve', lambda e: e.tensor_scalar(out=fa(nacs), in0=fa(acs), scalar1=-1.0, scalar2=None, op0=ALU.mult), reads=['acs'], writes=['nacs'])
            s.barrier()
        if stage in (2, 21):
            fa = lambda t: t[:].rearrange("p d c h -> p (d c h)")
            _finish_dbg(s, dbg_d, [(fa(dt), 'dt', NC8), (fa(acs), 'acs', NC8), (fa(cdec), 'cdec', NC8), (fa(wde), 'wde', NC8)])
            s.wait_all('sp', ['ztok'])
            return nc

        prevb = s.sb('prevb', [128, NCH, 256], BF16)
        stf = s.sb('stf', [128, 256]); stb = s.sb('stb', [128, 256])
        xde = [s.sb('xde%d' % i, [128, 256], BF16) for i in range(2)]
        pS = [s.ps('pS%d' % i, [128, 512]) for i in range(2)]
        h4 = lambda ap: ap.rearrange("p (h q) -> p h q", h=4)
        bc4 = lambda ap: ap.unsqueeze(2).to_broadcast([128, 4, 64])
        s.op('pool', lambda e: e.memset(stb[:], 0.0), writes=['stb'])
        s.op('pool', lambda e: e.memset(stf[:], 0.0), writes=['stf'])
        border = [1, 0] + list(range(NCH - 1, 1, -1))
        for i, c in enumerate(border):
            xd = xde[i % 2]; kx = 'xde%d' % (i % 2); ps_ = pS[i % 2]; kps = 'pS%d' % (i % 2)
            s.op('pool', lambda e: e.tensor_tensor(out=h4(xd[:]), in0=h4(xs[:, c, :]), in1=bc4(wde[:, 1, c, :]), op=ALU.mult),
                 reads=['xs', 'wde'], writes=[kx])
            s.op('pe', lambda e: e.matmul(ps_[:, 0:256], lhsT=Btok[:, c, :], rhs=xd[:], start=True, stop=True), reads=['Btok', kx], writes=[kps])
            s.op('act', lambda e: e.activation(out=prevb[:, c, :], in_=stb[:], func=AF.Copy), reads=['stb'], writes=['prevb'])
            s.op('dve', lambda e: e.tensor_tensor(out=h4(stb[:]), in0=h4(stb[:]), in1=bc4(cdec[:, 1, c, :]), op=ALU.mult),
                 reads=['stb', 'cdec'], writes=['stb'])
            s.op('dve', lambda e: e.scalar_tensor_tensor(out=stb[:], in0=ps_[:, 0:256], scalar=1.0, in1=stb[:], op0=ALU.mult, op1=ALU.add), reads=['stb', kps], writes=['stb'])
        if stage == 3:
            _finish_dbg(s, dbg_d, [(prevb[:, 0, :], 'prevb', 256), (prevb[:, 65, :], 'prevb', 256), (prevb[:, 2, :], 'prevb', 256), (stb[:], 'stb', 256)])
            s.wait_all('sp', ['ztok'])
            return nc

        prevf = s.sb('prevf', [128, 256], BF16)
        s.op('pool', lambda e: e.memset(prevf[:], 0.0), writes=['prevf'])
        xdf = [s.sb('xdf%d' % i, [128, 256], BF16) for i in range(2)]
        xdb = [s.sb('xdb%d' % i, [128, 256], BF16) for i in range(2)]
        dg = [s.sb('dg%d' % i, [128, 4, 128]) for i in range(2)]
        LT = [s.sb('LT%d' % i, [128, 4, 128]) for i in range(2)]
        MT = [[s.sb('MT%d_%d' % (d_, i_), [128, 4, 128], BF16) for i_ in range(2)] for d_ in range(2)]
        t1 = s.sb('t1', [128, 256]); t2 = s.sb('t2', [128, 256]); t3 = s.sb('t3', [128, 256])
        ysb = [s.sb('ysb%d' % i, [128, 256]) for i in range(2)]
        pG = s.ps('pG', [128, 512]); pSeg = [s.ps('pSeg%d' % i, [128, 512]) for i in range(2)]
        pY = s.ps('pY', [128, 512]); pO = s.ps('pO', [128, 512])
        def front4(c):
            i = c % 2
            cb = slice(c * 128, (c + 1) * 128)
            s.op(PENG4, lambda e: e.tensor_tensor(out=h4(xdf[i][:]), in0=h4(xs[:, c, :]), in1=bc4(dt[:, 0, c, :]), op=ALU.mult),
                 reads=['xs', 'dt'], writes=['xdf%d' % i])
            s.op(PENG4, lambda e: e.tensor_tensor(out=h4(xdb[i][:]), in0=h4(xs[:, c, :]), in1=bc4(dt[:, 1, c, :]), op=ALU.mult),
                 reads=['xs', 'dt'], writes=['xdb%d' % i])
            s.op(PENG4, lambda e: e.tensor_tensor(out=h4(xde[i][:]), in0=h4(xs[:, c, :]), in1=bc4(wde[:, 0, c, :]), op=ALU.mult),
                 reads=['xs', 'wde'], writes=['xde%d' % i])
            s.op('pe', lambda e: e.matmul(pG[:, 0:128], lhsT=BT[:, cb], rhs=CT[:, cb], start=True, stop=True), reads=['BT', 'CT'], writes=['pG'])
            for d in range(2):
                s.op('dve', lambda e: e.tensor_tensor(out=dg[d][:], in0=ident_f[:].unsqueeze(1).to_broadcast([128, 4, 128]),
                                                      in1=acs[:, d, c, :].unsqueeze(2).to_broadcast([128, 4, 128]), op=ALU.mult),
                     reads=['identf2', 'acs'], writes=['dg%d' % d])
            for d in range(2):
                s.op('pe', lambda e: e.matmul(pSeg[d][:, :], lhsT=ones_f[:], rhs=dg[d][:].rearrange("p h l -> p (h l)"), start=True, stop=False),
                     reads=['ones_f', 'dg%d' % d], writes=['pSeg%d' % d])
                s.op('pe', lambda e: e.matmul(pSeg[d][:, :], lhsT=identb[:], rhs=mb4[:, d, :, :].rearrange("p h l -> p (h l)"), start=False, stop=True),
                     reads=[idk, 'mb4'], writes=['pSeg%d' % d])
            for d in range(2):
                for h in range(4):
                    s.op('act', lambda e: e.activation(out=LT[d][:, h, :], in_=pSeg[d][:, h * 128:(h + 1) * 128], func=AF.Exp,
                                                       bias=nacs[:, d, c, h:h + 1], scale=1.0),
                         reads=['pSeg%d' % d, 'nacs'], writes=['LT%d' % d])
            for d in range(2):
                s.op('dve', lambda e: e.tensor_tensor(out=MT[d][i][:], in0=LT[d][:], in1=pG[:, 0:128].unsqueeze(1).to_broadcast([128, 4, 128]), op=ALU.mult),
                     reads=['LT%d' % d, 'pG'], writes=['MT%d_%d' % (d, i)])
        def back4(c):
            i = c % 2
            cb = slice(c * 128, (c + 1) * 128)
            for h in range(4):
                hs = slice(h * 64, (h + 1) * 64)
                s.op('pe', lambda e: e.matmul(pY[:, hs], lhsT=MT[0][i][:, h, :], rhs=xdf[i][:, hs], start=True, stop=False),
                     reads=['MT0_%d' % i, 'xdf%d' % i], writes=['pY'])
                s.op('pe', lambda e: e.matmul(pY[:, hs], lhsT=MT[1][i][:, h, :], rhs=xdb[i][:, hs], start=False, stop=True),
                     reads=['MT1_%d' % i, 'xdb%d' % i], writes=['pY'])
            s.op('pe', lambda e: e.matmul(pO[:, 0:256], lhsT=CT[:, cb], rhs=prevf[:], start=True, stop=True), reads=['CT', 'prevf'], writes=['pO'])
            s.op('pe', lambda e: e.matmul(pO[:, 256:512], lhsT=CT[:, cb], rhs=prevb[:, c, :], start=True, stop=True), reads=['CT', 'prevb'], writes=['pO'])
            ps_ = pS[i]; kps = 'pS%d' % i
            s.op('pe', lambda e: e.matmul(ps_[:, 0:256], lhsT=Btok[:, c, :], rhs=xde[i][:], start=True, stop=True), reads=['Btok', 'xde%d' % i], writes=[kps])
            s.op('dve', lambda e: e.tensor_tensor(out=h4(stf[:]), in0=h4(stf[:]), in1=bc4(cdec[:, 0, c, :]), op=ALU.mult), reads=['stf', 'cdec'], writes=['stf'])
            s.op('dve', lambda e: e.scalar_tensor_tensor(out=stf[:], in0=ps_[:, 0:256], scalar=1.0, in1=stf[:], op0=ALU.mult, op1=ALU.add), reads=['stf', kps], writes=['stf'])
            s.op('act', lambda e: e.activation(out=prevf[:], in_=stf[:], func=AF.Copy), reads=['stf'], writes=['prevf'])
            s.op('dve', lambda e: e.tensor_tensor(out=h4(t1[:]), in0=h4(pO[:, 0:256]), in1=bc4(eacs[:, 0, c, :]), op=ALU.mult), reads=['pO', 'eacs'], writes=['t1'])
            s.op('dve', lambda e: e.tensor_tensor(out=h4(t2[:]), in0=h4(pO[:, 256:512]), in1=bc4(eacs[:, 1, c, :]), op=ALU.mult), reads=['pO', 'eacs'], writes=['t2'])
            s.op(PENG4, lambda e: e.tensor_tensor(out=h4(t3[:]), in0=h4(xs[:, c, :]), in1=bc4(dssd[:, 0:4]), op=ALU.mult), reads=['xs', 'dssd'], writes=['t3'])
            s.op(PENG4, lambda e: e.tensor_tensor(out=t1[:], in0=t1[:], in1=t2[:], op=ALU.add), reads=['t1', 't2'], writes=['t1'])
            s.op(PENG4, lambda e: e.tensor_tensor(out=t1[:], in0=t1[:], in1=t3[:], op=ALU.add), reads=['t1', 't3'], writes=['t1'])
            s.op('dve', lambda e: e.scalar_tensor_tensor(out=ysb[i][:], in0=pY[:, 0:256], scalar=1.0, in1=t1[:], op0=ALU.mult, op1=ALU.add), reads=['t1', 'pY'], writes=['ysb%d' % i])
            s.dma('sp', 'dm_y%d' % i, yssd_d[c * 128:(c + 1) * 128, :], ysb[i][:], reads=['ysb%d' % i], writes=['yssd'])
        front4(0)
        for c in range(NCH):
            if c + 1 < NCH:
                front4(c + 1)
            back4(c)
        s.wait_all('sp', ['yssd', 'ztok'])
        s.barrier()
        s.bulk_on = False
    return nc


def build_l0a_ssd(stage=99, cut=None):
    nc = bass.Bass("TRN2", target_bir_lowering=False)
    dr = lambda n, sh, kind="ExternalInput", dt=F32: nc.dram_tensor(n, list(sh), dt, kind=kind).ap()
    xin = dr("xin", [T, D])
    cvec_d = dr("cvec", [128, 8, 2]); adaw_d = dr("adaw", [128, 8, 2048]); adab_d = dr("adab", [128, 16])
    NW = SSD_NCM * 128 + SSD_NTM
    win_d = dr("win", [128, 8, NW])
    convw_d = dr("convw", [128, 4, 5]); convb_d = dr("convb", [128, 4])
    dtb_d = dr("dtb", [128, 8]); alog_d = dr("alog", [128, 8]); dssd_d = dr("dssd", [128, 4])
    cst_d = dr("cst", [128, 6, 128])
    ztok_d = dr("ztok", [T, SSD_NTM], "ExternalOutput")
    yssd_d = dr("yssd", [T, 256], "ExternalOutput")
    dbg_d = dr("dbg", [128, 4096], "ExternalOutput")

    with ExitStack() as es0:
        s = Sched(nc, es0)
        emit_l0a_ssd(nc, s, '', xin, cvec_d, adaw_d, adab_d, win_d, convw_d, convb_d, dtb_d, alog_d, dssd_d, cst_d, ztok_d, yssd_d, dbg_d, stage=stage, cut=cut)
    return nc


NC8 = T // 8
NCX8 = NCTX // 8
TWO_PI = 6.283185307179586
NT8 = [(0, 512), (512, 512), (1024, 32)]


def emit_l0a_s5(nc, s, P, xin, cvec_d, adaw_d, adab_d, win_d, lam_d, bb_d, cc_d, esel_d, msk_d, gu_d, y5_d, dbg_d, y5tok_d=None, fsel_d=None, stage=99, cut=None, aT_all=None):
    NW = 640
    with ExitStack() as es:
        s.es = es
        s.prefix = P
        identb, idk = make_identity(s, 'identb', BF16)
        identf = s.sb('identf2', [128, 128])
        s.op('pool', lambda e: e.memset(identf[:], 1.0), writes=['identf2'])
        s.op('pool', lambda e: e.affine_select(out=identf[:], in_=identf[:], pattern=[[-1, 128]], compare_op=ALU.is_equal,
                                               fill=0.0, base=0, channel_multiplier=1), reads=['identf2'], writes=['identf2'])
        lam = s.sb('lam', [128, 3, 8]); bb = s.sb('bb', [128, 2, 4, 16]); cc = s.sb('cc', [128, 2, 8, 16])
        eself = s.sb('eself', [128, 3, 8, 128]); msk = s.sb('msk', [128, 2, 128])
        esel = s.sb('esel', [128, 3, 8, 128], BF16)
        s.dma('sp', 'dm_c', lam[:], lam_d[:, :, :], writes=['lam'])
        s.dma('sp', 'dm_c', bb[:], bb_d[:, :, :, :], writes=['bb'])
        s.dma('sp', 'dm_c', cc[:], cc_d[:, :, :, :], writes=['cc'])
        s.dma('sp', 'dm_c', eself[:], esel_d[:, :, :, :], writes=['eself'])
        s.dma('sp', 'dm_c', msk[:], msk_d[:, :, :], writes=['msk'])
        s.op('pool', lambda e: e.tensor_copy(out=esel[:], in_=eself[:]), reads=['eself'], writes=['esel'])
        if aT_all is None:
            modT = phase0_mod(s, nc, cvec_d, adaw_d, adab_d, 16)
            sc1 = s.sb('sc1', [128, 8, 2])
            s.op('dve', lambda e: e.tensor_scalar(out=sc1[:], in0=modT[:, 8:16, :], scalar1=1.0, scalar2=None, op0=ALU.add),
                 reads=['modT'], writes=['modv'])
            sh = modT[:, 0:8, :]

        U = s.sb('U', [128, 8, NC8], BF16)

        with ExitStack() as p1:
            wb = s.sb('wb', [128, 8, NW], BF16, es=p1)
            with ExitStack() as wtmp:
                load_cast_weights(s, nc, win_d, wb, 'wb', NW, wtmp, 'a')
                s.barrier()
            if aT_all is None:
                npj = NormProj(s, nc, xin, identb, idk, sc1, sh, p1)
                aT = s.sb('aT', [128, 8, 512], BF16, es=p1); kaT = 'aT'
            else:
                aTb = [s.sb('aT%d' % i_, [128, 8, 512], BF16, es=p1) for i_ in range(2)]
            tcnt = 0
            uT = s.sb('uT', [128, 3, 512], BF16, es=p1)
            gsb = [s.sb('gsb%d' % i, [128, 256], es=p1) for i in range(2)]
            pcm = [s.ps('pcm%d' % i, [128, 512], es=p1) for i in range(2)]
            ptm = [s.ps('ptm%d' % i, [128, 512], es=p1) for i in range(2)]
            pU = s.ps('pU', [128, 512], es=p1)
            ncm = 0; ntm = 0
            if cut is not None:
                s.mark(); s.skip_after = cut
            for (seg0, seglen, which) in SEGS:
                ntiles = (seglen + 511) // 512
                for ti in range(ntiles):
                    t0 = seg0 + ti * 512
                    nt = min(512, seg0 + seglen - t0)
                    if aT_all is None:
                        for sub in range(nt // 128):
                            npj.sub(t0 + sub * 128, which, aT, 'aT', sub * 128)
                    else:
                        aT = aTb[tcnt % 2]; kaT = 'aT%d' % (tcnt % 2)
                        s.dma('sp' if tcnt % 2 == 0 else 'act', 'dm_' + kaT, aT[:, :, 0:nt], aT_all[:, :, t0:t0 + nt], writes=[kaT])
                        tcnt += 1
                    for m in range(3):
                        pc = pcm[ncm % 2]; kpc = 'pcm%d' % (ncm % 2); ncm += 1
                        for k in range(8):
                            s.op('pe', lambda e: e.matmul(pc[:, 0:nt], lhsT=wb[:, k, m * 128:(m + 1) * 128], rhs=aT[:, k, 0:nt],
                                                          start=(k == 0), stop=(k == 7)), reads=['wb', kaT], writes=[kpc])
                        s.op('act', lambda e: e.activation(out=uT[:, m, 0:nt], in_=pc[:, 0:nt], func=AF.Copy), reads=[kpc], writes=['uT'])
                    for sub in range(nt // 128):
                        pt = ptm[ntm % 2]; kpt = 'ptm%d' % (ntm % 2)
                        gb = gsb[ntm % 2]; kgb = 'gsb%d' % (ntm % 2); ntm += 1
                        for k in range(8):
                            s.op('pe', lambda e: e.matmul(pt[:, 0:256], lhsT=aT[:, k, sub * 128:(sub + 1) * 128], rhs=wb[:, k, 384:640],
                                                          start=(k == 0), stop=(k == 7)), reads=['wb', kaT], writes=[kpt])
                        s.op('act', lambda e: e.activation(out=gb[:], in_=pt[:, 0:256], func=AF.Copy), reads=[kpt], writes=[kgb])
                        s.dma('sp', 'dm_g%d' % ((ntm - 1) % 2), gu_d[t0 + sub * 128:t0 + (sub + 1) * 128, :], gb[:], reads=[kgb], writes=['gutok'])
                    ncs = nt // 8; c0 = t0 // 8
                    for g in range(8):
                        hh, r = g // 3, g % 3
                        for sx in range(8):
                            s.op('pe', lambda e: e.matmul(pU[:, g * 64:g * 64 + ncs], lhsT=esel[:, r, sx, :],
                                                          rhs=uT[:, hh, sx:nt:8], start=(sx == 0), stop=(sx == 7)),
                                 reads=['esel', 'uT'], writes=['pU'])
                    s.op('act', lambda e: e.activation(out=U[:, :, c0:c0 + ncs], in_=pU[:].rearrange("p (g c) -> p g c", c=64)[:, :, 0:ncs], func=AF.Copy),
                         reads=['pU'], writes=['U'])
            s.barrier()
        if stage == 1:
            _finish_dbg(s, dbg_d, [(U[:, 3, 0:1056], 'U', 1056), (U[:, 6, 0:1056], 'U', 1056)])
            s.wait_all('sp', ['gutok'])
            return nc
        Tbf = s.sb('Tbf', [128, 16, 128], BF16)
        Wz_re = s.sb('Wz_re', [128, 2, 8, 128], BF16); Wz_im = s.sb('Wz_im', [128, 2, 8, 128], BF16)
        Vz_re = s.sb('Vz_re', [128, 2, 8, 128], BF16); Vz_imn = s.sb('Vz_imn', [128, 2, 8, 128], BF16)
        r8 = s.sb('r8', [128, 8]); zz = s.sb('zz', [128, 2, 8])
        for tz, kz in ((Wz_re, 'Wz_re'), (Wz_im, 'Wz_im'), (Vz_re, 'Vz_re'), (Vz_imn, 'Vz_imn')):
            s.op('pool', lambda e: e.memset(tz[:], 0.0), writes=[kz])
        with ExitStack() as p2:
            sv = lambda n, shape=(128, 8): s.sb('q_' + n, list(shape), es=p2)
            DV = lambda fn, r, w: s.op('dve', fn, reads=r, writes=w)
            step = sv('step'); th = sv('th'); ml = sv('ml'); mag = sv('mag'); tq = sv('tq'); thr = sv('thr'); sn = sv('sn'); cs = sv('cs')
            ar = sv('ar'); ai = sv('ai'); nai = sv('nai'); arm1 = sv('arm1'); den = sv('den'); f_re = sv('f_re'); f_im = sv('f_im'); nf_im = sv('nf_im')
            t1 = sv('t1'); t2 = sv('t2'); ir8 = sv('ir8'); im2 = sv('im2'); avr = sv('avr'); avi = sv('avi'); navi = sv('navi')
            lr, li, ls = lam[:, 0, :], lam[:, 1, :], lam[:, 2, :]
            s.op('act', lambda e: e.activation(out=step[:], in_=ls, func=AF.Exp), reads=['lam'], writes=['q_step'])
            DV(lambda e: e.tensor_tensor(out=th[:], in0=li, in1=step[:], op=ALU.mult), ['lam', 'q_step'], ['q_th'])
            DV(lambda e: e.tensor_tensor(out=ml[:], in0=lr, in1=step[:], op=ALU.mult), ['lam', 'q_step'], ['q_ml'])
            s.op('act', lambda e: e.activation(out=mag[:], in_=ml[:], func=AF.Exp), reads=['q_ml'], writes=['q_mag'])
            s.op('act', lambda e: e.activation(out=r8[:], in_=ml[:], func=AF.Exp, scale=8.0), reads=['q_ml'], writes=['r8'])
            s.op('act', lambda e: e.activation(out=ir8[:], in_=ml[:], func=AF.Exp, scale=-8.0), reads=['q_ml'], writes=['q_ir8'])
            DV(lambda e: e.tensor_scalar(out=tq[:], in0=th[:], scalar1=1.0 / TWO_PI, scalar2=12582912.0, op0=ALU.mult, op1=ALU.add), ['q_th'], ['q_tq'])
            DV(lambda e: e.tensor_scalar(out=tq[:], in0=tq[:], scalar1=12582912.0, scalar2=None, op0=ALU.subtract), ['q_tq'], ['q_tq'])
            DV(lambda e: e.scalar_tensor_tensor(out=thr[:], in0=tq[:], scalar=-TWO_PI, in1=th[:], op0=ALU.mult, op1=ALU.add), ['q_tq', 'q_th'], ['q_thr'])
            DV(lambda e: e.tensor_scalar(out=thr[:], in0=thr[:], scalar1=-3.1415925, scalar2=3.1415925, op0=ALU.max, op1=ALU.min), ['q_thr'], ['q_thr'])
            s.op('act', lambda e: e.activation(out=sn[:], in_=thr[:], func=AF.Sin), reads=['q_thr'], writes=['q_sn'])
            DV(lambda e: e.tensor_scalar(out=t2[:], in0=thr[:], scalar1=-1.0, scalar2=None, op0=ALU.mult), ['q_thr'], ['q_t2'])
            DV(lambda e: e.tensor_tensor(out=t1[:], in0=thr[:], in1=t2[:], op=ALU.max), ['q_thr', 'q_t2'], ['q_t1'])
            DV(lambda e: e.tensor_scalar(out=t1[:], in0=t1[:], scalar1=-1.0, scalar2=1.5707963, op0=ALU.mult, op1=ALU.add), ['q_t1'], ['q_t1'])
            s.op('act', lambda e: e.activation(out=cs[:], in_=t1[:], func=AF.Sin), reads=['q_t1'], writes=['q_cs'])
            DV(lambda e: e.tensor_tensor(out=ar[:], in0=mag[:], in1=cs[:], op=ALU.mult), ['q_mag', 'q_cs'], ['q_ar'])
            DV(lambda e: e.tensor_tensor(out=ai[:], in0=mag[:], in1=sn[:], op=ALU.mult), ['q_mag', 'q_sn'], ['q_ai'])
            DV(lambda e: e.tensor_scalar(out=nai[:], in0=ai[:], scalar1=-1.0, scalar2=None, op0=ALU.mult), ['q_ai'], ['q_nai'])
            DV(lambda e: e.tensor_scalar(out=arm1[:], in0=ar[:], scalar1=-1.0, scalar2=None, op0=ALU.add), ['q_ar'], ['q_arm1'])
            DV(lambda e: e.tensor_tensor(out=den[:], in0=lr, in1=lr, op=ALU.mult), ['lam'], ['q_den'])
            DV(lambda e: e.tensor_tensor(out=t1[:], in0=li, in1=li, op=ALU.mult), ['lam'], ['q_t1'])
            DV(lambda e: e.tensor_tensor(out=den[:], in0=den[:], in1=t1[:], op=ALU.add), ['q_den', 'q_t1'], ['q_den'])
            DV(lambda e: e.reciprocal(out=den[:], in_=den[:]), ['q_den'], ['q_den'])
            DV(lambda e: e.tensor_tensor(out=t1[:], in0=arm1[:], in1=lr, op=ALU.mult), ['q_arm1', 'lam'], ['q_t1'])
            DV(lambda e: e.tensor_tensor(out=t2[:], in0=ai[:], in1=li, op=ALU.mult), ['q_ai', 'lam'], ['q_t2'])
            DV(lambda e: e.tensor_tensor(out=t1[:], in0=t1[:], in1=t2[:], op=ALU.add), ['q_t1', 'q_t2'], ['q_t1'])
            DV(lambda e: e.tensor_tensor(out=f_re[:], in0=t1[:], in1=den[:], op=ALU.mult), ['q_t1', 'q_den'], ['q_f_re'])
            DV(lambda e: e.tensor_tensor(out=t1[:], in0=ai[:], in1=lr, op=ALU.mult), ['q_ai', 'lam'], ['q_t1'])
            DV(lambda e: e.tensor_tensor(out=t2[:], in0=arm1[:], in1=li, op=ALU.mult), ['q_arm1', 'lam'], ['q_t2'])
            DV(lambda e: e.tensor_tensor(out=t1[:], in0=t1[:], in1=t2[:], op=ALU.subtract), ['q_t1', 'q_t2'], ['q_t1'])
            DV(lambda e: e.tensor_tensor(out=f_im[:], in0=t1[:], in1=den[:], op=ALU.mult), ['q_t1', 'q_den'], ['q_f_im'])
            DV(lambda e: e.tensor_tensor(out=im2[:], in0=mag[:], in1=mag[:], op=ALU.mult), ['q_mag'], ['q_im2'])
            DV(lambda e: e.reciprocal(out=im2[:], in_=im2[:]), ['q_im2'], ['q_im2'])
            DV(lambda e: e.tensor_tensor(out=avr[:], in0=ar[:], in1=im2[:], op=ALU.mult), ['q_ar', 'q_im2'], ['q_avr'])
            DV(lambda e: e.tensor_tensor(out=avi[:], in0=nai[:], in1=im2[:], op=ALU.mult), ['q_nai', 'q_im2'], ['q_avi'])
            PW = s.sb('q_PW', [128, 2, 8, 9], es=p2); AV = s.sb('q_AV', [128, 2, 8, 9], es=p2)

            def powers(P, kP, br, bi, nmax):
                s.op('pool', lambda e: e.memset(P[:, 0, :, 0], 1.0), reads=[kP], writes=[kP])
                s.op('pool', lambda e: e.memset(P[:, 1, :, 0], 0.0), reads=[kP], writes=[kP])
                for n in range(1, nmax + 1):
                    DV(lambda e: e.tensor_tensor(out=t1[:], in0=P[:, 0, :, n - 1], in1=br[:], op=ALU.mult), [kP], ['q_t1'])
                    DV(lambda e: e.tensor_tensor(out=t2[:], in0=P[:, 1, :, n - 1], in1=bi[:], op=ALU.mult), [kP], ['q_t2'])
                    DV(lambda e: e.tensor_tensor(out=P[:, 0, :, n], in0=t1[:], in1=t2[:], op=ALU.subtract), ['q_t1', 'q_t2', kP], [kP])
                    DV(lambda e: e.tensor_tensor(out=t1[:], in0=P[:, 0, :, n - 1], in1=bi[:], op=ALU.mult), [kP], ['q_t1'])
                    DV(lambda e: e.tensor_tensor(out=t2[:], in0=P[:, 1, :, n - 1], in1=br[:], op=ALU.mult), [kP], ['q_t2'])
                    DV(lambda e: e.tensor_tensor(out=P[:, 1, :, n], in0=t1[:], in1=t2[:], op=ALU.add), ['q_t1', 'q_t2', kP], [kP])
            s.op('dve', lambda e: e.tensor_copy(out=t1[:], in_=ar[:]), reads=['q_ar', 'q_ai', 'q_avr', 'q_avi'], writes=['q_t1'])
            powers(PW, 'q_PW', ar, ai, 8)
            powers(AV, 'q_AV', avr, avi, 8)
            for c in range(2):
                DV(lambda e: e.tensor_tensor(out=zz[:, c, :], in0=PW[:, c, :, 8], in1=ir8[:], op=ALU.mult), ['q_PW', 'q_ir8'], ['zz'])
            Bb = s.sb('q_Bb', [128, 2, 8, 16], es=p2)
            tb1 = s.sb('q_tb1', [128, 4, 16], es=p2); tb2 = s.sb('q_tb2', [128, 4, 16], es=p2)
            for d in range(2):
                fr = f_re[:, 4 * d:4 * d + 4].unsqueeze(2).to_broadcast([128, 4, 16]); fi = f_im[:, 4 * d:4 * d + 4].unsqueeze(2).to_broadcast([128, 4, 16])
                DV(lambda e: e.tensor_tensor(out=tb1[:], in0=bb[:, 0, :, :], in1=fr, op=ALU.mult), ['bb', 'q_f_re'], ['q_tb1'])
                DV(lambda e: e.tensor_tensor(out=tb2[:], in0=bb[:, 1, :, :], in1=fi, op=ALU.mult), ['bb', 'q_f_im'], ['q_tb2'])
                DV(lambda e: e.tensor_tensor(out=Bb[:, 0, 4 * d:4 * d + 4, :], in0=tb1[:], in1=tb2[:], op=ALU.subtract), ['q_tb1', 'q_tb2'], ['q_Bb'])
                DV(lambda e: e.tensor_tensor(out=tb1[:], in0=bb[:, 1, :, :], in1=fr, op=ALU.mult), ['bb', 'q_f_re'], ['q_tb1'])
                DV(lambda e: e.tensor_tensor(out=tb2[:], in0=bb[:, 0, :, :], in1=fi, op=ALU.mult), ['bb', 'q_f_im'], ['q_tb2'])
                DV(lambda e: e.tensor_tensor(out=Bb[:, 1, 4 * d:4 * d + 4, :], in0=tb1[:], in1=tb2[:], op=ALU.add), ['q_tb1', 'q_tb2'], ['q_Bb'])

            tw1 = s.sb('q_tw1', [128, 4, 8, 16], es=p2); tw2 = s.sb('q_tw2', [128, 4, 8, 16], es=p2)

            def cprod(out_re, out_im, kout, P, kP, sl, M, kM, d, neg_im=False):
                dsl = slice(4 * d, 4 * d + 4)
                pr = P[:, 0, dsl, sl].unsqueeze(3).to_broadcast([128, 4, 8, 16]); pi_ = P[:, 1, dsl, sl].unsqueeze(3).to_broadcast([128, 4, 8, 16])
                mr = M[:, 0, dsl, :].unsqueeze(2).to_broadcast([128, 4, 8, 16]); mi = M[:, 1, dsl, :].unsqueeze(2).to_broadcast([128, 4, 8, 16])
                o_re = out_re[:, dsl, :].rearrange("p k (n j) -> p k n j", j=16); o_im = out_im[:, dsl, :].rearrange("p k (n j) -> p k n j", j=16)
                DV(lambda e: e.tensor_tensor(out=tw1[:], in0=pr, in1=mr, op=ALU.mult), [kP, kM], ['q_tw1'])
                DV(lambda e: e.tensor_tensor(out=tw2[:], in0=pi_, in1=mi, op=ALU.mult), [kP, kM], ['q_tw2'])
                DV(lambda e: e.tensor_tensor(out=o_re, in0=tw1[:], in1=tw2[:], op=ALU.subtract), ['q_tw1', 'q_tw2'], [kout])
                DV(lambda e: e.tensor_tensor(out=tw1[:], in0=pr, in1=mi, op=ALU.mult), [kP, kM], ['q_tw1'])
                DV(lambda e: e.tensor_tensor(out=tw2[:], in0=pi_, in1=mr, op=ALU.mult), [kP, kM], ['q_tw2'])
                if neg_im:
                    DV(lambda e: e.scalar_tensor_tensor(out=o_im, in0=tw1[:], scalar=-1.0, in1=tw2[:], op0=ALU.mult, op1=ALU.subtract), ['q_tw1', 'q_tw2'], [kout])
                else:
                    DV(lambda e: e.tensor_tensor(out=o_im, in0=tw1[:], in1=tw2[:], op=ALU.add), ['q_tw1', 'q_tw2'], [kout])
            wsh = [128, 8, 128]
            WT_re = s.sb('q_WT_re', wsh, es=p2); WT_im = s.sb('q_WT_im', wsh, es=p2)
            Km_re = s.sb('q_Km_re', wsh, es=p2); Km_imn = s.sb('q_Km_imn', wsh, es=p2)
            Q_re = s.sb('q_Q_re', wsh, es=p2); Q_im = s.sb('q_Q_im', wsh, es=p2)
            V_re = s.sb('q_V_re', wsh, es=p2); V_imn = s.sb('q_V_imn', wsh, es=p2)
            fw8 = slice(0, 8); rv7 = slice(7, None, -1); f19 = slice(1, 9); rv8 = slice(8, 0, -1)
            cprod(WT_re, WT_im, 'q_WT', PW, 'q_PW', rv7, Bb, 'q_Bb', 0); cprod(WT_re, WT_im, 'q_WT', PW, 'q_PW', fw8, Bb, 'q_Bb', 1)
            cprod(Km_re, Km_imn, 'q_Km', AV, 'q_AV', fw8, Bb, 'q_Bb', 0, True); cprod(Km_re, Km_imn, 'q_Km', PW, 'q_PW', fw8, Bb, 'q_Bb', 1, True)
            cprod(Q_re, Q_im, 'q_Q', PW, 'q_PW', fw8, cc, 'cc', 0); cprod(Q_re, Q_im, 'q_Q', AV, 'q_AV', fw8, cc, 'cc', 1)
            cprod(V_re, V_imn, 'q_V', PW, 'q_PW', f19, cc, 'cc', 0, True); cprod(V_re, V_imn, 'q_V', PW, 'q_PW', rv8, cc, 'cc', 1, True)
            Qz = [[s.sb('q_Qz%d%d' % (e_, c_), wsh, es=p2) for c_ in range(2)] for e_ in range(2)]
            for e_ in range(2):
                for c_, Qs in enumerate((Q_re, Q_im)):
                    s.op('pool', lambda e: e.tensor_copy(out=Qz[e_][c_][:], in_=Qs[:]), reads=['q_Q'], writes=['q_Qz'])
                    s.op('pool', lambda e: e.memset(Qz[e_][c_][64 * (1 - e_):64 * (1 - e_) + 64, :, :], 0.0), reads=['q_Qz'], writes=['q_Qz'])
            pW = [s.ps('pW%d' % i, [128, 512], es=p2) for i in range(2)]
            npw = 0
            for d in range(2):
                for k in range(4):
                    dk = 4 * d + k
                    for e_ in range(2):
                        g = 2 * k + e_
                        pw = pW[npw % 2]; kpw = 'pW%d' % (npw % 2); npw += 1
                        s.op('pe', lambda e: e.matmul(pw[:, 0:128], lhsT=Km_re[:, dk, :], rhs=Qz[e_][0][:, dk, :], start=True, stop=False),
                             reads=['q_Km', 'q_Qz'], writes=[kpw])
                        s.op('pe', lambda e: e.matmul(pw[:, 0:128], lhsT=Km_imn[:, dk, :], rhs=Qz[e_][1][:, dk, :], start=False, stop=True),
                             reads=['q_Km', 'q_Qz'], writes=[kpw])
                        s.op('dve', lambda e: e.tensor_tensor(out=Tbf[:, 2 * g + d, :], in0=pw[:, 0:128], in1=msk[:, d, :], op=ALU.mult),
                             reads=[kpw, 'msk'], writes=['Tbf'])
                        hs = slice(64 * e_, 64 * e_ + 64)
                        s.op('act', lambda e: e.activation(out=Vz_re[hs, d, g, :], in_=V_re[hs, dk, :], func=AF.Copy), reads=['q_V'], writes=['Vz_re'])
                        s.op('act', lambda e: e.activation(out=Vz_imn[hs, d, g, :], in_=V_imn[hs, dk, :], func=AF.Copy), reads=['q_V'], writes=['Vz_imn'])
                    for c_, (WTs, Wz) in enumerate(((WT_re, Wz_re), (WT_im, Wz_im))):
                        pw = pW[npw % 2]; kpw = 'pW%d' % (npw % 2); npw += 1
                        s.op('pe', lambda e: e.transpose(out=pw[:, 0:128], in_=WTs[:, dk, :], identity=identf[:]), reads=['q_WT', 'identf2'], writes=[kpw])
                        for e_ in range(2):
                            hs = slice(64 * e_, 64 * e_ + 64)
                            s.op('act', lambda e: e.activation(out=Wz[:, d, 2 * k + e_, hs], in_=pw[:, hs], func=AF.Copy), reads=[kpw], writes=['Wz'])
            s.barrier()
        if stage == 2:
            _finish_dbg(s, dbg_d, [(Tbf[:, 5, :], 'Tbf', 128), (Tbf[:, 6, :], 'Tbf', 128), (Wz_re[:, 1, 3, :], 'Wz', 128), (Wz_im[:, 0, 2, :], 'Wz', 128),
                                   (Vz_re[:, 0, 3, :], 'Vz_re', 128), (Vz_imn[:, 1, 4, :], 'Vz_imn', 128), (r8[:], 'r8', 8), (zz[:].rearrange("p c k -> p (c k)"), 'zz', 16)])
            s.wait_all('sp', ['gutok'])
            return nc
        Xin_re = s.sb('Xin_re', [128, 2, 4, NC8], BF16); Xin_im = s.sb('Xin_im', [128, 2, 4, NC8], BF16)
        with ExitStack() as p3:
            big = [128, 4, NC8]
            S_re = s.sb('S_re', big, es=p3); S_im = s.sb('S_im', big, es=p3)
            G_re = s.sb('G_re', big, es=p3); G_im = s.sb('G_im', big, es=p3)
            RT_re = s.sb('RT_re', big, es=p3); RT_im = s.sb('RT_im', big, es=p3)
            ta = s.sb('r_ta', [128, 4, 512], es=p3); tb_ = s.sb('r_tb', [128, 4, 512], es=p3)
            w_re = s.sb('r_wre', [128, 4], es=p3); w_im = s.sb('r_wim', [128, 4], es=p3)
            wt1 = s.sb('r_wt1', [128, 4], es=p3); wt2 = s.sb('r_wt2', [128, 4], es=p3)
            pSr = [s.ps('pSr%d' % i, [128, 512], es=p3) for i in range(2)]
            pSi = [s.ps('pSi%d' % i, [128, 512], es=p3) for i in range(2)]
            DV = lambda fn, r, w: s.op('dve', fn, reads=r, writes=w)
            nps = 0
            for d in range(2):
                s.op('pool', lambda e: e.memset(RT_re[:, :, 0:1], 1.0), reads=['RT'], writes=['RT'])
                s.op('pool', lambda e: e.memset(RT_im[:, :, 0:1], 0.0), reads=['RT'], writes=['RT'])
                DV(lambda e: e.tensor_copy(out=w_re[:], in_=zz[:, 0, 4 * d:4 * d + 4]), ['zz'], ['r_w'])
                DV(lambda e: e.tensor_copy(out=w_im[:], in_=zz[:, 1, 4 * d:4 * d + 4]), ['zz'], ['r_w'])
                n = 1
                while n < NC8:
                    cnt = min(n, NC8 - n)
                    bw = lambda w: w[:].unsqueeze(2).to_broadcast([128, 4, cnt])
                    DV(lambda e: e.tensor_tensor(out=ta[:, :, 0:cnt], in0=RT_re[:, :, 0:cnt], in1=bw(w_re), op=ALU.mult), ['RT', 'r_w'], ['r_ta'])
                    DV(lambda e: e.tensor_tensor(out=tb_[:, :, 0:cnt], in0=RT_im[:, :, 0:cnt], in1=bw(w_im), op=ALU.mult), ['RT', 'r_w'], ['r_tb'])
                    DV(lambda e: e.tensor_tensor(out=RT_re[:, :, n:n + cnt], in0=ta[:, :, 0:cnt], in1=tb_[:, :, 0:cnt], op=ALU.subtract), ['r_ta', 'r_tb', 'RT'], ['RT'])
                    DV(lambda e: e.tensor_tensor(out=ta[:, :, 0:cnt], in0=RT_re[:, :, 0:cnt], in1=bw(w_im), op=ALU.mult), ['RT', 'r_w'], ['r_ta'])
                    DV(lambda e: e.tensor_tensor(out=tb_[:, :, 0:cnt], in0=RT_im[:, :, 0:cnt], in1=bw(w_re), op=ALU.mult), ['RT', 'r_w'], ['r_tb'])
                    DV(lambda e: e.tensor_tensor(out=RT_im[:, :, n:n + cnt], in0=ta[:, :, 0:cnt], in1=tb_[:, :, 0:cnt], op=ALU.add), ['r_ta', 'r_tb', 'RT'], ['RT'])
                    DV(lambda e: e.tensor_tensor(out=wt1[:], in0=w_re[:], in1=w_re[:], op=ALU.mult), ['r_w'], ['r_wt1'])
                    DV(lambda e: e.tensor_tensor(out=wt2[:], in0=w_im[:], in1=w_im[:], op=ALU.mult), ['r_w'], ['r_wt2'])
                    DV(lambda e: e.tensor_tensor(out=wt2[:], in0=wt1[:], in1=wt2[:], op=ALU.subtract), ['r_wt1', 'r_wt2'], ['r_wt2'])
                    DV(lambda e: e.tensor_tensor(out=wt1[:], in0=w_re[:], in1=w_im[:], op=ALU.mult), ['r_w', 'r_wt2'], ['r_wt1'])
                    DV(lambda e: e.tensor_scalar(out=w_im[:], in0=wt1[:], scalar1=2.0, scalar2=None, op0=ALU.mult), ['r_wt1'], ['r_w'])
                    DV(lambda e: e.tensor_copy(out=w_re[:], in_=wt2[:]), ['r_wt2'], ['r_w'])
                    n *= 2
                for k in range(4):
                    for (n0, n) in NT8:
                        i = nps % 2; nps += 1
                        for (pp, kpp, Wz) in ((pSr[i], 'pSr%d' % i, Wz_re), (pSi[i], 'pSi%d' % i, Wz_im)):
                            for e_ in range(2):
                                g = 2 * k + e_
                                s.op('pe', lambda e: e.matmul(pp[:, 0:n], lhsT=Wz[:, d, g, :], rhs=U[:, g, n0:n0 + n], start=(e_ == 0), stop=(e_ == 1)),
                                     reads=['Wz', 'U'], writes=[kpp])
                        for (pp, kpp, Sd, kS) in ((pSr[i], 'pSr%d' % i, S_re, 'S_re'), (pSi[i], 'pSi%d' % i, S_im, 'S_im')):
                            if d == 0:
                                s.op('act', lambda e: e.activation(out=Sd[:, k, n0:n0 + n], in_=pp[:, 0:n], func=AF.Copy), reads=[kpp], writes=[kS])
                            else:
                                for (ca, cb, M0) in ((0, NCX8, NCX8 - 1), (NCX8, NC8, NC8 + NCX8 - 1)):
                                    lo, hi = max(ca, n0), min(cb, n0 + n)
                                    if lo >= hi:
                                        continue
                                    stop = lo - 1 - n0
                                    src_ = pp[:, hi - 1 - n0::-1] if stop < 0 else pp[:, hi - 1 - n0:stop:-1]
                                    DV(lambda e: e.tensor_copy(out=Sd[:, k, M0 - (hi - 1):M0 - lo + 1], in_=src_), [kpp], [kS])
                fl = lambda t: t[:].rearrange("p k m -> p (k m)")
                DV(lambda e: e.tensor_tensor(out=fl(G_re), in0=fl(S_re), in1=fl(RT_re), op=ALU.mult), ['S_re', 'RT'], ['G_re'])
                s.op('dve', lambda e: e.tensor_tensor(out=fl(G_im), in0=fl(S_im), in1=fl(RT_im), op=ALU.mult), reads=['S_im', 'RT'], writes=['G_im'])
                DV(lambda e: e.tensor_tensor(out=fl(G_re), in0=fl(G_re), in1=fl(G_im), op=ALU.add), ['G_re', 'G_im'], ['G_re'])
                s.op('dve', lambda e: e.tensor_tensor(out=fl(G_im), in0=fl(S_im), in1=fl(RT_re), op=ALU.mult), reads=['S_im', 'RT', 'G_re'], writes=['G_im'])
                DV(lambda e: e.tensor_tensor(out=fl(S_im), in0=fl(S_re), in1=fl(RT_im), op=ALU.mult), ['S_re', 'RT', 'G_im'], ['S_im'])
                DV(lambda e: e.tensor_tensor(out=fl(G_im), in0=fl(G_im), in1=fl(S_im), op=ALU.subtract), ['G_im', 'S_im'], ['G_im'])
                for k in range(4):
                    dk = 4 * d + k
                    for (Gs, kG, Sd, kS) in ((G_re, 'G_re', S_re, 'S_re'), (G_im, 'G_im', S_im, 'S_im')):
                        DV(lambda e: e.tensor_tensor_scan(out=Sd[:, k, :], data0=r8[:, dk:dk + 1].to_broadcast([128, NC8]), data1=Gs[:, k, :],
                                                          initial=0.0, op0=ALU.mult, op1=ALU.add), [kG, 'r8', kS], [kS])
                DV(lambda e: e.tensor_tensor(out=fl(G_re), in0=fl(S_re), in1=fl(RT_re), op=ALU.mult), ['S_re', 'RT'], ['G_re'])
                s.op('dve', lambda e: e.tensor_tensor(out=fl(G_im), in0=fl(S_im), in1=fl(RT_im), op=ALU.mult), reads=['S_im', 'RT'], writes=['G_im'])
                DV(lambda e: e.tensor_tensor(out=fl(G_re), in0=fl(G_re), in1=fl(G_im), op=ALU.subtract), ['G_re', 'G_im'], ['G_re'])
                s.op('dve', lambda e: e.tensor_tensor(out=fl(G_im), in0=fl(S_im), in1=fl(RT_re), op=ALU.mult), reads=['S_im', 'RT', 'G_re'], writes=['G_im'])
                DV(lambda e: e.tensor_tensor(out=fl(S_im), in0=fl(S_re), in1=fl(RT_im), op=ALU.mult), ['S_re', 'RT', 'G_im'], ['S_im'])
                DV(lambda e: e.tensor_tensor(out=fl(G_im), in0=fl(G_im), in1=fl(S_im), op=ALU.add), ['G_im', 'S_im'], ['G_im'])
                for (Gs, kG, Xd, kX) in ((G_re, 'G_re', Xin_re, 'Xin_re'), (G_im, 'G_im', Xin_im, 'Xin_im')):
                    if d == 0:
                        s.op('pool', lambda e: e.memset(Xd[:, 0, :, 0:1], 0.0), reads=[kX], writes=[kX])
                        DV(lambda e: e.tensor_copy(out=Xd[:, 0, :, 1:NC8], in_=Gs[:, :, 0:NC8 - 1]), [kG, kX], [kX])
                    else:
                        s.op('pool', lambda e: e.memset(Xd[:, 1, :, NCX8 - 1:NCX8], 0.0), reads=[kX], writes=[kX])
                        DV(lambda e: e.tensor_copy(out=Xd[:, 1, :, 0:NCX8 - 1], in_=Gs[:, :, NCX8 - 2::-1]), [kG, kX], [kX])
                        DV(lambda e: e.tensor_copy(out=Xd[:, 1, :, NCX8:NC8], in_=Gs[:, :, NC8 - 2:NCX8 - 2:-1]), [kG, kX], [kX])
            s.barrier()
        if stage == 3:
            _finish_dbg(s, dbg_d, [(Xin_re[:, 0, 1, :], 'Xin_re', NC8), (Xin_im[:, 1, 2, :], 'Xin_im', NC8)])
            s.wait_all('sp', ['gutok'])
            return nc

        pYb = [s.ps('pYb%d' % i, [128, 512]) for i in range(2)]
        y5sb = [s.sb('y5sb%d' % i, [128, 512]) for i in range(2)]
        Yb = s.sb('Yb', [128, 8, NC8], BF16) if y5tok_d is not None else None
        ny = 0
        for g in range(8):
            k = g // 2
            for (n0, n) in NT8:
                i = ny % 2; ny += 1
                py = pYb[i]; kpy = 'pYb%d' % i
                nr = slice(n0, n0 + n)
                ops = [(Tbf[:, 2 * g, :], 'Tbf', U[:, g, nr], 'U'), (Tbf[:, 2 * g + 1, :], 'Tbf', U[:, g, nr], 'U')]
                for d in range(2):
                    ops.append((Vz_re[:, d, g, :], 'Vz_re', Xin_re[:, d, k, nr], 'Xin_re'))
                    ops.append((Vz_imn[:, d, g, :], 'Vz_imn', Xin_im[:, d, k, nr], 'Xin_im'))
                for j, (lh, kl, rh, kr) in enumerate(ops):
                    s.op('pe', lambda e: e.matmul(py[:, 0:n], lhsT=lh, rhs=rh, start=(j == 0), stop=(j == len(ops) - 1)), reads=[kl, kr], writes=[kpy])
                if Yb is None:
                    s.op('act', lambda e: e.activation(out=y5sb[i][:, 0:n], in_=py[:, 0:n], func=AF.Copy), reads=[kpy], writes=['y5sb%d' % i])
                    s.dma('sp', 'dm_y5%d' % i, y5_d[g, :, nr], y5sb[i][:, 0:n], reads=['y5sb%d' % i], writes=['y5'])
                else:
                    s.op('act', lambda e: e.activation(out=Yb[:, g, nr], in_=py[:, 0:n], func=AF.Copy), reads=[kpy], writes=['Yb'])
        if Yb is not None:
            fself = s.sb('fself', [128, 8, 8, 128]); fsel = s.sb('fsel', [128, 8, 8, 128], BF16)
            s.dma('sp', 'dm_c', fself[:], fsel_d[:, :, :, :], writes=['fself'])
            s.op('pool', lambda e: e.tensor_copy(out=fsel[:], in_=fself[:]), reads=['fself'], writes=['fsel'])
            y5T = [s.sb('y5T%d' % i, [128, 512]) for i in range(2)]
            ytk = [s.sb('ytk%d' % i, [128, 4, 128]) for i in range(2)]
            pUn = pYb
            pTy = s.ps('pTy', [128, 4, 128])
            for cbk in range((NC8 + 63) // 64):
                i = cbk % 2
                nb = min(64, NC8 - 64 * cbk)
                for l in range(8):
                    for g in range(8):
                        s.op('pe', lambda e: e.matmul(pUn[i][:, l * 64:l * 64 + nb], lhsT=fsel[:, g, l, :], rhs=Yb[:, g, 64 * cbk:64 * cbk + nb],
                                                      start=(g == 0), stop=(g == 7)), reads=['fsel', 'Yb'], writes=['pYb%d' % i])
                s.op('act', lambda e: e.activation(out=y5T[i][:, 0:8 * nb].rearrange("p (c l) -> p c l", l=8),
                                                   in_=pUn[i][:, :].rearrange("p (l c) -> p c l", l=8)[:, 0:nb, :], func=AF.Copy),
                     reads=['pYb%d' % i], writes=['y5T%d' % i])
                nj = (8 * nb) // 128
                for j in range(nj):
                    s.op('pe', lambda e: e.transpose(out=pTy[:, j, :], in_=y5T[i][:, j * 128:(j + 1) * 128], identity=identf[:]),
                         reads=['y5T%d' % i, 'identf2'], writes=['pTy'])
                s.op('act', lambda e: e.activation(out=ytk[i][:, 0:nj, :], in_=pTy[:, 0:nj, :], func=AF.Copy), reads=['pTy'], writes=['ytk%d' % i])
                s.dma('sp', 'dm_y5%d' % i, y5tok_d[512 * cbk:512 * cbk + 128 * nj, :].rearrange("(j p) c -> p j c", p=128), ytk[i][:, 0:nj, :],
                      reads=['ytk%d' % i], writes=['y5'])
        s.wait_all('sp', ['y5', 'gutok'])
        s.barrier()
    return nc


def build_l0a_s5(stage=99, cut=None, unfold=False):
    nc = bass.Bass("TRN2", target_bir_lowering=False)
    dr = lambda n, sh, kind="ExternalInput", dt=F32: nc.dram_tensor(n, list(sh), dt, kind=kind).ap()
    xin = dr("xin", [T, D])
    cvec_d = dr("cvec", [128, 8, 2]); adaw_d = dr("adaw", [128, 8, 2048]); adab_d = dr("adab", [128, 16])
    NW = 640
    win_d = dr("win", [128, 8, NW])
    lam_d = dr("lam", [128, 3, 8])
    bb_d = dr("bb", [128, 2, 4, 16])
    cc_d = dr("cc", [128, 2, 8, 16])
    esel_d = dr("esel", [128, 3, 8, 128])
    msk_d = dr("msk", [128, 2, 128])
    gu_d = dr("gutok", [T, 256], "ExternalOutput")
    y5_d = dr("y5", [8, 128, NC8], "ExternalOutput")
    dbg_d = dr("dbg", [128, 4096], "ExternalOutput")

    with ExitStack() as es0:
        s = Sched(nc, es0)
        kw = {}
        if unfold:
            kw = dict(y5tok_d=nc.dram_tensor("y5tok", [T, 128], F32, kind="ExternalOutput").ap(),
                      fsel_d=nc.dram_tensor("fsel", [128, 8, 8, 128], F32, kind="ExternalInput").ap())
        emit_l0a_s5(nc, s, '', xin, cvec_d, adaw_d, adab_d, win_d, lam_d, bb_d, cc_d, esel_d, msk_d, gu_d, y5_d, dbg_d, stage=stage, cut=cut, **kw)
    return nc


NROW_B = 128 + 2048
NTILE_B = NROW_B // 128


def mod_rep(s, nc, scv, aw_d, abrep_d, ncols, out, out_key, es_tmp, tag):
    screp = s.sb('screp' + tag, [128, 2, 8, 128], es=es_tmp)
    for w in range(2):
        for k in range(8):
            s.op('act', lambda e: e.activation(out=screp[:, w, k, :], in_=scv[:, k, w:w + 1].to_broadcast([128, 128]), func=AF.Copy),
                 reads=['scv'], writes=['screp' + tag])
    s.dma('sp', 'dm_rep' + tag, out[:, 0, :], abrep_d[:, :], writes=[out_key])
    s.dma('sp', 'dm_rep' + tag, out[:, 1, :], abrep_d[:, :], writes=[out_key])
    aw = s.sb('awr' + tag, [128, 8, ncols], es=es_tmp)
    for k in range(8):
        s.dma('sp' if k % 2 == 0 else 'act', 'dm_awr' + tag, aw[:, k, :], aw_d[:, k, :], writes=['awr' + tag])
    pr = [s.ps('prep%s%d' % (tag, i), [128, 512], es=es_tmp) for i in range(2)]
    n = 0
    for w in range(2):
        for c0 in range(0, ncols, 512):
            p_ = pr[n % 2]; kp = 'prep%s%d' % (tag, n % 2); n += 1
            for k in range(8):
                s.op('pe', lambda e: e.matmul(p_[:, :], lhsT=screp[:, w, k, :], rhs=aw[:, k, c0:c0 + 512], start=(k == 0), stop=(k == 7)),
                     reads=['screp' + tag, 'awr' + tag], writes=[kp])
            s.op('dve', lambda e: e.scalar_tensor_tensor(out=out[:, w, c0:c0 + 512], in0=p_[:, :], scalar=1.0, in1=out[:, w, c0:c0 + 512], op0=ALU.mult, op1=ALU.add),
                 reads=[kp, out_key], writes=[out_key])


def build_l0b(stage=99, cut=None):
    nc = bass.Bass("TRN2", target_bir_lowering=False)
    dr = lambda n, sh, kind="ExternalInput", dt=F32: nc.dram_tensor(n, list(sh), dt, kind=kind).ap()
    xres_d = dr("xres", [NROW_B, D]); ys_d = dr("ys", [NROW_B, 1024]); z_d = dr("z", [NROW_B, 1024])
    y5_d = dr("y5", [NROW_B, 512]); u_d = dr("u", [NROW_B, 512]); g5_d = dr("g5", [NROW_B, 512])
    cvec_d = dr("cvec", [128, 8, 2])
    aw0g_d = dr("aw0g", [128, 8, 1024]); ab0g_d = dr("ab0g", [128, 1024])
    adaw1_d = dr("adaw1", [128, 8, 2048]); adab1_d = dr("adab1", [128, 16])
    reps_d = dr("reps", [128, 1024 + 512 + 512 + 64 + 64])
    gluw_d = dr("gluw", [128, 4, 512]); wout_d = dr("wout", [128, 12, 1024]); w1_d = dr("w1", [128, 8, 2560])
    rope_d = dr("rope", [128, NTILE_B, 2, 32])
    h1_d = dr("h1", [NROW_B, D], "ExternalOutput")
    q_d = dr("q", [NROW_B, 1024], "ExternalOutput", BF16); k_d = dr("k", [NROW_B, 256], "ExternalOutput", BF16)
    v_d = dr("v", [NROW_B, 256], "ExternalOutput", BF16); sg_d = dr("sg", [NROW_B, 1024], "ExternalOutput", BF16)
    dbg_d = dr("dbg", [128, 4096], "ExternalOutput")

    with ExitStack() as es:
        s = Sched(nc, es)
        identb, idk = make_identity(s, 'identb', BF16)
        reps = s.sb('reps', [128, 2176]); rope = s.sb('rope', [128, NTILE_B, 2, 32])
        s.dma('sp', 'dm_c', reps[:], reps_d[:, :], writes=['reps'])
        s.dma('sp', 'dm_c', rope[:], rope_d[:, :, :, :], writes=['rope'])
        ssdn = reps[:, 0:1024]; ds5 = reps[:, 1024:1536]; glub = reps[:, 1536:2048]; qg = reps[:, 2048:2112]; kg = reps[:, 2112:2176]
        modT = phase0_mod(s, nc, cvec_d, adaw1_d, adab1_d, 16)
        sc1 = s.sb('sc1', [128, 8, 2])
        s.op('dve', lambda e: e.tensor_scalar(out=sc1[:], in0=modT[:, 8:16, :], scalar1=1.0, scalar2=None, op0=ALU.add), reads=['modT'], writes=['modv'])
        sh = modT[:, 0:8, :]
        gate0 = s.sb('gate0', [128, 2, 1024])
        with ExitStack() as t0:
            cv = s.sb('cv2', [128, 8, 2], es=t0); scv = s.sb('scv2', [128, 8, 2], es=t0)
            s.dma('sp', 'dm_c', cv[:], cvec_d[:, :, :], writes=['cv2'])
            s.op('act', lambda e: e.activation(out=scv[:], in_=cv[:], func=AF.Silu), reads=['cv2'], writes=['scv'])
            mod_rep(s, nc, scv, aw0g_d, ab0g_d, 1024, gate0, 'gate0', t0, 'g0')
            s.barrier()
        gluw = s.sb('gluw', [128, 4, 512], BF16); wout = s.sb('wout', [128, 12, 1024], BF16); w1 = s.sb('w1', [128, 8, 2560], BF16)
        with ExitStack() as t1:
            stg = [s.sb('wst%d' % i, [128, 2560], es=t1) for i in range(2)]
            n = 0
            for (wd, wsb, kw, nk, ncol) in ((gluw_d, gluw, 'gluw', 4, 512), (wout_d, wout, 'wout', 12, 1024), (w1_d, w1, 'w1', 8, 2560)):
                for k in range(nk):
                    i = n % 2; n += 1
                    s.dma('sp' if i == 0 else 'act', 'dm_wst%d' % i, stg[i][:, 0:ncol], wd[:, k, :], writes=['wst%d' % i])
                    s.op('pool', lambda e: e.tensor_copy(out=wsb[:, k, :], in_=stg[i][:, 0:ncol]), reads=['wst%d' % i], writes=[kw])
            s.barrier()

        npj = NormProj(s, nc, None, identb, idk, sc1, sh, es)
        ld = lambda n, w: [s.sb('%s%d' % (n, i), [128, w]) for i in range(2)]
        ysb = ld('ysb', 1024); zsb = ld('zsb', 1024); xrb = ld('xrb', 1024); y5b = ld('y5b', 512); ub = ld('ub', 512); g5b = ld('g5b', 512)
        tt = s.sb('tt', [128, 1024]); Fb = s.sb('Fb', [128, 1536], BF16); vv = s.sb('vv', [128, 512]); vb = s.sb('vb', [128, 512], BF16)
        sg5 = s.sb('sg5', [128, 512]); sgg = s.sb('sgg', [128, 512]); stt = s.sb('stt', [128, 8]); junkb = s.sb('junkb', [128, 1024], BF16)
        vT = s.sb('vT', [128, 4, 128], BF16); FT = s.sb('FT', [128, 12, 128], BF16); h1 = [s.sb('h1_%d' % i, [128, 1024]) for i in range(2)]
        a1T = s.sb('a1T', [128, 8, 128], BF16)
        qf = s.sb('qf', [128, 512]); qs = s.sb('qs', [128, 512]); qst = s.sb('qst', [128, 4, 8]); qr = s.sb('qr', [128, 512])
        qo = [s.sb('qo%d' % i, [128, 1024], BF16) for i in range(2)]; ko = [s.sb('ko%d' % i, [128, 256], BF16) for i in range(2)]
        vo = [s.sb('vo%d' % i, [128, 256], BF16) for i in range(2)]; sgo = [s.sb('sgo%d' % i, [128, 1024], BF16) for i in range(2)]
        pTv = s.ps('pTv', [128, 8, 128], BF16); pF = [s.ps('pF%d' % i, [128, 8, 128], BF16) for i in range(2)]
        pA = [s.ps('pA%d' % i, [128, 512]) for i in range(3)]
        npa = [0]

        def nextpa():
            i = npa[0] % 3; npa[0] += 1
            return pA[i], 'pA%d' % i

        def headnorm(src_ps, kps, nh, gain, extra, dst, kdst, ti):
            w = nh * 64
            s.op('act', lambda e: e.activation(out=qs[:, 0:w], in_=src_ps, func=AF.Square), reads=[kps], writes=['qs'])
            s.op('dve', lambda e: e.tensor_reduce(out=qst[:, 0, 0:nh], in_=qs[:, 0:w].rearrange("p (h q) -> p h q", q=64), axis=AX.X, op=ALU.add),
                 reads=['qs'], writes=['qst'])
            s.op('dve', lambda e: e.tensor_scalar(out=qst[:, 1, 0:nh], in0=qst[:, 0, 0:nh], scalar1=1.0 / 64, scalar2=EPS, op0=ALU.mult, op1=ALU.add),
                 reads=['qst'], writes=['qst'])
            s.op('act', lambda e: e.activation(out=qst[:, 2, 0:nh], in_=qst[:, 1, 0:nh], func=AF.Sqrt), reads=['qst'], writes=['qst'])
            s.op('dve', lambda e: e.reciprocal(out=qst[:, 3, 0:nh], in_=qst[:, 2, 0:nh]), reads=['qst'], writes=['qst'])
            s.op('dve', lambda e: e.tensor_tensor(out=qf[:, 0:w].rearrange("p (h q) -> p h q", q=64), in0=src_ps.rearrange("p (h q) -> p h q", q=64),
                                                  in1=qst[:, 3, 0:nh].unsqueeze(2).to_broadcast([128, nh, 64]), op=ALU.mult), reads=[kps, 'qst'], writes=['qf'])
            s.op('dve', lambda e: e.scalar_tensor_tensor(out=qf[:, 0:w].rearrange("p (h q) -> p h q", q=64), in0=qf[:, 0:w].rearrange("p (h q) -> p h q", q=64),
                                                         scalar=extra, in1=gain.unsqueeze(1).to_broadcast([128, nh, 64]), op0=ALU.mult, op1=ALU.mult),
                 reads=['qf', 'reps'], writes=['qf'])
            x5 = qf[:, 0:w].rearrange("p (h a t r) -> p h a t r", a=2, t=2, r=16)
            o5 = dst.rearrange("p (h a t r) -> p h a t r", a=2, t=2, r=16)
            t5 = qr[:, 0:w].rearrange("p (h a t r) -> p h a t r", a=2, t=2, r=16)
            cs = rope[:, ti, 0, :].rearrange("p (a r) -> p a r", a=2).unsqueeze(1).to_broadcast([128, nh, 2, 16])
            sn = rope[:, ti, 1, :].rearrange("p (a r) -> p a r", a=2).unsqueeze(1).to_broadcast([128, nh, 2, 16])
            x1, x2 = x5[:, :, :, 0, :], x5[:, :, :, 1, :]
            s.op('dve', lambda e: e.tensor_tensor(out=t5[:, :, :, 0, :], in0=x1, in1=cs, op=ALU.mult), reads=['qf', 'rope'], writes=['qr'])
            s.op(PENGB, lambda e: e.tensor_tensor(out=t5[:, :, :, 1, :], in0=x2, in1=sn, op=ALU.mult), reads=['qf', 'rope'], writes=['qr1'])
            s.op('dve', lambda e: e.tensor_tensor(out=o5[:, :, :, 0, :], in0=t5[:, :, :, 0, :], in1=t5[:, :, :, 1, :], op=ALU.subtract), reads=['qr', 'qr1'], writes=[kdst])
            s.op('dve', lambda e: e.tensor_tensor(out=t5[:, :, :, 0, :], in0=x2, in1=cs, op=ALU.mult), reads=['qf', 'rope'], writes=['qr'])
            s.op(PENGB, lambda e: e.tensor_tensor(out=t5[:, :, :, 1, :], in0=x1, in1=sn, op=ALU.mult), reads=['qf', 'rope'], writes=['qr1'])
            s.op('dve', lambda e: e.tensor_tensor(out=o5[:, :, :, 1, :], in0=t5[:, :, :, 0, :], in1=t5[:, :, :, 1, :], op=ALU.add), reads=['qr', 'qr1'], writes=[kdst])

        for ti in range(NTILE_B):
            i = ti % 2
            which = 1 if ti == 0 else 0
            rows = slice(ti * 128, (ti + 1) * 128)
            for (dd_, tl, nm) in ((ys_d, ysb, 'ysb'), (z_d, zsb, 'zsb'), (xres_d, xrb, 'xrb'), (y5_d, y5b, 'y5b'), (u_d, ub, 'ub'), (g5_d, g5b, 'g5b')):
                s.dma('sp', 'dm_%s%d' % (nm, i), tl[i][:], dd_[rows, :], writes=['%s%d' % (nm, i)])
            s.op('act', lambda e: e.activation(out=tt[:], in_=zsb[i][:], func=AF.Silu), reads=['zsb%d' % i], writes=['tt'])
            s.op('dve', lambda e: e.tensor_tensor(out=tt[:], in0=tt[:], in1=ysb[i][:], op=ALU.mult), reads=['tt', 'ysb%d' % i], writes=['tt'])
            s.op('act', lambda e: e.activation(out=junkb[:], in_=tt[:], func=AF.Square, accum_out=stt[:, 0:1]), reads=['tt'], writes=['junkb', 'stt'])
            s.op('dve', lambda e: e.tensor_scalar(out=stt[:, 1:2], in0=stt[:, 0:1], scalar1=1.0 / 1024, scalar2=EPS, op0=ALU.mult, op1=ALU.add), reads=['stt'], writes=['stt'])
            s.op('act', lambda e: e.activation(out=stt[:, 2:3], in_=stt[:, 1:2], func=AF.Sqrt), reads=['stt'], writes=['stt'])
            s.op('dve', lambda e: e.reciprocal(out=stt[:, 3:4], in_=stt[:, 2:3]), reads=['stt'], writes=['stt'])
            s.op('dve', lambda e: e.scalar_tensor_tensor(out=Fb[:, 0:1024], in0=tt[:], scalar=stt[:, 3:4], in1=ssdn, op0=ALU.mult, op1=ALU.mult),
                 reads=['tt', 'stt', 'reps'], writes=['Fb'])
            s.op(PENGB, lambda e: e.tensor_tensor(out=vv[:], in0=ub[i][:], in1=ds5, op=ALU.mult), reads=['ub%d' % i, 'reps'], writes=['vv'])
            s.op(PENGB, lambda e: e.tensor_tensor(out=vv[:], in0=vv[:], in1=y5b[i][:], op=ALU.add), reads=['vv', 'y5b%d' % i], writes=['vv'])
            s.op('act', lambda e: e.activation(out=vv[:], in_=vv[:], func=AF.Gelu_apprx_tanh), reads=['vv'], writes=['vv'])
            s.op(PENGB, lambda e: e.tensor_copy(out=vb[:], in_=vv[:]), reads=['vv'], writes=['vb'])
            for k in range(4):
                s.op('pe', lambda e: e.transpose(out=pTv[:, k, :], in_=vb[:, k * 128:(k + 1) * 128], identity=identb[:]), reads=['vb', idk], writes=['pTv'])
            s.op('act', lambda e: e.activation(out=vT[:], in_=pTv[:, 0:4, :], func=AF.Copy), reads=['pTv'], writes=['vT'])
            pg, kpg = nextpa()
            for k in range(4):
                s.op('pe', lambda e: e.matmul(pg[:, :], lhsT=vT[:, k, :], rhs=gluw[:, k, :], start=(k == 0), stop=(k == 3)), reads=['vT', 'gluw'], writes=[kpg])
            s.op('dve', lambda e: e.scalar_tensor_tensor(out=sg5[:], in0=pg[:, :], scalar=1.0, in1=glub, op0=ALU.mult, op1=ALU.add), reads=[kpg, 'reps'], writes=['sg5'])
            s.op('act', lambda e: e.activation(out=sg5[:], in_=sg5[:], func=AF.Sigmoid), reads=['sg5'], writes=['sg5'])
            s.op('act', lambda e: e.activation(out=sgg[:], in_=g5b[i][:], func=AF.Silu), reads=['g5b%d' % i], writes=['sgg'])
            s.op(PENGB, lambda e: e.tensor_tensor(out=sg5[:], in0=sg5[:], in1=sgg[:], op=ALU.mult), reads=['sg5', 'sgg'], writes=['sg5'])
            s.op(PENGB, lambda e: e.tensor_tensor(out=Fb[:, 1024:1536], in0=sg5[:], in1=vv[:], op=ALU.mult), reads=['sg5', 'vv'], writes=['Fb'])
            for k in range(12):
                pf = pF[k // 8]
                s.op('pe', lambda e: e.transpose(out=pf[:, k % 8, :], in_=Fb[:, k * 128:(k + 1) * 128], identity=identb[:]), reads=['Fb', idk], writes=['pF%d' % (k // 8)])
            s.op('act', lambda e: e.activation(out=FT[:, 0:8, :], in_=pF[0][:, :, :], func=AF.Copy), reads=['pF0'], writes=['FT'])
            s.op('act', lambda e: e.activation(out=FT[:, 8:12, :], in_=pF[1][:, 0:4, :], func=AF.Copy), reads=['pF1'], writes=['FT'])
            for half in range(2):
                po, kpo = nextpa()
                cs_ = slice(half * 512, (half + 1) * 512)
                for k in range(12):
                    s.op('pe', lambda e: e.matmul(po[:, :], lhsT=FT[:, k, :], rhs=wout[:, k, cs_], start=(k == 0), stop=(k == 11)), reads=['FT', 'wout'], writes=[kpo])
                s.op('dve', lambda e: e.tensor_tensor(out=h1[i][:, cs_], in0=po[:, :], in1=gate0[:, which, cs_], op=ALU.mult), reads=[kpo, 'gate0'], writes=['h1_%d' % i])
                s.op(PENGB, lambda e: e.tensor_tensor(out=h1[i][:, cs_], in0=h1[i][:, cs_], in1=xrb[i][:, cs_], op=ALU.add), reads=['h1_%d' % i, 'xrb%d' % i], writes=['h1_%d' % i])
            s.dma('act', 'dm_h1%d' % i, h1_d[rows, :], h1[i][:], reads=['h1_%d' % i], writes=['h1o'])
            npj.sub(0, which, a1T, 'a1T', 0, src=(h1[i], 'h1_%d' % i))
            for nt_ in range(5):
                pq, kpq = nextpa()
                for k in range(8):
                    s.op('pe', lambda e: e.matmul(pq[:, :], lhsT=a1T[:, k, :], rhs=w1[:, k, nt_ * 512:(nt_ + 1) * 512], start=(k == 0), stop=(k == 7)),
                         reads=['a1T', 'w1'], writes=[kpq])
                if nt_ < 2:
                    headnorm(pq[:, :], kpq, 8, qg, 0.125, qo[i][:, nt_ * 512:(nt_ + 1) * 512], 'qo%d' % i, ti)
                elif nt_ == 2:
                    headnorm(pq[:, 0:256], kpq, 4, kg, 1.0, ko[i][:, :], 'ko%d' % i, ti)
                    s.op('act', lambda e: e.activation(out=vo[i][:], in_=pq[:, 256:512], func=AF.Copy), reads=[kpq], writes=['vo%d' % i])
                else:
                    s.op('act', lambda e: e.activation(out=sgo[i][:, (nt_ - 3) * 512:(nt_ - 2) * 512], in_=pq[:, :], func=AF.Silu), reads=[kpq], writes=['sgo%d' % i])
            s.dma('act', 'dm_q%d' % i, q_d[rows, :], qo[i][:], reads=['qo%d' % i], writes=['qout'])
            s.dma('act', 'dm_q%d' % i, k_d[rows, :], ko[i][:], reads=['ko%d' % i], writes=['kout'])
            s.dma('act', 'dm_q%d' % i, v_d[rows, :], vo[i][:], reads=['vo%d' % i], writes=['vout'])
            s.dma('act', 'dm_q%d' % i, sg_d[rows, :], sgo[i][:], reads=['sgo%d' % i], writes=['sgout'])
        s.wait_all('sp', ['h1o', 'qout', 'kout', 'vout', 'sgout'])
        s.barrier()
    return nc


def build_l1b(stage=99, nq_tiles=4):
    nc = bass.Bass("TRN2", target_bir_lowering=False)
    dr = lambda n, sh, kind="ExternalInput", dt=F32: nc.dram_tensor(n, list(sh), dt, kind=kind).ap()
    qz_d = dr("qz", [4, 128, 16, 512], dt=BF16)
    kt_d = dr("kt", [128, 2, T], dt=BF16)
    v_d = dr("vv", [128, NCH, 4, 65], dt=BF16)
    sg_d = dr("sg", [2048, 1024], dt=BF16)
    h1_d = dr("h1", [2048, D])
    wo_d = dr("wo", [128, 8, 1024])
    cvec_d = dr("cvec", [128, 8, 2]); aw1g_d = dr("aw1g", [128, 8, 1024]); ab1g_d = dr("ab1g", [128, 1024])
    fg_d = dr("fg", [128, 1024])
    out_d = dr("out", [2048, D], "ExternalOutput")

    with ExitStack() as es:
        s = Sched(nc, es)
        identb, idk = make_identity(s, 'identb', BF16)
        identf = s.sb('identb_f_alias', [1, 1])
        KT = s.sb('KT', [128, 2, T], BF16); V = s.sb('V', [128, NCH, 4, 65], BF16)
        s.dma('sp', 'dm_kt', KT[:, 0, :], kt_d[:, 0, :], writes=['KT'])
        s.dma('act', 'dm_kt', KT[:, 1, :], kt_d[:, 1, :], writes=['KT'])
        for c4 in range(0, NCH, 11):
            s.dma('sp', 'dm_v', V[:, c4:c4 + 11, :, :], v_d[:, c4:c4 + 11, :, :], writes=['V'])
        fg = s.sb('fg', [128, 1024])
        s.dma('sp', 'dm_c', fg[:], fg_d[:, :], writes=['fg'])
        gate1 = s.sb('gate1', [128, 2, 1024])
        with ExitStack() as t0:
            cv = s.sb('cv2', [128, 8, 2], es=t0); scv = s.sb('scv2', [128, 8, 2], es=t0)
            s.dma('sp', 'dm_c', cv[:], cvec_d[:, :, :], writes=['cv2'])
            s.op('act', lambda e: e.activation(out=scv[:], in_=cv[:], func=AF.Silu), reads=['cv2'], writes=['scv'])
            mod_rep(s, nc, scv, aw1g_d, ab1g_d, 1024, gate1, 'gate1', t0, 'g1')
            s.barrier()
        wo = s.sb('wo', [128, 8, 1024], BF16)
        with ExitStack() as t1:
            stg = [s.sb('wst%d' % i, [128, 1024], es=t1) for i in range(2)]
            for k in range(8):
                i = k % 2
                s.dma('sp' if i == 0 else 'act', 'dm_wst%d' % i, stg[i][:], wo_d[:, k, :], writes=['wst%d' % i])
                s.op('pool', lambda e: e.tensor_copy(out=wo[:, k, :], in_=stg[i][:]), reads=['wst%d' % i], writes=['wo'])
            s.barrier()
        idf = None
        qz = [s.sb('qz%d' % i, [128, 16, 512], BF16) for i in range(2)]
        og = s.sb('og', [128, 4, 1024])
        PT = [s.sb('PT%d' % i, [128, 512], BF16) for i in range(3)]
        OTs = s.sb('OTs', [65, 512]); rec = s.sb('rec', [128, 4])
        sgt = [s.sb('sgt%d' % i, [128, 1024], BF16) for i in range(2)]; h1t = [s.sb('h1t%d' % i, [128, 1024]) for i in range(2)]
        ogg = s.sb('ogg', [128, 1024], BF16); ogT = s.sb('ogT', [128, 8, 128], BF16)
        h2 = s.sb('h2', [128, 1024]); osb = [s.sb('osb%d' % i, [128, 1024]) for i in range(2)]
        junk = s.sb('junkc', [128, 1024], BF16); st = s.sb('stc', [128, 8])
        pS = [s.ps('pS%d' % i, [128, 512]) for i in range(3)]
        pO = [s.ps('pO%d' % i, [128, 512]) for i in range(2)]
        pOT = s.ps('pOT', [128, 4, 65])
        pTg = s.ps('pTg', [128, 8, 128], BF16)
        pA = s.ps('pAo', [128, 512])
        idf32 = s.sb('idf32', [128, 128])
        s.op('pool', lambda e: e.memset(idf32[:], 1.0), writes=['idf32'])
        s.op('pool', lambda e: e.affine_select(out=idf32[:], in_=idf32[:], pattern=[[-1, 128]], compare_op=ALU.is_equal,
                                               fill=0.0, base=0, channel_multiplier=1), reads=['idf32'], writes=['idf32'])
        s.dma('sp', 'dm_qz0', qz[0][:], qz_d[0, :, :, :], writes=['qz0'])
        nit = 0
        nho = 0
        for qt in range(nq_tiles):
            qi = qt % 2
            if qt + 1 < nq_tiles:
                s.dma('sp', 'dm_qz%d' % ((qt + 1) % 2), qz[(qt + 1) % 2][:], qz_d[qt + 1, :, :, :], writes=['qz%d' % ((qt + 1) % 2)])
            for h in range(16):
                kh = h // 4; pair, e_ = kh // 2, kh % 2
                po = pO[nho % 2]; kpo = 'pO%d' % (nho % 2); nho += 1

                def qk(kc, it):
                    ps_ = pS[it % 3]
                    s.op('pe', lambda e: e.matmul(ps_[:, :], lhsT=KT[:, pair, kc * 128:(kc + 1) * 128], rhs=qz[qi][:, h, :], start=True, stop=True),
                         reads=['KT', 'qz%d' % qi], writes=['pS%d' % (it % 3)])
                qk(0, nit)
                for kc in range(NCH):
                    it = nit + kc
                    if kc + 1 < NCH:
                        qk(kc + 1, it + 1)
                    s.op('act', lambda e: e.activation(out=PT[it % 3][:], in_=pS[it % 3][:, :], func=AF.Exp), reads=['pS%d' % (it % 3)], writes=['PT%d' % (it % 3)])
                    s.op('pe', lambda e: e.matmul(po[0:65, :], lhsT=V[:, kc, kh, :], rhs=PT[it % 3][:], start=(kc == 0), stop=(kc == NCH - 1)),
                         reads=['V', 'PT%d' % (it % 3)], writes=[kpo])
                nit += NCH
                s.op('act', lambda e: e.activation(out=OTs[:, :], in_=po[0:65, :], func=AF.Copy), reads=[kpo], writes=['OTs'])
                for j in range(4):
                    s.op('pe', lambda e: e.transpose(out=pOT[:, j, :], in_=OTs[:, j * 128:(j + 1) * 128], identity=idf32[0:65, 0:65]),
                         reads=['OTs', 'idf32'], writes=['pOT'])
                s.op('dve', lambda e: e.reciprocal(out=rec[:], in_=pOT[:, :, 64]), reads=['pOT'], writes=['rec'])
                s.op('dve', lambda e: e.tensor_tensor(out=og[:, :, h * 64:(h + 1) * 64], in0=pOT[:, :, 0:64], in1=rec[:].unsqueeze(2).to_broadcast([128, 4, 64]), op=ALU.mult),
                     reads=['pOT', 'rec'], writes=['og'])
            for j in range(4):
                i = j % 2
                rows = slice(qt * 512 + j * 128, qt * 512 + (j + 1) * 128)
                s.dma('sp', 'dm_sg%d' % i, sgt[i][:], sg_d[rows, :], writes=['sgt%d' % i])
                s.dma('sp', 'dm_h1%d' % i, h1t[i][:], h1_d[rows, :], writes=['h1t%d' % i])
                s.op('dve', lambda e: e.tensor_tensor(out=ogg[:], in0=og[:, j, :], in1=sgt[i][:], op=ALU.mult), reads=['og', 'sgt%d' % i], writes=['ogg'])
                for k in range(8):
                    s.op('pe', lambda e: e.transpose(out=pTg[:, k, :], in_=ogg[:, k * 128:(k + 1) * 128], identity=identb[:]), reads=['ogg', idk], writes=['pTg'])
                s.op('act', lambda e: e.activation(out=ogT[:], in_=pTg[:], func=AF.Copy), reads=['pTg'], writes=['ogT'])
                for half in range(2):
                    cs_ = slice(half * 512, (half + 1) * 512)
                    for k in range(8):
                        s.op('pe', lambda e: e.matmul(pA[:, :], lhsT=ogT[:, k, :], rhs=wo[:, k, cs_], start=(k == 0), stop=(k == 7)), reads=['ogT', 'wo'], writes=['pAo'])
                    s.op('dve', lambda e: e.tensor_tensor(out=h2[:, cs_], in0=pA[:, :], in1=gate1[:, 0, cs_], op=ALU.mult), reads=['pAo', 'gate1'], writes=['h2'])
                s.op('pool', lambda e: e.tensor_tensor(out=h2[:], in0=h2[:], in1=h1t[i][:], op=ALU.add), reads=['h2', 'h1t%d' % i], writes=['h2'])
                s.op('act', lambda e: e.activation(out=junk[:], in_=h2[:], func=AF.Square, accum_out=st[:, 0:1]), reads=['h2'], writes=['junkc', 'stc'])
                s.op('dve', lambda e: e.tensor_scalar(out=st[:, 1:2], in0=st[:, 0:1], scalar1=1.0 / D, scalar2=EPS, op0=ALU.mult, op1=ALU.add), reads=['stc'], writes=['stc'])
                s.op('act', lambda e: e.activation(out=st[:, 2:3], in_=st[:, 1:2], func=AF.Sqrt), reads=['stc'], writes=['stc'])
                s.op('dve', lambda e: e.reciprocal(out=st[:, 3:4], in_=st[:, 2:3]), reads=['stc'], writes=['stc'])
                s.op('dve', lambda e: e.scalar_tensor_tensor(out=osb[i][:], in0=h2[:], scalar=st[:, 3:4], in1=fg[:], op0=ALU.mult, op1=ALU.mult),
                     reads=['h2', 'stc', 'fg'], writes=['osb%d' % i])
                s.dma('act', 'dm_o%d' % i, out_d[rows, :], osb[i][:], reads=['osb%d' % i], writes=['out'])
        s.wait_all('sp', ['out'])
        s.barrier()
    return nc


def emit_l0b_f(nc, s, P, xin, yssd_all, ztok_all, gutok_all, y5tok_all, cvec_d, aw0g_d, ab0g_d, adaw1_d, adab1_d, reps_d,
               gluw_d, wout_d, w1_d, rope_d, h1_all, sg_all, v_all, kt_all, qz_all):
    NTILE_B = NCH
    with ExitStack() as es:
        s.es = es
        s.prefix = P
        s.bulk_on = False
        identb, idk = make_identity(s, 'identb', BF16)
        reps = s.sb('reps', [128, 2176]); rope = s.sb('rope', [128, NTILE_B, 2, 32])
        s.dma('sp', 'dm_c', reps[:], reps_d[:, :], writes=['reps'])
        s.dma('sp', 'dm_c', rope[:], rope_d[:, :, :, :], writes=['rope'])
        ssdn = reps[:, 0:1024]; ds5 = reps[:, 1024:1536]; glub = reps[:, 1536:2048]; qg = reps[:, 2048:2112]; kg = reps[:, 2112:2176]
        modT = phase0_mod(s, nc, cvec_d, adaw1_d, adab1_d, 16)
        sc1 = s.sb('sc1', [128, 8, 2])
        s.op('dve', lambda e: e.tensor_scalar(out=sc1[:], in0=modT[:, 8:16, :], scalar1=1.0, scalar2=None, op0=ALU.add), reads=['modT'], writes=['modv'])
        sh = modT[:, 0:8, :]
        gate0 = s.sb('gate0', [128, 2, 1024])
        with ExitStack() as t0:
            cv = s.sb('cv2', [128, 8, 2], es=t0); scv = s.sb('scv2', [128, 8, 2], es=t0)
            s.dma('sp', 'dm_c', cv[:], cvec_d[:, :, :], writes=['cv2'])
            s.op('act', lambda e: e.activation(out=scv[:], in_=cv[:], func=AF.Silu), reads=['cv2'], writes=['scv'])
            mod_rep(s, nc, scv, aw0g_d, ab0g_d, 1024, gate0, 'gate0', t0, 'g0')
            s.barrier()
        gluw = s.sb('gluw', [128, 4, 512], BF16); wout = s.sb('wout', [128, 12, 1024], BF16); w1 = s.sb('w1', [128, 8, 2560], BF16)
        with ExitStack() as t1:
            stg = [s.sb('wst%d' % i, [128, 2560], es=t1) for i in range(2)]
            n = 0
            for (wd, wsb, kw, nk, ncol) in ((gluw_d, gluw, 'gluw', 4, 512), (wout_d, wout, 'wout', 12, 1024), (w1_d, w1, 'w1', 8, 2560)):
                for k in range(nk):
                    i = n % 2; n += 1
                    s.dma('sp' if i == 0 else 'act', 'dm_wst%d' % i, stg[i][:, 0:ncol], wd[:, k, :], writes=['wst%d' % i])
                    s.op('pool', lambda e: e.tensor_copy(out=wsb[:, k, :], in_=stg[i][:, 0:ncol]), reads=['wst%d' % i], writes=[kw])
            s.barrier()

        npj = NormProj(s, nc, None, identb, idk, sc1, sh, es)
        ld = lambda n, w: [s.sb('%s%d' % (n, i), [128, w]) for i in range(2)]
        ysb = ld('ysb', 1024); zsb = ld('zsb', 1024); xrb = ld('xrb', 1024); y5b = ld('y5b', 512); ub = ld('ub', 512); g5b = ld('g5b', 512)
        tt = s.sb('tt', [128, 1024]); Fb = s.sb('Fb', [128, 1536], BF16); vv = s.sb('vv', [128, 512]); vb = s.sb('vb', [128, 512], BF16)
        sg5 = s.sb('sg5', [128, 512]); sgg = s.sb('sgg', [128, 512]); stt = s.sb('stt', [128, 8]); junkb = s.sb('junkb', [128, 1024], BF16)
        vT = s.sb('vT', [128, 4, 128], BF16); FT = s.sb('FT', [128, 12, 128], BF16); h1 = [s.sb('h1_%d' % i, [128, 1024]) for i in range(2)]
        a1T = s.sb('a1T', [128, 8, 128], BF16)
        qf = s.sb('qf', [128, 512]); qs = s.sb('qs', [128, 512]); qst = s.sb('qst', [128, 4, 8]); qr = s.sb('qr', [128, 512])
        qo = [s.sb('qo%d' % i, [128, 1024], BF16) for i in range(2)]; ko = [s.sb('ko%d' % i, [128, 256], BF16) for i in range(2)]
        vo = [s.sb('vo%d' % i, [128, 256], BF16) for i in range(2)]; sgo = [s.sb('sgo%d' % i, [128, 1024], BF16) for i in range(2)]
        qzt = [s.sb('qzt%d' % i, [128, 16, 128], BF16) for i in range(2)]; ktt = [s.sb('ktt%d' % i, [128, 2, 128], BF16) for i in range(2)]
        for i_ in range(2):
            s.op('pool', lambda e: e.memset(qzt[i_][:], 0.0), writes=['qzt%d' % i_])
        pTv = s.ps('pTv', [128, 8, 128], BF16); pF = [s.ps('pF%d' % i, [128, 8, 128], BF16) for i in range(2)]
        pA = [s.ps('pA%d' % i, [128, 512]) for i in range(3)]
        npa = [0]

        def nextpa():
            i = npa[0] % 3; npa[0] += 1
            return pA[i], 'pA%d' % i

        def headnorm(src_ps, kps, nh, gain, extra, dst, kdst, ti):
            w = nh * 64
            s.op('act', lambda e: e.activation(out=qs[:, 0:w], in_=src_ps, func=AF.Square), reads=[kps], writes=['qs'])
            s.op('dve', lambda e: e.tensor_reduce(out=qst[:, 0, 0:nh], in_=qs[:, 0:w].rearrange("p (h q) -> p h q", q=64), axis=AX.X, op=ALU.add),
                 reads=['qs'], writes=['qst'])
            s.op('dve', lambda e: e.tensor_scalar(out=qst[:, 1, 0:nh], in0=qst[:, 0, 0:nh], scalar1=1.0 / 64, scalar2=EPS, op0=ALU.mult, op1=ALU.add),
                 reads=['qst'], writes=['qst'])
            s.op('act', lambda e: e.activation(out=qst[:, 2, 0:nh], in_=qst[:, 1, 0:nh], func=AF.Sqrt), reads=['qst'], writes=['qst'])
            s.op('dve', lambda e: e.reciprocal(out=qst[:, 3, 0:nh], in_=qst[:, 2, 0:nh]), reads=['qst'], writes=['qst'])
            s.op('dve', lambda e: e.tensor_tensor(out=qf[:, 0:w].rearrange("p (h q) -> p h q", q=64), in0=src_ps.rearrange("p (h q) -> p h q", q=64),
                                                  in1=qst[:, 3, 0:nh].unsqueeze(2).to_broadcast([128, nh, 64]), op=ALU.mult), reads=[kps, 'qst'], writes=['qf'])
            s.op('dve', lambda e: e.scalar_tensor_tensor(out=qf[:, 0:w].rearrange("p (h q) -> p h q", q=64), in0=qf[:, 0:w].rearrange("p (h q) -> p h q", q=64),
                                                         scalar=extra, in1=gain.unsqueeze(1).to_broadcast([128, nh, 64]), op0=ALU.mult, op1=ALU.mult),
                 reads=['qf', 'reps'], writes=['qf'])
            x5 = qf[:, 0:w].rearrange("p (h a t r) -> p h a t r", a=2, t=2, r=16)
            o5 = dst.rearrange("p (h a t r) -> p h a t r", a=2, t=2, r=16)
            t5 = qr[:, 0:w].rearrange("p (h a t r) -> p h a t r", a=2, t=2, r=16)
            cs = rope[:, ti, 0, :].rearrange("p (a r) -> p a r", a=2).unsqueeze(1).to_broadcast([128, nh, 2, 16])
            sn = rope[:, ti, 1, :].rearrange("p (a r) -> p a r", a=2).unsqueeze(1).to_broadcast([128, nh, 2, 16])
            x1, x2 = x5[:, :, :, 0, :], x5[:, :, :, 1, :]
            s.op('dve', lambda e: e.tensor_tensor(out=t5[:, :, :, 0, :], in0=x1, in1=cs, op=ALU.mult), reads=['qf', 'rope'], writes=['qr'])
            s.op(PENGB, lambda e: e.tensor_tensor(out=t5[:, :, :, 1, :], in0=x2, in1=sn, op=ALU.mult), reads=['qf', 'rope'], writes=['qr1'])
            s.op('dve', lambda e: e.tensor_tensor(out=o5[:, :, :, 0, :], in0=t5[:, :, :, 0, :], in1=t5[:, :, :, 1, :], op=ALU.subtract), reads=['qr', 'qr1'], writes=[kdst])
            s.op('dve', lambda e: e.tensor_tensor(out=t5[:, :, :, 0, :], in0=x2, in1=cs, op=ALU.mult), reads=['qf', 'rope'], writes=['qr'])
            s.op(PENGB, lambda e: e.tensor_tensor(out=t5[:, :, :, 1, :], in0=x1, in1=sn, op=ALU.mult), reads=['qf', 'rope'], writes=['qr1'])
            s.op('dve', lambda e: e.tensor_tensor(out=o5[:, :, :, 1, :], in0=t5[:, :, :, 0, :], in1=t5[:, :, :, 1, :], op=ALU.add), reads=['qr', 'qr1'], writes=[kdst])

        for ti in range(NTILE_B):
            i = ti % 2
            which = 1 if ti < 2 else 0
            rows = slice(ti * 128, (ti + 1) * 128)
            s.dma('sp', 'dm_xrb%d' % i, xrb[i][:], xin[rows, :], writes=['xrb%d' % i])
            s.dma('sp', 'dm_ysb%d' % i, ysb[i][:].rearrange("p (q c) -> p q c", q=4), yssd_all[:, rows, :].rearrange("q p c -> p q c"), writes=['ysb%d' % i])
            s.dma('sp', 'dm_zsb%d' % i, zsb[i][:].rearrange("p (q c) -> p q c", q=4), ztok_all[:, rows, 0:256].rearrange("q p c -> p q c"), writes=['zsb%d' % i])
            s.dma('sp', 'dm_y5b%d' % i, y5b[i][:].rearrange("p (q c) -> p q c", q=4), y5tok_all[:, rows, :].rearrange("q p c -> p q c"), writes=['y5b%d' % i])
            s.dma('sp', 'dm_ub%d' % i, ub[i][:].rearrange("p (q c) -> p q c", q=4), gutok_all[:, rows, 128:256].rearrange("q p c -> p q c"), writes=['ub%d' % i])
            s.dma('sp', 'dm_g5b%d' % i, g5b[i][:].rearrange("p (q c) -> p q c", q=4), gutok_all[:, rows, 0:128].rearrange("q p c -> p q c"), writes=['g5b%d' % i])
            s.op('act', lambda e: e.activation(out=tt[:], in_=zsb[i][:], func=AF.Silu), reads=['zsb%d' % i], writes=['tt'])
            s.op('dve', lambda e: e.tensor_tensor(out=tt[:], in0=tt[:], in1=ysb[i][:], op=ALU.mult), reads=['tt', 'ysb%d' % i], writes=['tt'])
            s.op('act', lambda e: e.activation(out=junkb[:], in_=tt[:], func=AF.Square, accum_out=stt[:, 0:1]), reads=['tt'], writes=['junkb', 'stt'])
            s.op('dve', lambda e: e.tensor_scalar(out=stt[:, 1:2], in0=stt[:, 0:1], scalar1=1.0 / 1024, scalar2=EPS, op0=ALU.mult, op1=ALU.add), reads=['stt'], writes=['stt'])
            s.op('act', lambda e: e.activation(out=stt[:, 2:3], in_=stt[:, 1:2], func=AF.Sqrt), reads=['stt'], writes=['stt'])
            s.op('dve', lambda e: e.reciprocal(out=stt[:, 3:4], in_=stt[:, 2:3]), reads=['stt'], writes=['stt'])
            s.op('dve', lambda e: e.scalar_tensor_tensor(out=Fb[:, 0:1024], in0=tt[:], scalar=stt[:, 3:4], in1=ssdn, op0=ALU.mult, op1=ALU.mult),
                 reads=['tt', 'stt', 'reps'], writes=['Fb'])
            s.op(PENGB, lambda e: e.tensor_tensor(out=vv[:], in0=ub[i][:], in1=ds5, op=ALU.mult), reads=['ub%d' % i, 'reps'], writes=['vv'])
            s.op(PENGB, lambda e: e.tensor_tensor(out=vv[:], in0=vv[:], in1=y5b[i][:], op=ALU.add), reads=['vv', 'y5b%d' % i], writes=['vv'])
            s.op('act', lambda e: e.activation(out=vv[:], in_=vv[:], func=AF.Gelu_apprx_tanh), reads=['vv'], writes=['vv'])
            s.op(PENGB, lambda e: e.tensor_copy(out=vb[:], in_=vv[:]), reads=['vv'], writes=['vb'])
            for k in range(4):
                s.op('pe', lambda e: e.transpose(out=pTv[:, k, :], in_=vb[:, k * 128:(k + 1) * 128], identity=identb[:]), reads=['vb', idk], writes=['pTv'])
            s.op('act', lambda e: e.activation(out=vT[:], in_=pTv[:, 0:4, :], func=AF.Copy), reads=['pTv'], writes=['vT'])
            pg, kpg = nextpa()
            for k in range(4):
                s.op('pe', lambda e: e.matmul(pg[:, :], lhsT=vT[:, k, :], rhs=gluw[:, k, :], start=(k == 0), stop=(k == 3)), reads=['vT', 'gluw'], writes=[kpg])
            s.op('dve', lambda e: e.scalar_tensor_tensor(out=sg5[:], in0=pg[:, :], scalar=1.0, in1=glub, op0=ALU.mult, op1=ALU.add), reads=[kpg, 'reps'], writes=['sg5'])
            s.op('act', lambda e: e.activation(out=sg5[:], in_=sg5[:], func=AF.Sigmoid), reads=['sg5'], writes=['sg5'])
            s.op('act', lambda e: e.activation(out=sgg[:], in_=g5b[i][:], func=AF.Silu), reads=['g5b%d' % i], writes=['sgg'])
            s.op(PENGB, lambda e: e.tensor_tensor(out=sg5[:], in0=sg5[:], in1=sgg[:], op=ALU.mult), reads=['sg5', 'sgg'], writes=['sg5'])
            s.op(PENGB, lambda e: e.tensor_tensor(out=Fb[:, 1024:1536], in0=sg5[:], in1=vv[:], op=ALU.mult), reads=['sg5', 'vv'], writes=['Fb'])
            for k in range(12):
                pf = pF[k // 8]
                s.op('pe', lambda e: e.transpose(out=pf[:, k % 8, :], in_=Fb[:, k * 128:(k + 1) * 128], identity=identb[:]), reads=['Fb', idk], writes=['pF%d' % (k // 8)])
            s.op('act', lambda e: e.activation(out=FT[:, 0:8, :], in_=pF[0][:, :, :], func=AF.Copy), reads=['pF0'], writes=['FT'])
            s.op('act', lambda e: e.activation(out=FT[:, 8:12, :], in_=pF[1][:, 0:4, :], func=AF.Copy), reads=['pF1'], writes=['FT'])
            for half in range(2):
                po, kpo = nextpa()
                cs_ = slice(half * 512, (half + 1) * 512)
                for k in range(12):
                    s.op('pe', lambda e: e.matmul(po[:, :], lhsT=FT[:, k, :], rhs=wout[:, k, cs_], start=(k == 0), stop=(k == 11)), reads=['FT', 'wout'], writes=[kpo])
                s.op('dve', lambda e: e.tensor_tensor(out=h1[i][:, cs_], in0=po[:, :], in1=gate0[:, which, cs_], op=ALU.mult), reads=[kpo, 'gate0'], writes=['h1_%d' % i])
                s.op(PENGB, lambda e: e.tensor_tensor(out=h1[i][:, cs_], in0=h1[i][:, cs_], in1=xrb[i][:, cs_], op=ALU.add), reads=['h1_%d' % i, 'xrb%d' % i], writes=['h1_%d' % i])
            s.dma('act', 'dm_h1%d' % i, h1_all[rows, :], h1[i][:], reads=['h1_%d' % i], writes=['h1o'])
            npj.sub(0, which, a1T, 'a1T', 0, src=(h1[i], 'h1_%d' % i))
            for nt_ in range(5):
                pq, kpq = nextpa()
                for k in range(8):
                    s.op('pe', lambda e: e.matmul(pq[:, :], lhsT=a1T[:, k, :], rhs=w1[:, k, nt_ * 512:(nt_ + 1) * 512], start=(k == 0), stop=(k == 7)),
                         reads=['a1T', 'w1'], writes=[kpq])
                if nt_ < 2:
                    headnorm(pq[:, :], kpq, 8, qg, 0.125, qo[i][:, nt_ * 512:(nt_ + 1) * 512], 'qo%d' % i, ti)
                elif nt_ == 2:
                    headnorm(pq[:, 0:256], kpq, 4, kg, 1.0, ko[i][:, :], 'ko%d' % i, ti)
                    s.op('act', lambda e: e.activation(out=vo[i][:], in_=pq[:, 256:512], func=AF.Copy), reads=[kpq], writes=['vo%d' % i])
                else:
                    s.op('act', lambda e: e.activation(out=sgo[i][:, (nt_ - 3) * 512:(nt_ - 2) * 512], in_=pq[:, :], func=AF.Silu), reads=[kpq], writes=['sgo%d' % i])
            for j in range(8):
                s.op('pe', lambda e: e.transpose(out=pF[0][:, j, :], in_=qo[i][:, j * 128:(j + 1) * 128], identity=identb[:]), reads=['qo%d' % i, idk], writes=['pF0'])
            s.op('act', lambda e: e.activation(out=qzt[i][0:64, 0:16:2, :], in_=pF[0][0:64, :, :], func=AF.Copy), reads=['pF0'], writes=['qzt%d' % i])
            s.op('act', lambda e: e.activation(out=qzt[i][64:128, 1:16:2, :], in_=pF[0][64:128, :, :], func=AF.Copy), reads=['pF0'], writes=['qzt%d' % i])
            for j in range(2):
                s.op('pe', lambda e: e.transpose(out=pTv[:, j, :], in_=ko[i][:, j * 128:(j + 1) * 128], identity=identb[:]), reads=['ko%d' % i, idk], writes=['pTv'])
            s.op('act', lambda e: e.activation(out=ktt[i][:], in_=pTv[:, 0:2, :], func=AF.Copy), reads=['pTv'], writes=['ktt%d' % i])
            s.dma('act', 'dm_q%d' % i, qz_all[ti, :, :, :], qzt[i][:], reads=['qzt%d' % i], writes=['qout'])
            s.dma('act', 'dm_q%d' % i, kt_all[:, :, rows], ktt[i][:], reads=['ktt%d' % i], writes=['kout'])
            s.dma('act', 'dm_q%d' % i, v_all[rows, :], vo[i][:], reads=['vo%d' % i], writes=['vout'])
            s.dma('act', 'dm_q%d' % i, sg_all[rows, :], sgo[i][:], reads=['sgo%d' % i], writes=['sgout'])
        s.wait_all('sp', ['h1o', 'qout', 'kout', 'vout', 'sgout'])
        s.barrier()
        s.bulk_on = False
    return


def emit_l1b_f(nc, s, P, qz_all, kt_d, v_all, sg_all, h1_all, wo_d, cvec_d, aw1g_d, ab1g_d, fg_d, out_d, nq_tiles=4):
    PERM = [0, 4, 1, 5, 2, 6, 3, 7, 8, 12, 9, 13, 10, 14, 11, 15]
    sx = nc.sync.partition_id() % 4
    tile0 = 2 + 16 * sx
    with ExitStack() as es:
        s.es = es
        s.prefix = P
        s.bulk_on = False
        identb, idk = make_identity(s, 'identb', BF16)
        identf = s.sb('identb_f_alias', [1, 1])
        KT = s.sb('KT', [128, 2, T], BF16); V = s.sb('V', [128, NCH, 4, 65], BF16)
        s.dma('sp', 'dm_kt', KT[:, 0, :], kt_d[:, 0, :], writes=['KT'])
        s.dma('act', 'dm_kt', KT[:, 1, :], kt_d[:, 1, :], writes=['KT'])
        s.op('pool', lambda e: e.memset(V[:, :, :, 64:65], 1.0), writes=['V'])
        for c4 in range(0, NCH, 11):
            for kh_ in range(4):
                s.dma('sp' if kh_ % 2 == 0 else 'act', 'dm_v', V[:, c4:c4 + 11, kh_, 0:64],
                      v_all[c4 * 128:(c4 + 11) * 128, kh_ * 64:(kh_ + 1) * 64].rearrange("(c p) d -> p c d", p=128), reads=['V'], writes=['V'])
        fg = s.sb('fg', [128, 1024])
        s.dma('sp', 'dm_c', fg[:], fg_d[:, :], writes=['fg'])
        gate1 = s.sb('gate1', [128, 2, 1024])
        with ExitStack() as t0:
            cv = s.sb('cv2', [128, 8, 2], es=t0); scv = s.sb('scv2', [128, 8, 2], es=t0)
            s.dma('sp', 'dm_c', cv[:], cvec_d[:, :, :], writes=['cv2'])
            s.op('act', lambda e: e.activation(out=scv[:], in_=cv[:], func=AF.Silu), reads=['cv2'], writes=['scv'])
            mod_rep(s, nc, scv, aw1g_d, ab1g_d, 1024, gate1, 'gate1', t0, 'g1')
            s.barrier()
        wo = s.sb('wo', [128, 8, 1024], BF16)
        with ExitStack() as t1:
            stg = [s.sb('wst%d' % i, [128, 1024], es=t1) for i in range(2)]
            for k in range(8):
                i = k % 2
                s.dma('sp' if i == 0 else 'act', 'dm_wst%d' % i, stg[i][:], wo_d[:, k, :], writes=['wst%d' % i])
                s.op('pool', lambda e: e.tensor_copy(out=wo[:, k, :], in_=stg[i][:]), reads=['wst%d' % i], writes=['wo'])
            s.barrier()
        idf = None
        qz = [s.sb('qz%d' % i, [128, 16, 512], BF16) for i in range(2)]
        og = s.sb('og', [128, 4, 1024])
        PT = [s.sb('PT%d' % i, [128, 1024], BF16) for i in range(2)]
        OTs = s.sb('OTs', [65, 512]); rec = s.sb('rec', [128, 4])
        sgt = [s.sb('sgt%d' % i, [128, 1024], BF16) for i in range(2)]; h1t = [s.sb('h1t%d' % i, [128, 1024]) for i in range(2)]
        ogg = s.sb('ogg', [128, 1024], BF16); ogT = s.sb('ogT', [128, 8, 128], BF16)
        h2 = s.sb('h2', [128, 1024]); osb = [s.sb('osb%d' % i, [128, 1024]) for i in range(2)]
        junk = s.sb('junkc', [128, 1024], BF16); st = s.sb('stc', [128, 8])
        pS = [s.ps('pS%d' % i, [128, 1024]) for i in range(2)]
        pO = [s.ps('pO%d' % i, [128, 512]) for i in range(1)]
        pOT = s.ps('pOT', [128, 4, 65])
        pTg = s.ps('pTg', [128, 8, 128], BF16)
        pA = s.ps('pAo', [128, 512])
        idf32 = s.sb('idf32', [128, 128])
        s.op('pool', lambda e: e.memset(idf32[:], 1.0), writes=['idf32'])
        s.op('pool', lambda e: e.affine_select(out=idf32[:], in_=idf32[:], pattern=[[-1, 128]], compare_op=ALU.is_equal,
                                               fill=0.0, base=0, channel_multiplier=1), reads=['idf32'], writes=['idf32'])
        def dyn_dma(sem, out, in_, wkey, eng='sp'):
            s._deps(eng, [], [wkey])
            if sem not in s.sems:
                s._mk(sem); s.dma_sems.add(sem)
            ins = s.E[eng].dma_start(out=out, in_=in_)
            s.cnt[sem] += 16; ins.then_inc(s.sems[sem], 16); s.nins += 1
            s._done((sem, s.cnt[sem]), [], [wkey])

        qzv = qz_all[bass.ds(tile0, 16), :, :, :]
        sgv = sg_all.rearrange("(t p) c -> t p c", p=128)[bass.ds(2 + 16 * (nc.scalar.partition_id() % 4), 16), :, :]
        h1v = h1_all.rearrange("(t p) c -> t p c", p=128)[bass.ds(2 + 16 * (nc.gpsimd.partition_id() % 4), 16), :, :]

        def load_q(qt_, buf):
            dyn_dma('dm_qz%d' % buf, qz[buf][:].rearrange("p h (j t) -> p h j t", j=4),
                    qzv[4 * qt_:4 * qt_ + 4, :, :, :].rearrange("j p h t -> p h j t"), 'qz%d' % buf)
        load_q(0, 0)
        nit = 0
        nho = 0
        for qt in range(nq_tiles):
            qi = qt % 2
            if qt + 1 < nq_tiles:
                load_q(qt + 1, (qt + 1) % 2)
            for h in range(16):
                kh = PERM[h] // 4; pair, e_ = kh // 2, kh % 2
                po = pO[0]; kpo = 'pO0'; nho += 1
                NP = NCH // 2

                def qk2(pp, it):
                    ps_ = pS[it % 2]
                    for u in range(2):
                        kc = 2 * pp + u
                        s.op('pe', lambda e: e.matmul(ps_[:, u * 512:(u + 1) * 512], lhsT=KT[:, pair, kc * 128:(kc + 1) * 128], rhs=qz[qi][:, h, :], start=True, stop=True),
                             reads=['KT', 'qz%d' % qi], writes=['pS%d' % (it % 2)])
                qk2(0, nit)
                for pp in range(NP):
                    it = nit + pp
                    s.op('act', lambda e: e.activation(out=PT[it % 2][:], in_=pS[it % 2][:, :], func=AF.Exp), reads=['pS%d' % (it % 2)], writes=['PT%d' % (it % 2)])
                    if pp + 1 < NP:
                        qk2(pp + 1, it + 1)
                    for u in range(2):
                        kc = 2 * pp + u
                        s.op('pe', lambda e: e.matmul(po[0:65, :], lhsT=V[:, kc, kh, :], rhs=PT[it % 2][:, u * 512:(u + 1) * 512], start=(kc == 0), stop=(kc == NCH - 1)),
                             reads=['V', 'PT%d' % (it % 2)], writes=[kpo])
                nit += NP
                s.op('dve', lambda e: e.tensor_copy(out=OTs[:, :], in_=po[0:65, :]), reads=[kpo], writes=['OTs'])
                for j in range(4):
                    s.op('pe', lambda e: e.transpose(out=pOT[:, j, :], in_=OTs[:, j * 128:(j + 1) * 128], identity=idf32[0:65, 0:65]),
                         reads=['OTs', 'idf32'], writes=['pOT'])
                s.op('dve', lambda e: e.reciprocal(out=rec[:], in_=pOT[:, :, 64]), reads=['pOT'], writes=['rec'])
                s.op('dve', lambda e: e.tensor_tensor(out=og[:, :, h * 64:(h + 1) * 64], in0=pOT[:, :, 0:64], in1=rec[:].unsqueeze(2).to_broadcast([128, 4, 64]), op=ALU.mult),
                     reads=['pOT', 'rec'], writes=['og'])
            for j in range(4):
                i = j % 2
                rows = slice(qt * 512 + j * 128, qt * 512 + (j + 1) * 128)
                dyn_dma('dm_sg%d' % i, sgt[i][:], sgv[4 * qt + j, :, :], 'sgt%d' % i, 'act')
                dyn_dma('dm_h1%d' % i, h1t[i][:].rearrange("p (a c) -> p a c", a=8), h1v[4 * qt + j, :, :].rearrange("p (a c) -> p a c", a=8), 'h1t%d' % i, 'pool')
                s.op('dve', lambda e: e.tensor_tensor(out=ogg[:], in0=og[:, j, :], in1=sgt[i][:], op=ALU.mult), reads=['og', 'sgt%d' % i], writes=['ogg'])
                for k in range(8):
                    s.op('pe', lambda e: e.transpose(out=pTg[:, k, :], in_=ogg[:, k * 128:(k + 1) * 128], identity=identb[:]), reads=['ogg', idk], writes=['pTg'])
                s.op('act', lambda e: e.activation(out=ogT[:], in_=pTg[:], func=AF.Copy), reads=['pTg'], writes=['ogT'])
                for half in range(2):
                    cs_ = slice(half * 512, (half + 1) * 512)
                    for k in range(8):
                        s.op('pe', lambda e: e.matmul(pA[:, :], lhsT=ogT[:, k, :], rhs=wo[:, k, cs_], start=(k == 0), stop=(k == 7)), reads=['ogT', 'wo'], writes=['pAo'])
                    s.op('dve', lambda e: e.tensor_tensor(out=h2[:, cs_], in0=pA[:, :], in1=gate1[:, 0, cs_], op=ALU.mult), reads=['pAo', 'gate1'], writes=['h2'])
                s.op('pool', lambda e: e.tensor_tensor(out=h2[:], in0=h2[:], in1=h1t[i][:], op=ALU.add), reads=['h2', 'h1t%d' % i], writes=['h2'])
                s.op('act', lambda e: e.activation(out=junk[:], in_=h2[:], func=AF.Square, accum_out=st[:, 0:1]), reads=['h2'], writes=['junkc', 'stc'])
                s.op('dve', lambda e: e.tensor_scalar(out=st[:, 1:2], in0=st[:, 0:1], scalar1=1.0 / D, scalar2=EPS, op0=ALU.mult, op1=ALU.add), reads=['stc'], writes=['stc'])
                s.op('act', lambda e: e.activation(out=st[:, 2:3], in_=st[:, 1:2], func=AF.Sqrt), reads=['stc'], writes=['stc'])
                s.op('dve', lambda e: e.reciprocal(out=st[:, 3:4], in_=st[:, 2:3]), reads=['stc'], writes=['stc'])
                s.op('dve', lambda e: e.scalar_tensor_tensor(out=osb[i][:], in0=h2[:], scalar=st[:, 3:4], in1=fg[:], op0=ALU.mult, op1=ALU.mult),
                     reads=['h2', 'stc', 'fg'], writes=['osb%d' % i])
                s.dma('act', 'dm_o%d' % i, out_d[rows, :], osb[i][:], reads=['osb%d' % i], writes=['out'])
        s.wait_all('sp', ['out'])
        s.barrier()
        s.bulk_on = False
    return


def build_fused():
    nc = bass.Bass("TRN2", target_bir_lowering=False)
    dr = lambda n, sh, kind="ExternalInput", dt=F32: nc.dram_tensor(n, list(sh), dt, kind=kind).ap()
    sc = lambda n, sh, dt=F32: nc.dram_tensor(n, list(sh), dt).ap()
    xin = dr("xin", [T, D]); cvec = dr("cvec", [128, 8, 2])
    adaw0 = dr("adaw0", [128, 8, 2048]); adab0 = dr("adab0", [128, 16])
    NW1 = SSD_NCM * 128 + SSD_NTM
    a1 = [dict(win=dr("a1_win%d" % q, [128, 8, NW1]), convw=dr("a1_convw%d" % q, [128, 4, 5]), convb=dr("a1_convb%d" % q, [128, 4]),
               dtb=dr("a1_dtb%d" % q, [128, 8]), alog=dr("a1_alog%d" % q, [128, 8]), dssd=dr("a1_dssd%d" % q, [128, 4])) for q in range(4)]
    cst = dr("cst", [128, 6, 128])
    a2 = [dict(win=dr("a2_win%d" % q, [128, 8, 640]), lam=dr("a2_lam%d" % q, [128, 3, 8]), bb=dr("a2_bb%d" % q, [128, 2, 4, 16]),
               cc=dr("a2_cc%d" % q, [128, 2, 8, 16])) for q in range(4)]
    esel = dr("esel", [128, 3, 8, 128]); msk = dr("msk", [128, 2, 128]); fsel = dr("fsel", [128, 8, 8, 128])
    aw0g = dr("aw0g", [128, 8, 1024]); ab0g = dr("ab0g", [128, 1024]); adaw1 = dr("adaw1", [128, 8, 2048]); adab1 = dr("adab1", [128, 16])
    reps = dr("reps", [128, 2176]); gluw = dr("gluw", [128, 4, 512]); wout = dr("wout", [128, 12, 1024]); w1 = dr("w1", [128, 8, 2560])
    rope = dr("rope", [128, NCH, 2, 32])
    wo = dr("wo", [128, 8, 1024]); aw1g = dr("aw1g", [128, 8, 1024]); ab1g = dr("ab1g", [128, 1024]); fg = dr("fg", [128, 1024])
    out_d = dr("out", [2048, D], "ExternalOutput")
    ztok_all = sc("ztok_all", [4, T, SSD_NTM]); yssd_all = sc("yssd_all", [4, T, 256]); gutok_all = sc("gutok_all", [4, T, 256])
    y5tok_all = sc("y5tok_all", [4, T, 128]); h1_all = sc("h1_all", [T, D]); sg_all = sc("sg_all", [T, 1024], BF16)
    v_all = sc("v_all", [T, 256], BF16); kt_all = sc("kt_all", [128, 2, T], BF16); qz_all = sc("qz_all", [NCH, 128, 16, 128], BF16)
    dbg = sc("dbg_scratch", [128, 4096])
    aT_all = sc("aT_all", [128, 8, T], BF16)
    with ExitStack() as es0:
        s = Sched(nc, es0)
        emit_norm0(nc, s, 'n0_', xin, cvec, adaw0, adab0, aT_all)
        for q in range(4):
            w = a1[q]
            emit_l0a_ssd(nc, s, 'a1%d_' % q, xin, cvec, adaw0, adab0, w['win'], w['convw'], w['convb'], w['dtb'], w['alog'], w['dssd'], cst,
                         ztok_all[q], yssd_all[q], dbg, aT_all=aT_all)
            w = a2[q]
            emit_l0a_s5(nc, s, 'a2%d_' % q, xin, cvec, adaw0, adab0, w['win'], w['lam'], w['bb'], w['cc'], esel, msk, gutok_all[q], None, dbg,
                        y5tok_d=y5tok_all[q], fsel_d=fsel, aT_all=aT_all)
        emit_l0b_f(nc, s, 'b_', xin, yssd_all, ztok_all, gutok_all, y5tok_all, cvec, aw0g, ab0g, adaw1, adab1, reps, gluw, wout, w1, rope,
                   h1_all, sg_all, v_all, kt_all, qz_all)
        emit_l1b_f(nc, s, 'c_', qz_all, kt_all, v_all, sg_all, h1_all, wo, cvec, aw1g, ab1g, fg, out_d)
        s.force = True
        s.wait_all('sp', ['out'])
        s.barrier()
        print("fused program: %d instructions, %d waits" % (s.nins, s.nwait))
    return nc


HEAD_PERM = [0, 4, 1, 5, 2, 6, 3, 7, 8, 12, 9, 13, 10, 14, 11, 15]


def fused_inputs(I, b):
    d = {}
    s0 = l0a_ssd_inputs(I, b, 0)
    d['xin'] = s0['xin']; d['cvec'] = s0['cvec']; d['adaw0'] = s0['adaw']; d['adab0'] = s0['adab']; d['cst'] = s0['cst']
    for q in range(4):
        a = l0a_ssd_inputs(I, b, q)
        for k in ('win', 'convw', 'convb', 'dtb', 'alog', 'dssd'):
            d['a1_%s%d' % (k, q)] = a[k]
        a = l0a_s5_inputs(I, b, q)
        for k in ('win', 'lam', 'bb', 'cc'):
            d['a2_%s%d' % (k, q)] = a[k]
        if q == 0:
            d['esel'] = a['esel']; d['msk'] = a['msk']
    d['fsel'] = fsel_const()
    d['aw0g'] = kp(I['ada_w'][0][:, 2048:3072]); d['ab0g'] = rep(I['ada_b'][0][2048:3072])
    d['adaw1'] = kp(I['ada_w'][1][:, :2048]); d['adab1'] = colv(I['ada_b'][1][:2048])
    d['reps'] = rep(np.concatenate([I['ev_ssd_norm'][0], I['ev_d_s5'][0], I['ev_glu_b'][0], I['od_q_gain'][0], I['od_k_gain'][0]]))
    d['gluw'] = np.ascontiguousarray(I['ev_glu_w'][0].reshape(4, 128, 512).transpose(1, 0, 2))
    d['wout'] = np.ascontiguousarray(I['ev_w_out'][0].reshape(12, 128, 1024).transpose(1, 0, 2))
    W1 = I['od_w_in'][0]
    hp = np.concatenate([np.arange(64) + 64 * h for h in HEAD_PERM])
    W1p = np.concatenate([W1[:, 0:1024][:, hp], W1[:, 1024:1536], W1[:, 1536:2560][:, hp]], axis=1)
    d['w1'] = kp(W1p)
    cos, sin = rope_tables()
    rp = np.zeros((T, 2, 32), np.float32); rp[:NCTX, 0] = 1.0; rp[NCTX:, 0] = cos; rp[NCTX:, 1] = sin
    d['rope'] = np.ascontiguousarray(rp.reshape(NCH, 128, 2, 32).transpose(1, 0, 2, 3))
    d['wo'] = kp(I['od_w_out'][0][hp, :])
    d['aw1g'] = kp(I['ada_w'][1][:, 2048:3072]); d['ab1g'] = rep(I['ada_b'][1][2048:3072]); d['fg'] = rep(I['final_gain'])
    return d


P = 128
def kp(w):
    K, N = w.shape
    return np.ascontiguousarray(w.reshape(K // P, P, N).transpose(1, 0, 2))
def colv(v):
    return np.ascontiguousarray(v.reshape(-1, P).T)
def rep(v):
    return np.ascontiguousarray(np.tile(np.asarray(v, np.float32).reshape(1, -1), (P, 1)))
def consts_ssd():
    s = np.arange(P)[:, None]; l = np.arange(P)[None, :]
    trif = (s <= l).astype(np.float32); trib = (s >= l).astype(np.float32)
    mbf = np.where(l >= s, 0.0, -30000.0).astype(np.float32)
    mbb = np.where(l <= s, 0.0, -30000.0).astype(np.float32)
    sellast = np.zeros((P, P), np.float32); sellast[P - 1, :] = 1
    selfirst = np.zeros((P, P), np.float32); selfirst[0, :] = 1
    return np.ascontiguousarray(np.stack([trif, trib, mbf, mbb, sellast, selfirst], axis=1))
def l0a_ssd_inputs(I, b, q):
    f = np.float32
    xin = np.ascontiguousarray(np.concatenate([I['ctx'][b], I['x'][b]], axis=0))
    cvec = np.ascontiguousarray(np.stack([colv(I['c'][b]), colv(I['c_ctx'])], axis=2))
    adaw = kp(I['ada_w'][0][:, :2048]); adab = colv(I['ada_b'][0][:2048])
    W = I['ev_w_in'][0]
    zc = W[:, 256 * q:256 * (q + 1)]
    xc = W[:, 1024 + 256 * q:1024 + 256 * (q + 1)]
    Bc = W[:, 2048 + 128 * q:2048 + 128 * (q + 1)]
    Cc = W[:, 2560 + 128 * q:2560 + 128 * (q + 1)]
    dtc = np.concatenate([W[:, 3072 + 4 * q:3072 + 4 * q + 4], W[:, 3072 + 16 + 4 * q:3072 + 16 + 4 * q + 4]], axis=1)
    win = kp(np.concatenate([xc, Bc, Cc, zc, dtc], axis=1))
    cw = I['ev_conv_w'][0]; cb = I['ev_conv_b'][0]
    idx = np.concatenate([np.arange(256 * q, 256 * (q + 1)), 1024 + np.arange(128 * q, 128 * (q + 1)), 1536 + np.arange(128 * q, 128 * (q + 1))])
    convw = np.ascontiguousarray(cw[idx].reshape(4, P, 5).transpose(1, 0, 2)); convb = np.ascontiguousarray(cb[idx].reshape(4, P).T)
    dtb = rep(np.concatenate([I['ev_dt_bias'][0][0][4 * q:4 * q + 4], I['ev_dt_bias'][0][1][4 * q:4 * q + 4]]))
    alog = rep(np.concatenate([I['ev_a_log'][0][0][4 * q:4 * q + 4], I['ev_a_log'][0][1][4 * q:4 * q + 4]]))
    dssd = rep(I['ev_d_ssd'][0][4 * q:4 * q + 4])
    return dict(xin=xin, cvec=cvec, adaw=adaw, adab=adab, win=win, convw=convw, convb=convb, dtb=dtb, alog=alog, dssd=dssd, cst=consts_ssd())

def l0a_s5_inputs(I, b, q):
    xin = np.ascontiguousarray(np.concatenate([I['ctx'][b], I['x'][b]], axis=0))
    cvec = np.ascontiguousarray(np.stack([colv(I['c'][b]), colv(I['c_ctx'])], axis=2))
    adaw = kp(I['ada_w'][0][:, :2048]); adab = colv(I['ada_b'][0][:2048])
    W = I['ev_w_in'][0]
    ucols = W[:, 3104 + 128 * q:3104 + 128 * (q + 1)]
    gcols = W[:, 3616 + 128 * q:3616 + 128 * (q + 1)]
    upad = np.zeros((1024, 384), np.float32)
    for g in range(8):
        hh, r = g // 3, g % 3
        upad[:, hh * 128 + 32 * r:hh * 128 + 32 * r + 16] = ucols[:, 16 * g:16 * g + 16]
    win = kp(np.concatenate([upad, gcols, ucols], axis=1))
    G0 = 8 * q
    def ep(fn):
        return np.ascontiguousarray(np.concatenate([fn(0), fn(1)], axis=0))
    lam = np.zeros((128, 3, 8), np.float32)
    for d in range(2):
        for k in range(4):
            for e in range(2):
                g = G0 + 2 * k + e
                lam[64 * e:64 * e + 64, 0, d * 4 + k] = I['ev_lam_re'][0][d, g]
                lam[64 * e:64 * e + 64, 1, d * 4 + k] = I['ev_lam_im'][0][d, g]
                lam[64 * e:64 * e + 64, 2, d * 4 + k] = I['ev_log_step'][0][d, g]
    bb = np.zeros((128, 2, 4, 16), np.float32); cc = np.zeros((128, 2, 8, 16), np.float32)
    for k in range(4):
        for e in range(2):
            g = G0 + 2 * k + e
            bb[64 * e:64 * e + 64, 0, k] = I['ev_b_re'][0][g]; bb[64 * e:64 * e + 64, 1, k] = I['ev_b_im'][0][g]
            for d in range(2):
                cc[64 * e:64 * e + 64, 0, d * 4 + k] = I['ev_c_re'][0][d, g].T; cc[64 * e:64 * e + 64, 1, d * 4 + k] = I['ev_c_im'][0][d, g].T
    esel = np.zeros((128, 3, 8, 128), np.float32)
    for r in range(3):
        for j in range(16):
            for sx in range(8):
                esel[32 * r + j, r, sx, sx * 16 + j] = 1.0
    msk = np.zeros((128, 2, 128), np.float32)
    sidx = np.arange(128) // 16
    msk[:, 0, :] = (sidx[None, :] >= sidx[:, None]); msk[:, 1, :] = (sidx[None, :] <= sidx[:, None])
    return dict(xin=xin, cvec=cvec, adaw=adaw, adab=adab, win=win, lam=lam, bb=bb, cc=cc, esel=esel, msk=msk)

def fsel_const():
    f = np.zeros((128, 8, 8, 128), np.float32)
    for l in range(8):
        for g in range(8):
            for i in range(16):
                f[l * 16 + i, g, l, g * 16 + i] = 1.0
    return f


def rope_tables():
    rows = 8192 // 64
    row = np.repeat(np.arange(rows), 64).astype(np.float32); col = np.tile(np.arange(64), rows).astype(np.float32)
    inv = (10000.0 ** (-np.arange(16, dtype=np.float32) / 16)).astype(np.float32)
    ang = np.concatenate([row[:, None] * inv, col[:, None] * inv], axis=-1).astype(np.float32)
    return np.cos(ang).astype(np.float32), np.sin(ang).astype(np.float32)

def l0b_inputs(I, b, sidx, ys, z, y5, u, g5):
    def rows(a):
        w = a.shape[1]
        out = np.zeros((128 + 2048, w), np.float32)
        out[0:64] = a[64 * sidx:64 * sidx + 64]
        out[128:] = a[256 + 2048 * sidx:256 + 2048 * (sidx + 1)]
        return out
    xall = np.concatenate([I['ctx'][b], I['x'][b]], 0)
    cvec = np.ascontiguousarray(np.stack([colv(I['c'][b]), colv(I['c_ctx'])], axis=2))
    reps = rep(np.concatenate([I['ev_ssd_norm'][0], I['ev_d_s5'][0], I['ev_glu_b'][0], I['od_q_gain'][0], I['od_k_gain'][0]]))
    cos, sin = rope_tables()
    rp = np.zeros((128 + 2048, 2, 32), np.float32); rp[:128, 0] = 1.0
    rp[128:, 0] = cos[2048 * sidx:2048 * (sidx + 1)]; rp[128:, 1] = sin[2048 * sidx:2048 * (sidx + 1)]
    rope = np.ascontiguousarray(rp.reshape(17, 128, 2, 32).transpose(1, 0, 2, 3))
    return dict(xres=rows(xall), ys=rows(ys), z=rows(z), y5=rows(y5), u=rows(u), g5=rows(g5), cvec=cvec,
                aw0g=kp(I['ada_w'][0][:, 2048:3072]), ab0g=rep(I['ada_b'][0][2048:3072]),
                adaw1=kp(I['ada_w'][1][:, :2048]), adab1=colv(I['ada_b'][1][:2048]), reps=reps,
                gluw=np.ascontiguousarray(I['ev_glu_w'][0].reshape(4, 128, 512).transpose(1, 0, 2)),
                wout=np.ascontiguousarray(I['ev_w_out'][0].reshape(12, 128, 1024).transpose(1, 0, 2)),
                w1=kp(I['od_w_in'][0]), rope=rope)

def l1b_inputs(I, b, sidx, q_rows, k_all, v_all, sg_rows, h1_rows):
    qz = np.zeros((4, 128, 16, 512), BF)
    qq = q_rows.reshape(4, 512, 16, 64)
    for h in range(16):
        e = (h // 4) % 2
        qz[:, 64 * e:64 * e + 64, h, :] = qq[:, :, h, :].transpose(0, 2, 1)
    kt = np.zeros((128, 2, 8448), BF)
    kk = k_all.reshape(8448, 4, 64)
    for kh in range(4):
        kt[64 * (kh % 2):64 * (kh % 2) + 64, kh // 2, :] = kk[:, kh, :].T
    vv = np.ones((128, 66, 4, 65), BF)
    vv[:, :, :, 0:64] = v_all.reshape(66, 128, 4, 64).transpose(1, 0, 2, 3)
    cvec = np.ascontiguousarray(np.stack([colv(I['c'][b]), colv(I['c_ctx'])], axis=2))
    return dict(qz=qz, kt=kt, vv=vv, sg=np.ascontiguousarray(sg_rows), h1=np.ascontiguousarray(h1_rows), wo=kp(I['od_w_out'][0]), cvec=cvec,
                aw1g=kp(I['ada_w'][1][:, 2048:3072]), ab1g=rep(I['ada_b'][1][2048:3072]), fg=rep(I['final_gain']))


_PROGS = {}


def _prog(name, fn):
    if name not in _PROGS:
        _PROGS[name] = fn()
    return _PROGS[name]


def _run(nc, in_maps):
    res = run_bass_kernel_spmd(nc, in_maps, core_ids=list(range(8)))
    return res.results


def kernel(**inputs):
    I = {k: np.asarray(v) for k, v in inputs.items()}
    per_b = [fused_inputs(I, b) for b in range(2)]
    res = _run(_prog('fused', build_fused), [per_b[ci // 4] for ci in range(8)])
    outs = np.zeros((2, NLAT, D), np.float32)
    for ci in range(8):
        outs[ci // 4, 2048 * (ci % 4):2048 * (ci % 4 + 1)] = res[ci]['out']
    return outs


def kernel_unfused(**inputs):
    I = {k: np.asarray(v) for k, v in inputs.items()}
    cores = [(b, q) for b in range(2) for q in range(4)]
    rA1 = _run(_prog('a1', build_l0a_ssd), [l0a_ssd_inputs(I, b, q) for (b, q) in cores])
    rA2 = _run(_prog('a2', build_l0a_s5), [l0a_s5_inputs(I, b, q) for (b, q) in cores])
    per_b = []
    for b in range(2):
        ys = np.concatenate([rA1[4 * b + q]['yssd'] for q in range(4)], axis=1)
        z = np.concatenate([rA1[4 * b + q]['ztok'][:, :256] for q in range(4)], axis=1)
        g5 = np.concatenate([rA2[4 * b + q]['gutok'][:, :128] for q in range(4)], axis=1)
        u = np.concatenate([rA2[4 * b + q]['gutok'][:, 128:] for q in range(4)], axis=1)
        y5 = np.concatenate([rA2[4 * b + q]['y5'].reshape(8, 8, 16, NC8).transpose(3, 1, 0, 2).reshape(T, 128) for q in range(4)], axis=1)
        per_b.append((ys, z, y5, u, g5))
    rB = _run(_prog('b', build_l0b), [l0b_inputs(I, b, sx, *per_b[b]) for (b, sx) in cores])
    outs = np.zeros((2, NLAT, D), np.float32)
    inC = []
    for b in range(2):
        k_all = np.concatenate([rB[4 * b + sx]['k'][:64] for sx in range(4)] + [rB[4 * b + sx]['k'][128:] for sx in range(4)], axis=0)
        v_all = np.concatenate([rB[4 * b + sx]['v'][:64] for sx in range(4)] + [rB[4 * b + sx]['v'][128:] for sx in range(4)], axis=0)
        for sx in range(4):
            r = rB[4 * b + sx]
            inC.append(l1b_inputs(I, b, sx, r['q'][128:], k_all, v_all, r['sg'][128:], r['h1'][128:]))
    rC = _run(_prog('c', build_l1b), inC)
    for ci, (b, sx) in enumerate(cores):
        outs[b, 2048 * sx:2048 * (sx + 1)] = rC[ci]['out']
    return outs
```

```python
import numpy as np
from contextlib import ExitStack
import concourse.bass as bass
import concourse.mybir as mybir
from concourse.bass_utils import run_bass_kernel_spmd
import ml_dtypes

BF = ml_dtypes.bfloat16

F32 = mybir.dt.float32
BF16 = mybir.dt.bfloat16
AF = mybir.ActivationFunctionType
ALU = mybir.AluOpType
AX = mybir.AxisListType

D = 1024
NCTX = 256
NLAT = 8192
T = NCTX + NLAT
NCH = T // 128
EPS = 1e-6
NEG = -30000.0
SKIP_SAME_ENGINE_WAITS_IN_SSD = False
PENGB = 'dve'
PENG4 = 'dve'


class Sched:
    def __init__(self, nc, es, same_engine_sync=True):
        self.nc, self.es = nc, es
        self.sem_es = es
        self.prefix = ''
        self.E = dict(pe=nc.tensor, dve=nc.vector, act=nc.scalar, pool=nc.gpsimd, sp=nc.sync)
        self.sems, self.cnt = {}, {}
        self.seen = {e: {} for e in self.E}
        self.lw, self.rd = {}, {}
        self.same = same_engine_sync
        self.nwait = 0
        self.nins = 0
        self.dma_sems = set()
        self.psum_keys = set()
        for e in ('pe', 'dve', 'act', 'pool'):
            self._mk(e)

    def _mk(self, name):
        self.sems[name] = self.sem_es.enter_context(self.nc.semaphore('s_' + name))
        self.cnt[name] = 0

    def sb(self, name, shape, dt=F32, es=None):
        return (es or self.es).enter_context(self.nc.sbuf_tensor('sb_' + self.prefix + name, list(shape), dt))

    def ps(self, name, shape, dt=F32, es=None):
        self.psum_keys.add(name)
        return (es or self.es).enter_context(self.nc.psum_tensor('ps_' + self.prefix + name, list(shape), dt))

    def _deps(self, eng, reads, writes):
        need = {}

        def add(tok, key=None):
            sn, v = tok
            if sn == eng and key is not None and self.bulk_on and self._bulk(key):
                return
            if sn in self.dma_sems:
                v = self.cnt[sn]
            if v > need.get(sn, 0):
                need[sn] = v
        for k in reads:
            if k in self.lw:
                add(self.lw[k], k)
            if k in self.psum_keys:
                for r in self.rd.get(k, ()):
                    if r[0] != eng:
                        add(r)
        for k in writes:
            if k in self.lw:
                add(self.lw[k], k)
            for r in self.rd.get(k, ()):
                add(r, k)
        E = self.E[eng]
        for sn, v in need.items():
            if sn == eng and (eng == 'pe' or not self.same):
                continue
            if self.seen[eng].get(sn, 0) >= v:
                continue
            E.wait_ge(self.sems[sn], v)
            self.nwait += 1
            self.seen[eng][sn] = v

    BULK = ('aT', 'pre', 'acc', 'xpT', 'BT', 'CT', 'xs', 'Btok', 'zsb', 'pcm', 'ptm', 'np_xt', 'np_xn', 'np_junk', 'np_pT', 'pX',
            'xdf', 'xdb', 'xde', 'dg', 'LT', 'MT', 'pSeg', 'pG', 'pY', 'pO', 'pS', 't1', 't2', 't3', 'ysb', 'stf', 'stb', 'prevf', 'prevb',
            'uT', 'U', 'pU', 'gsb', 'S_re', 'S_im', 'G_re', 'G_im', 'RT', 'r_ta', 'r_tb', 'Xin_', 'pSr', 'pSi', 'pYb', 'Yb', 'y5T', 'ytk', 'pTy',
            'tt', 'Fb', 'vv', 'vb', 'sg5', 'sgg', 'vT', 'FT', 'h1_', 'a1T', 'qf', 'qs', 'qr', 'qo', 'ko', 'vo', 'sgo', 'qzt', 'ktt', 'pTv', 'pF', 'pA',
            'PT', 'OTs', 'pOT', 'og', 'ogg', 'ogT', 'pTg', 'h2', 'osb', 'junkc', 'wb', 'wst', 'w1', 'wout', 'gluw', 'wo', 'aTs', 'q_tw', 'q_WT', 'q_Km',
            'q_Q', 'q_V', 'q_Qz', 'pW', 'Tbf', 'Wz', 'Vz')
    _bulk_cache = {}
    bulk_on = False

    def _bulk(self, key):
        r = self._bulk_cache.get(key)
        if r is None:
            r = any(key == b or (key.startswith(b) and (key[len(b):].isdigit() or key[len(b):].replace('_', '').isdigit() or b.endswith('_')))
                    for b in self.BULK)
            self._bulk_cache[key] = r
        return r

    def _done(self, tok, reads, writes):
        for k in reads:
            self.rd.setdefault(k, []).append(tok)
        for k in writes:
            self.lw[k] = tok
            self.rd[k] = []

    marked = False
    nmark = 0
    skip_after = 1 << 60
    force = False

    def mark(self):
        self.marked = True

    def _skip(self):
        if not self.marked or self.force:
            return False
        self.nmark += 1
        return self.nmark > self.skip_after

    def op(self, eng, fn, reads=(), writes=()):
        if self._skip():
            return
        self._deps(eng, reads, writes)
        ins = fn(self.E[eng])
        self.cnt[eng] += 1
        ins.then_inc(self.sems[eng], 1)
        self.nins += 1
        self._done((eng, self.cnt[eng]), reads, writes)

    def dma(self, eng, sem, out, in_, reads=(), writes=(), **kw):
        if self._skip():
            return
        if sem not in self.sems:
            self._mk(sem)
            self.dma_sems.add(sem)
        self._deps(eng, reads, writes)
        ins = self.E[eng].dma_start(out=out, in_=in_, **kw)
        self.cnt[sem] += 16
        ins.then_inc(self.sems[sem], 16)
        self.nins += 1
        self._done((sem, self.cnt[sem]), reads, writes)

    def wait_all(self, eng, keys):
        self._deps(eng, list(keys), [])

    def barrier(self):
        for eng, E in self.E.items():
            for sn, v in self.cnt.items():
                if v == 0 or self.seen[eng].get(sn, 0) >= v:
                    continue
                if sn == eng and eng == 'pe':
                    continue
                E.wait_ge(self.sems[sn], v)
                self.nwait += 1
                self.seen[eng][sn] = v


def make_identity(s, name, dt):
    idf = s.sb(name + '_f', [128, 128], F32)
    s.op('pool', lambda e: e.memset(idf[:], 1.0), writes=[name + '_f'])
    s.op('pool', lambda e: e.affine_select(out=idf[:], in_=idf[:], pattern=[[-1, 128]], compare_op=ALU.is_equal,
                                           fill=0.0, base=0, channel_multiplier=1), reads=[name + '_f'], writes=[name + '_f'])
    if dt == F32:
        return idf, name + '_f'
    idb = s.sb(name, [128, 128], dt)
    s.op('dve', lambda e: e.tensor_copy(out=idb[:], in_=idf[:]), reads=[name + '_f'], writes=[name])
    return idb, name


def phase0_mod(s, nc, cvec_d, adaw_d, adab_d, nj):
    modT = s.sb('modT', [128, nj, 2])
    with ExitStack() as tes:
        cv = s.sb('cv', [128, 8, 2], es=tes)
        scv = s.sb('scv', [128, 8, 2], es=tes)
        ab = s.sb('ab', [128, nj], es=tes)
        aw = s.sb('aw', [128, 8, nj * 128], es=tes)
        pm = s.ps('pm', [128, 512], es=tes)
        s.dma('sp', 'dm_c', cv[:], cvec_d[:, :, :], writes=['cv'])
        s.dma('sp', 'dm_c', ab[:], adab_d[:, :], writes=['ab'])
        for k in range(8):
            s.dma('sp' if k % 2 == 0 else 'act', 'dm_aw', aw[:, k, :], adaw_d[:, k, :], writes=['aw%d' % k])
        s.op('act', lambda e: e.activation(out=scv[:], in_=cv[:], func=AF.Silu), reads=['cv'], writes=['scv'])
        for j in range(nj):
            for k in range(8):
                s.op('pe', lambda e: e.matmul(pm[:, 2 * j:2 * j + 2], lhsT=aw[:, k, j * 128:(j + 1) * 128], rhs=scv[:, k, :],
                                              start=(k == 0), stop=(k == 7)), reads=['aw%d' % k, 'scv'], writes=['pm'])
        s.op('dve', lambda e: e.tensor_tensor(out=modT[:], in0=pm[:, 0:2 * nj].rearrange("p (j t) -> p j t", t=2),
                                              in1=ab[:].unsqueeze(2).to_broadcast([128, nj, 2]), op=ALU.add),
             reads=['pm', 'ab'], writes=['modT'])
        s.barrier()
    return modT


class NormProj:
    def __init__(self, s, nc, x_d, identb, identb_key, sc1, sh, es):
        self.s, self.nc, self.x_d = s, nc, x_d
        self.identb, self.idk = identb, identb_key
        self.sc1, self.sh = sc1, sh
        self.xt = [s.sb('np_xt%d' % i, [128, D], es=es) for i in range(2)] if x_d is not None else [None, None]
        self.junk = [s.sb('np_junk%d' % i, [128, D], BF16, es=es) for i in range(2)]
        self.xn = [s.sb('np_xn%d' % i, [128, D], BF16, es=es) for i in range(2)]
        self.st = [s.sb('np_st%d' % i, [128, 8], es=es) for i in range(2)]
        self.pT = [s.ps('np_pT%d' % i, [128, 8, 128], BF16, es=es) for i in range(2)]
        self.n = 0

    def sub(self, row0, which, aT, aT_key, col0, src=None):
        s = self.s
        i = self.n % 2
        self.n += 1
        xt, xn, pT = self.xt[i], self.xn[i], self.pT[i]
        kx, kn, kp, kst = 'np_xt%d' % i, 'np_xn%d' % i, 'np_pT%d' % i, 'np_st%d' % i
        if src is None:
            s.dma('sp' if i == 0 else 'act', 'dm_x%d' % i, xt[:], self.x_d[row0:row0 + 128, :], writes=[kx])
        else:
            xt, kx = src
        st = self.st[i]
        s.op('act', lambda e: e.activation(out=self.junk[i][:], in_=xt[:], func=AF.Square, accum_out=st[:, 0:1]),
             reads=[kx], writes=['np_junk%d' % i, kst])
        s.op('dve', lambda e: e.tensor_scalar(out=st[:, 1:2], in0=st[:, 0:1], scalar1=1.0 / D, scalar2=EPS, op0=ALU.mult, op1=ALU.add),
             reads=[kst], writes=[kst])
        s.op('act', lambda e: e.activation(out=st[:, 2:3], in_=st[:, 1:2], func=AF.Sqrt), reads=[kst], writes=[kst])
        s.op('dve', lambda e: e.reciprocal(out=st[:, 3:4], in_=st[:, 2:3]), reads=[kst], writes=[kst])
        s.op('dve', lambda e: e.tensor_scalar(out=xn[:], in0=xt[:], scalar1=st[:, 3:4], scalar2=None, op0=ALU.mult),
             reads=[kx, kst], writes=[kn])
        for k in range(8):
            s.op('pe', lambda e: e.transpose(out=pT[:, k, :], in_=xn[:, k * 128:(k + 1) * 128], identity=self.identb[:]),
                 reads=[kn, self.idk], writes=[kp])
        for k in range(8):
            dst = aT[:, k, col0:col0 + 128]
            if i == 0:
                s.op('act', lambda e: e.activation(out=dst, in_=pT[:, k, :], func=AF.Identity,
                                                   bias=self.sh[:, k, which:which + 1], scale=self.sc1[:, k, which:which + 1]),
                     reads=[kp, 'modv'], writes=[aT_key])
            else:
                s.op('dve', lambda e: e.tensor_scalar(out=dst, in0=pT[:, k, :], scalar1=self.sc1[:, k, which:which + 1],
                                                      scalar2=self.sh[:, k, which:which + 1], op0=ALU.mult, op1=ALU.add),
                     reads=[kp, 'modv'], writes=[aT_key])


def load_cast_weights(s, nc, w_d, wb, wb_key, ncols, es_tmp, tag):
    stg = [s.sb('wstg%s%d' % (tag, i), [128, ncols], es=es_tmp) for i in range(2)]
    for k in range(8):
        i = k % 2
        s.dma('sp' if i == 0 else 'act', 'dm_w%s%d' % (tag, i), stg[i][:], w_d[:, k, :], writes=['wstg%s%d' % (tag, i)])
        s.op('pool', lambda e: e.tensor_copy(out=wb[:, k, :], in_=stg[i][:]), reads=['wstg%s%d' % (tag, i)], writes=[wb_key])


def emit_norm0(nc, s, P, xin, cvec_d, adaw_d, adab_d, aT_all):
    with ExitStack() as es:
        s.es = es
        s.prefix = P
        identb, idk = make_identity(s, 'identb', BF16)
        modT = phase0_mod(s, nc, cvec_d, adaw_d, adab_d, 16)
        sc1 = s.sb('sc1', [128, 8, 2])
        s.op('dve', lambda e: e.tensor_scalar(out=sc1[:], in0=modT[:, 8:16, :], scalar1=1.0, scalar2=None, op0=ALU.add), reads=['modT'], writes=['modv'])
        sh = modT[:, 0:8, :]
        npj = NormProj(s, nc, xin, identb, idk, sc1, sh, es)
        aTs = [s.sb('aTs%d' % i, [128, 8, 128], BF16) for i in range(2)]
        for c in range(NCH):
            i = c % 2
            npj.sub(c * 128, 1 if c < NCTX // 128 else 0, aTs[i], 'aTs%d' % i, 0)
            s.dma('pool', 'dm_aTs%d' % i, aT_all[:, :, c * 128:(c + 1) * 128], aTs[i][:], reads=['aTs%d' % i], writes=['aT_all'])
        s.wait_all('sp', ['aT_all'])
        s.barrier()


SSD_NCM = 4
SSD_NTM = 264
SEGS = [(0, NCTX, 1), (NCTX, NLAT, 0)]


def _finish_dbg(s, dbg_d, srcs):
    s.force = True
    with ExitStack() as de:
        dd = s.sb('ddx', [128, 4096], es=de)
        s.op('dve', lambda e: e.memset(dd[:], 0.0), writes=['ddx'])
        o = 0
        for (ap, key, n) in srcs:
            s.op('dve', lambda e: e.tensor_copy(out=dd[:, o:o + n], in_=ap), reads=[key], writes=['ddx'])
            o += n
        s.dma('sp', 'dm_o', dbg_d[:, :], dd[:], reads=['ddx'], writes=['dbg'])
        s.wait_all('sp', ['dbg'])
        s.barrier()


def emit_l0a_ssd(nc, s, P, xin, cvec_d, adaw_d, adab_d, win_d, convw_d, convb_d, dtb_d, alog_d, dssd_d, cst_d, ztok_d, yssd_d, dbg_d, stage=99, cut=None, aT_all=None):
    NW = SSD_NCM * 128 + SSD_NTM
    with ExitStack() as es:
        s.es = es
        s.prefix = P
        s.bulk_on = SKIP_SAME_ENGINE_WAITS_IN_SSD
        identb, idk = make_identity(s, 'identb', BF16)
        identf, idfk = identb, idk
        identf = None
        cst = s.sb('cst', [128, 6, 128])
        s.dma('sp', 'dm_c', cst[:], cst_d[:, :, :], writes=['cst'])
        convw = s.sb('convw', [128, 4, 5]); convb = s.sb('convb', [128, 4])
        dtb = s.sb('dtb', [128, 8]); alog = s.sb('alog', [128, 8]); dssd = s.sb('dssd', [128, 4])
        s.dma('sp', 'dm_c', convw[:], convw_d[:, :, :], writes=['convw'])
        s.dma('sp', 'dm_c', convb[:], convb_d[:, :], writes=['convb'])
        s.dma('sp', 'dm_c', dtb[:], dtb_d[:, :], writes=['dtb'])
        s.dma('sp', 'dm_c', alog[:], alog_d[:, :], writes=['alog'])
        s.dma('sp', 'dm_c', dssd[:], dssd_d[:, :], writes=['dssd'])
        if aT_all is None:
            modT = phase0_mod(s, nc, cvec_d, adaw_d, adab_d, 16)
            sc1 = s.sb('sc1', [128, 8, 2])
            s.op('dve', lambda e: e.tensor_scalar(out=sc1[:], in0=modT[:, 8:16, :], scalar1=1.0, scalar2=None, op0=ALU.add),
                 reads=['modT'], writes=['modv'])
            sh = modT[:, 0:8, :]
        if stage == 0:
            with ExitStack() as de:
                dd = s.sb('dd', [128, 4096], es=de)
                s.op('dve', lambda e: e.memset(dd[:], 0.0), writes=['dd'])
                s.op('dve', lambda e: e.tensor_copy(out=dd[:, 0:32], in_=modT[:].rearrange("p j t -> p (j t)")), reads=['modT'], writes=['dd'])
                s.dma('sp', 'dm_o', dbg_d[:, :], dd[:], reads=['dd'], writes=['dbg'])
                s.wait_all('sp', ['dbg'])
                s.barrier()
            return nc

        xpT = s.sb('xpT', [128, 2, T], BF16)
        BT = s.sb('BT', [128, T], BF16)
        CT = s.sb('CT', [128, T], BF16)
        xs = s.sb('xs', [128, NCH, 256], BF16)
        Btok = s.sb('Btok', [128, NCH, 128], BF16)
        dtraw = s.sb('dtraw', [128, NCH, 8])

        with ExitStack() as p1:
            wb = s.sb('wb', [128, 8, NW], BF16, es=p1)
            with ExitStack() as wtmp:
                load_cast_weights(s, nc, win_d, wb, 'wb', NW, wtmp, 'a')
                s.barrier()
            if stage == 11:
                _finish_dbg(s, dbg_d, [(wb[:, 3, 0:512], 'wb', 512)])
                return nc
            if aT_all is None:
                npj = NormProj(s, nc, xin, identb, idk, sc1, sh, p1)
                aT = s.sb('aT', [128, 8, 512], BF16, es=p1); kaT = 'aT'
            else:
                aTb = [s.sb('aT%d' % i_, [128, 8, 512], BF16, es=p1) for i_ in range(2)]
            tcnt = 0
            pre = s.sb('pre', [128, 4, 520], es=p1)
            acc = s.sb('acc', [128, 4, 512], es=p1)
            zsb = [s.sb('zsb%d' % i, [128, SSD_NTM], es=p1) for i in range(2)]
            pcm = [s.ps('pcm%d' % i, [128, 512], es=p1) for i in range(2)]
            ptm = [s.ps('ptm%d' % i, [128, 512], es=p1) for i in range(2)]
            ncm = 0
            ntm = 0
            dests = [lambda a, b: xpT[:, 0, a:b], lambda a, b: xpT[:, 1, a:b], lambda a, b: BT[:, a:b], lambda a, b: CT[:, a:b]]
            dkeys = ['xpT', 'xpT', 'BT', 'CT']

            def conv(g0, j_lo, j_hi):
                n = j_hi - j_lo
                for m in range(4):
                    a = acc[:, m, 0:n]
                    s.op('dve', lambda e: e.tensor_scalar(out=a, in0=pre[:, m, j_lo + 2:j_lo + 2 + n], scalar1=convw[:, m, 0:1],
                                                          scalar2=convb[:, m:m + 1], op0=ALU.mult, op1=ALU.add),
                         reads=['pre', 'convw', 'convb'], writes=['acc%d' % m])
                    for k in range(1, 5):
                        s.op('dve', lambda e: e.scalar_tensor_tensor(out=a, in0=pre[:, m, j_lo + 2 + k:j_lo + 2 + k + n],
                                                                     scalar=convw[:, m, k:k + 1], in1=a, op0=ALU.mult, op1=ALU.add),
                             reads=['pre', 'acc%d' % m], writes=['acc%d' % m])
                    s.op('act', lambda e: e.activation(out=dests[m](g0 + j_lo, g0 + j_hi), in_=a, func=AF.Silu),
                         reads=['acc%d' % m], writes=[dkeys[m]])

            for (seg0, seglen, which) in SEGS:
                ntiles = (seglen + 511) // 512
                s.op('pool', lambda e: e.memset(pre[:, :, 0:4], 0.0), reads=['pre'], writes=['pre'])
                for ti in range(ntiles):
                    t0 = seg0 + ti * 512
                    nt = min(512, seg0 + seglen - t0)
                    if aT_all is None:
                        for sub in range(nt // 128):
                            npj.sub(t0 + sub * 128, which, aT, 'aT', sub * 128)
                    else:
                        aT = aTb[tcnt % 2]; kaT = 'aT%d' % (tcnt % 2)
                        s.dma('sp' if tcnt % 2 == 0 else 'act', 'dm_' + kaT, aT[:, :, 0:nt], aT_all[:, :, t0:t0 + nt], writes=[kaT])
                        tcnt += 1
                    if stage == 12:
                        _finish_dbg(s, dbg_d, [(aT[:, 2, 0:256], 'aT', 256)])
                        return nc
                    for m in range(SSD_NCM):
                        pc = pcm[ncm % 2]; kpc = 'pcm%d' % (ncm % 2); ncm += 1
                        for k in range(8):
                            s.op('pe', lambda e: e.matmul(pc[:, 0:nt], lhsT=wb[:, k, m * 128:(m + 1) * 128], rhs=aT[:, k, 0:nt],
                                                          start=(k == 0), stop=(k == 7)), reads=['wb', kaT], writes=[kpc])
                        s.op('act', lambda e: e.activation(out=pre[:, m, 4:4 + nt], in_=pc[:, 0:nt], func=AF.Copy),
                             reads=[kpc], writes=['pre'])
                    if stage == 13:
                        _finish_dbg(s, dbg_d, [(pre[:, 2, 4:260], 'pre', 256)])
                        return nc
                    for sub in range(nt // 128):
                        pt = ptm[ntm % 2]; kpt = 'ptm%d' % (ntm % 2)
                        zb = zsb[ntm % 2]; kzb = 'zsb%d' % (ntm % 2); ntm += 1
                        for k in range(8):
                            s.op('pe', lambda e: e.matmul(pt[:, 0:SSD_NTM], lhsT=aT[:, k, sub * 128:(sub + 1) * 128],
                                                          rhs=wb[:, k, SSD_NCM * 128:NW], start=(k == 0), stop=(k == 7)),
                                 reads=['wb', kaT], writes=[kpt])
                        s.op('act', lambda e: e.activation(out=zb[:], in_=pt[:, 0:SSD_NTM], func=AF.Copy), reads=[kpt], writes=[kzb])
                        c = (t0 + sub * 128) // 128
                        s.op('pool', lambda e: e.tensor_copy(out=dtraw[:, c, :], in_=zb[:, 256:264]), reads=[kzb], writes=['dtraw'])
                        s.dma('sp', 'dm_z%d' % ((ntm - 1) % 2), ztok_d[t0 + sub * 128:t0 + (sub + 1) * 128, :], zb[:], reads=[kzb], writes=['ztok'])
                    if stage == 14:
                        _finish_dbg(s, dbg_d, [(zsb[1][:, 0:264], 'zsb1', 264)])
                        s.wait_all('sp', ['ztok'])
                        return nc
                    conv(t0, 0 if ti == 0 else -2, nt - 2)
                    if stage == 15:
                        _finish_dbg(s, dbg_d, [(BT[:, 0:254], 'BT', 254)])
                        s.wait_all('sp', ['ztok'])
                        return nc
                    s.op('pool', lambda e: e.tensor_copy(out=pre[:, :, 0:4], in_=pre[:, :, nt:nt + 4]), reads=['pre'], writes=['pre'])
                    last_t0, last_nt = t0, nt
                s.op('pool', lambda e: e.memset(pre[:, :, 4:8], 0.0), reads=['pre'], writes=['pre'])
                conv(last_t0 + last_nt, -2, 0)
            if stage == 16:
                _finish_dbg(s, dbg_d, [(BT[:, 0:1024], 'BT', 1024)])
                s.wait_all('sp', ['ztok'])
                return nc
            pX = [s.ps('pX%d' % i, [128, 512], BF16, es=p1) for i in range(2)]
            for c in range(NCH if stage not in (18, 19, 20) else 4):
                px = pX[c % 2]; kpx = 'pX%d' % (c % 2)
                if stage != 20:
                    for hh in range(2):
                        s.op('pe', lambda e: e.transpose(out=px[:, hh * 128:(hh + 1) * 128], in_=xpT[:, hh, c * 128:(c + 1) * 128], identity=identb[:]),
                             reads=['xpT', idk], writes=[kpx])
                    s.op('act', lambda e: e.activation(out=xs[:, c, :], in_=px[:, 0:256], func=AF.Copy), reads=[kpx], writes=['xs'])
                if stage != 19:
                    s.op('pe', lambda e: e.transpose(out=px[:, 256:384], in_=BT[:, c * 128:(c + 1) * 128], identity=identb[:]),
                         reads=['BT', idk], writes=[kpx])
                    s.op('act', lambda e: e.activation(out=Btok[:, c, :], in_=px[:, 256:384], func=AF.Copy), reads=[kpx], writes=['Btok'])
            if stage in (17, 18, 19, 20):
                _finish_dbg(s, dbg_d, [(xs[:, 3, :], 'xs', 256), (Btok[:, 65, :], 'Btok', 128)])
                s.wait_all('sp', ['ztok'])
                return nc
            s.barrier()
        if stage == 1:
            with ExitStack() as de:
                dd = s.sb('dd', [128, 4096], es=de)
                s.op('dve', lambda e: e.tensor_copy(out=dd[:, 0:1024], in_=xpT[:, 0, 0:1024]), reads=['xpT'], writes=['dd'])
                s.op('dve', lambda e: e.tensor_copy(out=dd[:, 1024:2048], in_=BT[:, 0:1024]), reads=['BT'], writes=['dd'])
                s.op('dve', lambda e: e.tensor_copy(out=dd[:, 2048:3072], in_=CT[:, T - 1024:T]), reads=['CT'], writes=['dd'])
                s.op('dve', lambda e: e.tensor_copy(out=dd[:, 3072:3072 + 256], in_=xs[:, 3, :]), reads=['xs'], writes=['dd'])
                s.op('dve', lambda e: e.tensor_copy(out=dd[:, 3328:3328 + 128], in_=Btok[:, 65, :]), reads=['Btok'], writes=['dd'])
                s.op('dve', lambda e: e.tensor_copy(out=dd[:, 3456:3456 + 16], in_=modT[:, :, 0]), reads=['modT'], writes=['dd'])
                s.dma('sp', 'dm_o', dbg_d[:, :], dd[:], reads=['dd'], writes=['dbg'])
                s.wait_all('sp', ['dbg', 'ztok'])
                s.barrier()
            return nc
        if cut is not None:
            s.mark(); s.skip_after = cut
        ident_f = s.sb('identf2', [128, 128])
        s.op('pool', lambda e: e.memset(ident_f[:], 1.0), writes=['identf2'])
        s.op('pool', lambda e: e.affine_select(out=ident_f[:], in_=ident_f[:], pattern=[[-1, 128]], compare_op=ALU.is_equal,
                                               fill=0.0, base=0, channel_multiplier=1), reads=['identf2'], writes=['identf2'])
        ones_f = s.sb('ones_f', [128, 128])
        s.op('pool', lambda e: e.memset(ones_f[:], 1.0), writes=['ones_f'])
        mb4 = s.sb('mb4', [128, 2, 4, 128], BF16)
        for d in range(2):
            for h in range(4):
                s.op('act', lambda e: e.activation(out=mb4[:, d, h, :], in_=cst[:, 2 + d, :], func=AF.Copy), reads=['cst'], writes=['mb4'])
        NC8 = NCH * 8
        shp = [128, 2, NCH, 4]
        dt = s.sb('dt', shp); acs = s.sb('acs', shp); nacs = s.sb('nacs', shp)
        eacs = s.sb('eacs', shp); cdec = s.sb('cdec', shp); wde = s.sb('wde', shp)
        abc = s.sb('abc', [128, 8])
        fl = lambda t, d: t[:, d, :, :].rearrange("p c h -> p (c h)")
        with ExitStack() as p2:
            tmp = s.sb('p2tmp', shp, es=p2)
            pc2l = [s.ps('pc2_%d' % i, [128, 512], es=p2) for i in range(2)]
            pe2l = [s.ps('pe2_%d' % i, [128, 512], es=p2) for i in range(2)]
            for d in range(2):
                s.op('dve', lambda e: e.tensor_tensor(out=tmp[:, d, :, :], in0=dtraw[:, :, 4 * d:4 * d + 4],
                                                      in1=dtb[:, 4 * d:4 * d + 4].unsqueeze(1).to_broadcast([128, NCH, 4]), op=ALU.add),
                     reads=['dtraw', 'dtb'], writes=['p2tmp'])
            fa_ = lambda t: t[:].rearrange("p d c h -> p (d c h)")
            sp1 = s.sb('sp1', shp, es=p2); sp2 = s.sb('sp2', shp, es=p2)
            s.op('dve', lambda e: e.tensor_scalar(out=fa_(sp1), in0=fa_(tmp), scalar1=-1.0, scalar2=None, op0=ALU.mult), reads=['p2tmp'], writes=['sp1'])
            s.op('dve', lambda e: e.tensor_tensor(out=fa_(sp1), in0=fa_(sp1), in1=fa_(tmp), op=ALU.max), reads=['sp1', 'p2tmp'], writes=['sp1'])
            s.op('act', lambda e: e.activation(out=fa_(sp2), in_=fa_(sp1), func=AF.Exp, scale=-1.0), reads=['sp1'], writes=['sp2'])
            s.op('act', lambda e: e.activation(out=fa_(sp2), in_=fa_(sp2), func=AF.Ln, bias=1.0), reads=['sp2'], writes=['sp2'])
            s.op('dve', lambda e: e.tensor_scalar(out=fa_(sp1), in0=fa_(tmp), scalar1=0.0, scalar2=None, op0=ALU.max), reads=['p2tmp', 'sp1'], writes=['sp1'])
            s.op('dve', lambda e: e.tensor_tensor(out=fa_(dt), in0=fa_(sp1), in1=fa_(sp2), op=ALU.add), reads=['sp1', 'sp2'], writes=['dt'])
            s.op('act', lambda e: e.activation(out=abc[:], in_=alog[:], func=AF.Exp), reads=['alog'], writes=['abc'])
            s.op('dve', lambda e: e.tensor_scalar(out=abc[:], in0=abc[:], scalar1=-1.0, scalar2=None, op0=ALU.mult), reads=['abc'], writes=['abc'])
            NQ = NCH * 4
            for d in range(2):
                s.op('pe', lambda e: e.matmul(pc2l[d][:, 0:NQ], lhsT=cst[:, d, :], rhs=fl(dt, d), start=True, stop=True),
                     reads=['cst', 'dt'], writes=['pc2_%d' % d])
                s.op('dve', lambda e: e.tensor_tensor(out=acs[:, d, :, :], in0=pc2l[d][:, 0:NQ].rearrange("p (c h) -> p c h", h=4),
                                                      in1=abc[:, 4 * d:4 * d + 4].unsqueeze(1).to_broadcast([128, NCH, 4]), op=ALU.mult),
                     reads=['pc2_%d' % d, 'abc'], writes=['acs'])
            for d in range(2):
                s.op('pe', lambda e: e.matmul(pe2l[d][:, 0:NQ], lhsT=cst[:, 4 + d, :], rhs=fl(acs, d), start=True, stop=True),
                     reads=['cst', 'acs'], writes=['pe2_%d' % d])
                s.op('act', lambda e: e.activation(out=fl(cdec, d), in_=pe2l[d][:, 0:NQ], func=AF.Exp), reads=['pe2_%d' % d], writes=['cdec'])
                s.op('dve', lambda e: e.scalar_tensor_tensor(out=fl(tmp, d), in0=pe2l[d][:, 0:NQ], scalar=1.0, in1=fl(acs, d), op0=ALU.mult, op1=ALU.subtract),
                     reads=['pe2_%d' % d, 'acs'], writes=['p2tmp'])
            fa = lambda t: t[:].rearrange("p d c h -> p (d c h)")
            s.op('act', lambda e: e.activation(out=fa(wde), in_=fa(tmp), func=AF.Exp), reads=['p2tmp'], writes=['wde'])
            s.op('dve', lambda e: e.tensor_tensor(out=fa(wde), in0=fa(wde), in1=fa(dt), op=ALU.mult), reads=['wde', 'dt'], writes=['wde'])
            s.op('act', lambda e: e.activation(out=fa(eacs), in_=fa(acs), func=AF.Exp), reads=['acs'], writes=['eacs'])
            s.op('dve', lambda e: e.tensor_scalar(out=fa(nacs), in0=fa(acs), scalar1=-1.0, scalar2=None, op0=ALU.mult), reads=['acs'], writes=['nacs'])
            s.barrier()
        if stage in (2, 21):
            fa = lambda t: t[:].rearrange("p d c h -> p (d c h)")
            _finish_dbg(s, dbg_d, [(fa(dt), 'dt', NC8), (fa(acs), 'acs', NC8), (fa(cdec), 'cdec', NC8), (fa(wde), 'wde', NC8)])
            s.wait_all('sp', ['ztok'])
            return nc

        prevb = s.sb('prevb', [128, NCH, 256], BF16)
        stf = s.sb('stf', [128, 256]); stb = s.sb('stb', [128, 256])
        xde = [s.sb('xde%d' % i, [128, 256], BF16) for i in range(2)]
        pS = [s.ps('pS%d' % i, [128, 512]) for i in range(2)]
        h4 = lambda ap: ap.rearrange("p (h q) -> p h q", h=4)
        bc4 = lambda ap: ap.unsqueeze(2).to_broadcast([128, 4, 64])
        s.op('pool', lambda e: e.memset(stb[:], 0.0), writes=['stb'])
        s.op('pool', lambda e: e.memset(stf[:], 0.0), writes=['stf'])
        border = [1, 0] + list(range(NCH - 1, 1, -1))
        for i, c in enumerate(border):
            xd = xde[i % 2]; kx = 'xde%d' % (i % 2); ps_ = pS[i % 2]; kps = 'pS%d' % (i % 2)
            s.op('pool', lambda e: e.tensor_tensor(out=h4(xd[:]), in0=h4(xs[:, c, :]), in1=bc4(wde[:, 1, c, :]), op=ALU.mult),
                 reads=['xs', 'wde'], writes=[kx])
            s.op('pe', lambda e: e.matmul(ps_[:, 0:256], lhsT=Btok[:, c, :], rhs=xd[:], start=True, stop=True), reads=['Btok', kx], writes=[kps])
            s.op('act', lambda e: e.activation(out=prevb[:, c, :], in_=stb[:], func=AF.Copy), reads=['stb'], writes=['prevb'])
            s.op('dve', lambda e: e.tensor_tensor(out=h4(stb[:]), in0=h4(stb[:]), in1=bc4(cdec[:, 1, c, :]), op=ALU.mult),
                 reads=['stb', 'cdec'], writes=['stb'])
            s.op('dve', lambda e: e.scalar_tensor_tensor(out=stb[:], in0=ps_[:, 0:256], scalar=1.0, in1=stb[:], op0=ALU.mult, op1=ALU.add), reads=['stb', kps], writes=['stb'])
        if stage == 3:
            _finish_dbg(s, dbg_d, [(prevb[:, 0, :], 'prevb', 256), (prevb[:, 65, :], 'prevb', 256), (prevb[:, 2, :], 'prevb', 256), (stb[:], 'stb', 256)])
            s.wait_all('sp', ['ztok'])
            return nc

        prevf = s.sb('prevf', [128, 256], BF16)
        s.op('pool', lambda e: e.memset(prevf[:], 0.0), writes=['prevf'])
        xdf = [s.sb('xdf%d' % i, [128, 256], BF16) for i in range(2)]
        xdb = [s.sb('xdb%d' % i, [128, 256], BF16) for i in range(2)]
        dg = [s.sb('dg%d' % i, [128, 4, 128]) for i in range(2)]
        LT = [s.sb('LT%d' % i, [128, 4, 128]) for i in range(2)]
        MT = [[s.sb('MT%d_%d' % (d_, i_), [128, 4, 128], BF16) for i_ in range(2)] for d_ in range(2)]
        t1 = s.sb('t1', [128, 256]); t2 = s.sb('t2', [128, 256]); t3 = s.sb('t3', [128, 256])
        ysb = [s.sb('ysb%d' % i, [128, 256]) for i in range(2)]
        pG = s.ps('pG', [128, 512]); pSeg = [s.ps('pSeg%d' % i, [128, 512]) for i in range(2)]
        pY = s.ps('pY', [128, 512]); pO = s.ps('pO', [128, 512])
        def front4(c):
            i = c % 2
            cb = slice(c * 128, (c + 1) * 128)
            s.op(PENG4, lambda e: e.tensor_tensor(out=h4(xdf[i][:]), in0=h4(xs[:, c, :]), in1=bc4(dt[:, 0, c, :]), op=ALU.mult),
                 reads=['xs', 'dt'], writes=['xdf%d' % i])
            s.op(PENG4, lambda e: e.tensor_tensor(out=h4(xdb[i][:]), in0=h4(xs[:, c, :]), in1=bc4(dt[:, 1, c, :]), op=ALU.mult),
                 reads=['xs', 'dt'], writes=['xdb%d' % i])
            s.op(PENG4, lambda e: e.tensor_tensor(out=h4(xde[i][:]), in0=h4(xs[:, c, :]), in1=bc4(wde[:, 0, c, :]), op=ALU.mult),
                 reads=['xs', 'wde'], writes=['xde%d' % i])
            s.op('pe', lambda e: e.matmul(pG[:, 0:128], lhsT=BT[:, cb], rhs=CT[:, cb], start=True, stop=True), reads=['BT', 'CT'], writes=['pG'])
            for d in range(2):
                s.op('dve', lambda e: e.tensor_tensor(out=dg[d][:], in0=ident_f[:].unsqueeze(1).to_broadcast([128, 4, 128]),
                                                      in1=acs[:, d, c, :].unsqueeze(2).to_broadcast([128, 4, 128]), op=ALU.mult),
                     reads=['identf2', 'acs'], writes=['dg%d' % d])
            for d in range(2):
                s.op('pe', lambda e: e.matmul(pSeg[d][:, :], lhsT=ones_f[:], rhs=dg[d][:].rearrange("p h l -> p (h l)"), start=True, stop=False),
                     reads=['ones_f', 'dg%d' % d], writes=['pSeg%d' % d])
                s.op('pe', lambda e: e.matmul(pSeg[d][:, :], lhsT=identb[:], rhs=mb4[:, d, :, :].rearrange("p h l -> p (h l)"), start=False, stop=True),
                     reads=[idk, 'mb4'], writes=['pSeg%d' % d])
            for d in range(2):
                for h in range(4):
                    s.op('act', lambda e: e.activation(out=LT[d][:, h, :], in_=pSeg[d][:, h * 128:(h + 1) * 128], func=AF.Exp,
                                                       bias=nacs[:, d, c, h:h + 1], scale=1.0),
                         reads=['pSeg%d' % d, 'nacs'], writes=['LT%d' % d])
            for d in range(2):
                s.op('dve', lambda e: e.tensor_tensor(out=MT[d][i][:], in0=LT[d][:], in1=pG[:, 0:128].unsqueeze(1).to_broadcast([128, 4, 128]), op=ALU.mult),
                     reads=['LT%d' % d, 'pG'], writes=['MT%d_%d' % (d, i)])
        def back4(c):
            i = c % 2
            cb = slice(c * 128, (c + 1) * 128)
            for h in range(4):
                hs = slice(h * 64, (h + 1) * 64)
                s.op('pe', lambda e: e.matmul(pY[:, hs], lhsT=MT[0][i][:, h, :], rhs=xdf[i][:, hs], start=True, stop=False),
                     reads=['MT0_%d' % i, 'xdf%d' % i], writes=['pY'])
                s.op('pe', lambda e: e.matmul(pY[:, hs], lhsT=MT[1][i][:, h, :], rhs=xdb[i][:, hs], start=False, stop=True),
                     reads=['MT1_%d' % i, 'xdb%d' % i], writes=['pY'])
            s.op('pe', lambda e: e.matmul(pO[:, 0:256], lhsT=CT[:, cb], rhs=prevf[:], start=True, stop=True), reads=['CT', 'prevf'], writes=['pO'])
            s.op('pe', lambda e: e.matmul(pO[:, 256:512], lhsT=CT[:, cb], rhs=prevb[:, c, :], start=True, stop=True), reads=['CT', 'prevb'], writes=['pO'])
            ps_ = pS[i]; kps = 'pS%d' % i
            s.op('pe', lambda e: e.matmul(ps_[:, 0:256], lhsT=Btok[:, c, :], rhs=xde[i][:], start=True, stop=True), reads=['Btok', 'xde%d' % i], writes=[kps])
            s.op('dve', lambda e: e.tensor_tensor(out=h4(stf[:]), in0=h4(stf[:]), in1=bc4(cdec[:, 0, c, :]), op=ALU.mult), reads=['stf', 'cdec'], writes=['stf'])
            s.op('dve', lambda e: e.scalar_tensor_tensor(out=stf[:], in0=ps_[:, 0:256], scalar=1.0, in1=stf[:], op0=ALU.mult, op1=ALU.add), reads=['stf', kps], writes=['stf'])
            s.op('act', lambda e: e.activation(out=prevf[:], in_=stf[:], func=AF.Copy), reads=['stf'], writes=['prevf'])
            s.op('dve', lambda e: e.tensor_tensor(out=h4(t1[:]), in0=h4(pO[:, 0:256]), in1=bc4(eacs[:, 0, c, :]), op=ALU.mult), reads=['pO', 'eacs'], writes=['t1'])
            s.op('dve', lambda e: e.tensor_tensor(out=h4(t2[:]), in0=h4(pO[:, 256:512]), in1=bc4(eacs[:, 1, c, :]), op=ALU.mult), reads=['pO', 'eacs'], writes=['t2'])
            s.op(PENG4, lambda e: e.tensor_tensor(out=h4(t3[:]), in0=h4(xs[:, c, :]), in1=bc4(dssd[:, 0:4]), op=ALU.mult), reads=['xs', 'dssd'], writes=['t3'])
            s.op(PENG4, lambda e: e.tensor_tensor(out=t1[:], in0=t1[:], in1=t2[:], op=ALU.add), reads=['t1', 't2'], writes=['t1'])
            s.op(PENG4, lambda e: e.tensor_tensor(out=t1[:], in0=t1[:], in1=t3[:], op=ALU.add), reads=['t1', 't3'], writes=['t1'])
            s.op('dve', lambda e: e.scalar_tensor_tensor(out=ysb[i][:], in0=pY[:, 0:256], scalar=1.0, in1=t1[:], op0=ALU.mult, op1=ALU.add), reads=['t1', 'pY'], writes=['ysb%d' % i])
            s.dma('sp', 'dm_y%d' % i, yssd_d[c * 128:(c + 1) * 128, :], ysb[i][:], reads=['ysb%d' % i], writes=['yssd'])
        front4(0)
        for c in range(NCH):
            if c + 1 < NCH:
                front4(c + 1)
            back4(c)
        s.wait_all('sp', ['yssd', 'ztok'])
        s.barrier()
        s.bulk_on = False
    return nc


def build_l0a_ssd(stage=99, cut=None):
    nc = bass.Bass("TRN2", target_bir_lowering=False)
    dr = lambda n, sh, kind="ExternalInput", dt=F32: nc.dram_tensor(n, list(sh), dt, kind=kind).ap()
    xin = dr("xin", [T, D])
    cvec_d = dr("cvec", [128, 8, 2]); adaw_d = dr("adaw", [128, 8, 2048]); adab_d = dr("adab", [128, 16])
    NW = SSD_NCM * 128 + SSD_NTM
    win_d = dr("win", [128, 8, NW])
    convw_d = dr("convw", [128, 4, 5]); convb_d = dr("convb", [128, 4])
    dtb_d = dr("dtb", [128, 8]); alog_d = dr("alog", [128, 8]); dssd_d = dr("dssd", [128, 4])
    cst_d = dr("cst", [128, 6, 128])
    ztok_d = dr("ztok", [T, SSD_NTM], "ExternalOutput")
    yssd_d = dr("yssd", [T, 256], "ExternalOutput")
    dbg_d = dr("dbg", [128, 4096], "ExternalOutput")

    with ExitStack() as es0:
        s = Sched(nc, es0)
        emit_l0a_ssd(nc, s, '', xin, cvec_d, adaw_d, adab_d, win_d, convw_d, convb_d, dtb_d, alog_d, dssd_d, cst_d, ztok_d, yssd_d, dbg_d, stage=stage, cut=cut)
    return nc


NC8 = T // 8
NCX8 = NCTX // 8
TWO_PI = 6.283185307179586
NT8 = [(0, 512), (512, 512), (1024, 32)]


def emit_l0a_s5(nc, s, P, xin, cvec_d, adaw_d, adab_d, win_d, lam_d, bb_d, cc_d, esel_d, msk_d, gu_d, y5_d, dbg_d, y5tok_d=None, fsel_d=None, stage=99, cut=None, aT_all=None):
    NW = 640
    with ExitStack() as es:
        s.es = es
        s.prefix = P
        identb, idk = make_identity(s, 'identb', BF16)
        identf = s.sb('identf2', [128, 128])
        s.op('pool', lambda e: e.memset(identf[:], 1.0), writes=['identf2'])
        s.op('pool', lambda e: e.affine_select(out=identf[:], in_=identf[:], pattern=[[-1, 128]], compare_op=ALU.is_equal,
                                               fill=0.0, base=0, channel_multiplier=1), reads=['identf2'], writes=['identf2'])
        lam = s.sb('lam', [128, 3, 8]); bb = s.sb('bb', [128, 2, 4, 16]); cc = s.sb('cc', [128, 2, 8, 16])
        eself = s.sb('eself', [128, 3, 8, 128]); msk = s.sb('msk', [128, 2, 128])
        esel = s.sb('esel', [128, 3, 8, 128], BF16)
        s.dma('sp', 'dm_c', lam[:], lam_d[:, :, :], writes=['lam'])
        s.dma('sp', 'dm_c', bb[:], bb_d[:, :, :, :], writes=['bb'])
        s.dma('sp', 'dm_c', cc[:], cc_d[:, :, :, :], writes=['cc'])
        s.dma('sp', 'dm_c', eself[:], esel_d[:, :, :, :], writes=['eself'])
        s.dma('sp', 'dm_c', msk[:], msk_d[:, :, :], writes=['msk'])
        s.op('pool', lambda e: e.tensor_copy(out=esel[:], in_=eself[:]), reads=['eself'], writes=['esel'])
        if aT_all is None:
            modT = phase0_mod(s, nc, cvec_d, adaw_d, adab_d, 16)
            sc1 = s.sb('sc1', [128, 8, 2])
            s.op('dve', lambda e: e.tensor_scalar(out=sc1[:], in0=modT[:, 8:16, :], scalar1=1.0, scalar2=None, op0=ALU.add),
                 reads=['modT'], writes=['modv'])
            sh = modT[:, 0:8, :]

        U = s.sb('U', [128, 8, NC8], BF16)

        with ExitStack() as p1:
            wb = s.sb('wb', [128, 8, NW], BF16, es=p1)
            with ExitStack() as wtmp:
                load_cast_weights(s, nc, win_d, wb, 'wb', NW, wtmp, 'a')
                s.barrier()
            if aT_all is None:
                npj = NormProj(s, nc, xin, identb, idk, sc1, sh, p1)
                aT = s.sb('aT', [128, 8, 512], BF16, es=p1); kaT = 'aT'
            else:
                aTb = [s.sb('aT%d' % i_, [128, 8, 512], BF16, es=p1) for i_ in range(2)]
            tcnt = 0
            uT = s.sb('uT', [128, 3, 512], BF16, es=p1)
            gsb = [s.sb('gsb%d' % i, [128, 256], es=p1) for i in range(2)]
            pcm = [s.ps('pcm%d' % i, [128, 512], es=p1) for i in range(2)]
            ptm = [s.ps('ptm%d' % i, [128, 512], es=p1) for i in range(2)]
            pU = s.ps('pU', [128, 512], es=p1)
            ncm = 0; ntm = 0
            if cut is not None:
                s.mark(); s.skip_after = cut
            for (seg0, seglen, which) in SEGS:
                ntiles = (seglen + 511) // 512
                for ti in range(ntiles):
                    t0 = seg0 + ti * 512
                    nt = min(512, seg0 + seglen - t0)
                    if aT_all is None:
                        for sub in range(nt // 128):
                            npj.sub(t0 + sub * 128, which, aT, 'aT', sub * 128)
                    else:
                        aT = aTb[tcnt % 2]; kaT = 'aT%d' % (tcnt % 2)
                        s.dma('sp' if tcnt % 2 == 0 else 'act', 'dm_' + kaT, aT[:, :, 0:nt], aT_all[:, :, t0:t0 + nt], writes=[kaT])
                        tcnt += 1
                    for m in range(3):
                        pc = pcm[ncm % 2]; kpc = 'pcm%d' % (ncm % 2); ncm += 1
                        for k in range(8):
                            s.op('pe', lambda e: e.matmul(pc[:, 0:nt], lhsT=wb[:, k, m * 128:(m + 1) * 128], rhs=aT[:, k, 0:nt],
                                                          start=(k == 0), stop=(k == 7)), reads=['wb', kaT], writes=[kpc])
                        s.op('act', lambda e: e.activation(out=uT[:, m, 0:nt], in_=pc[:, 0:nt], func=AF.Copy), reads=[kpc], writes=['uT'])
                    for sub in range(nt // 128):
                        pt = ptm[ntm % 2]; kpt = 'ptm%d' % (ntm % 2)
                        gb = gsb[ntm % 2]; kgb = 'gsb%d' % (ntm % 2); ntm += 1
                        for k in range(8):
                            s.op('pe', lambda e: e.matmul(pt[:, 0:256], lhsT=aT[:, k, sub * 128:(sub + 1) * 128], rhs=wb[:, k, 384:640],
                                                          start=(k == 0), stop=(k == 7)), reads=['wb', kaT], writes=[kpt])
                        s.op('act', lambda e: e.activation(out=gb[:], in_=pt[:, 0:256], func=AF.Copy), reads=[kpt], writes=[kgb])
                        s.dma('sp', 'dm_g%d' % ((ntm - 1) % 2), gu_d[t0 + sub * 128:t0 + (sub + 1) * 128, :], gb[:], reads=[kgb], writes=['gutok'])
                    ncs = nt // 8; c0 = t0 // 8
                    for g in range(8):
                        hh, r = g // 3, g % 3
                        for sx in range(8):
                            s.op('pe', lambda e: e.matmul(pU[:, g * 64:g * 64 + ncs], lhsT=esel[:, r, sx, :],
                                                          rhs=uT[:, hh, sx:nt:8], start=(sx == 0), stop=(sx == 7)),
                                 reads=['esel', 'uT'], writes=['pU'])
                    s.op('act', lambda e: e.activation(out=U[:, :, c0:c0 + ncs], in_=pU[:].rearrange("p (g c) -> p g c", c=64)[:, :, 0:ncs], func=AF.Copy),
                         reads=['pU'], writes=['U'])
            s.barrier()
        if stage == 1:
            _finish_dbg(s, dbg_d, [(U[:, 3, 0:1056], 'U', 1056), (U[:, 6, 0:1056], 'U', 1056)])
            s.wait_all('sp', ['gutok'])
            return nc
        Tbf = s.sb('Tbf', [128, 16, 128], BF16)
        Wz_re = s.sb('Wz_re', [128, 2, 8, 128], BF16); Wz_im = s.sb('Wz_im', [128, 2, 8, 128], BF16)
        Vz_re = s.sb('Vz_re', [128, 2, 8, 128], BF16); Vz_imn = s.sb('Vz_imn', [128, 2, 8, 128], BF16)
        r8 = s.sb('r8', [128, 8]); zz = s.sb('zz', [128, 2, 8])
        for tz, kz in ((Wz_re, 'Wz_re'), (Wz_im, 'Wz_im'), (Vz_re, 'Vz_re'), (Vz_imn, 'Vz_imn')):
            s.op('pool', lambda e: e.memset(tz[:], 0.0), writes=[kz])
        with ExitStack() as p2:
            sv = lambda n, shape=(128, 8): s.sb('q_' + n, list(shape), es=p2)
            DV = lambda fn, r, w: s.op('dve', fn, reads=r, writes=w)
            step = sv('step'); th = sv('th'); ml = sv('ml'); mag = sv('mag'); tq = sv('tq'); thr = sv('thr'); sn = sv('sn'); cs = sv('cs')
            ar = sv('ar'); ai = sv('ai'); nai = sv('nai'); arm1 = sv('arm1'); den = sv('den'); f_re = sv('f_re'); f_im = sv('f_im'); nf_im = sv('nf_im')
            t1 = sv('t1'); t2 = sv('t2'); ir8 = sv('ir8'); im2 = sv('im2'); avr = sv('avr'); avi = sv('avi'); navi = sv('navi')
            lr, li, ls = lam[:, 0, :], lam[:, 1, :], lam[:, 2, :]
            s.op('act', lambda e: e.activation(out=step[:], in_=ls, func=AF.Exp), reads=['lam'], writes=['q_step'])
            DV(lambda e: e.tensor_tensor(out=th[:], in0=li, in1=step[:], op=ALU.mult), ['lam', 'q_step'], ['q_th'])
            DV(lambda e: e.tensor_tensor(out=ml[:], in0=lr, in1=step[:], op=ALU.mult), ['lam', 'q_step'], ['q_ml'])
            s.op('act', lambda e: e.activation(out=mag[:], in_=ml[:], func=AF.Exp), reads=['q_ml'], writes=['q_mag'])
            s.op('act', lambda e: e.activation(out=r8[:], in_=ml[:], func=AF.Exp, scale=8.0), reads=['q_ml'], writes=['r8'])
            s.op('act', lambda e: e.activation(out=ir8[:], in_=ml[:], func=AF.Exp, scale=-8.0), reads=['q_ml'], writes=['q_ir8'])
            DV(lambda e: e.tensor_scalar(out=tq[:], in0=th[:], scalar1=1.0 / TWO_PI, scalar2=12582912.0, op0=ALU.mult, op1=ALU.add), ['q_th'], ['q_tq'])
            DV(lambda e: e.tensor_scalar(out=tq[:], in0=tq[:], scalar1=12582912.0, scalar2=None, op0=ALU.subtract), ['q_tq'], ['q_tq'])
            DV(lambda e: e.scalar_tensor_tensor(out=thr[:], in0=tq[:], scalar=-TWO_PI, in1=th[:], op0=ALU.mult, op1=ALU.add), ['q_tq', 'q_th'], ['q_thr'])
            DV(lambda e: e.tensor_scalar(out=thr[:], in0=thr[:], scalar1=-3.1415925, scalar2=3.1415925, op0=ALU.max, op1=ALU.min), ['q_thr'], ['q_thr'])
            s.op('act', lambda e: e.activation(out=sn[:], in_=thr[:], func=AF.Sin), reads=['q_thr'], writes=['q_sn'])
            DV(lambda e: e.tensor_scalar(out=t2[:], in0=thr[:], scalar1=-1.0, scalar2=None, op0=ALU.mult), ['q_thr'], ['q_t2'])
            DV(lambda e: e.tensor_tensor(out=t1[:], in0=thr[:], in1=t2[:], op=ALU.max), ['q_thr', 'q_t2'], ['q_t1'])
            DV(lambda e: e.tensor_scalar(out=t1[:], in0=t1[:], scalar1=-1.0, scalar2=1.5707963, op0=ALU.mult, op1=ALU.add), ['q_t1'], ['q_t1'])
            s.op('act', lambda e: e.activation(out=cs[:], in_=t1[:], func=AF.Sin), reads=['q_t1'], writes=['q_cs'])
            DV(lambda e: e.tensor_tensor(out=ar[:], in0=mag[:], in1=cs[:], op=ALU.mult), ['q_mag', 'q_cs'], ['q_ar'])
            DV(lambda e: e.tensor_tensor(out=ai[:], in0=mag[:], in1=sn[:], op=ALU.mult), ['q_mag', 'q_sn'], ['q_ai'])
            DV(lambda e: e.tensor_scalar(out=nai[:], in0=ai[:], scalar1=-1.0, scalar2=None, op0=ALU.mult), ['q_ai'], ['q_nai'])
            DV(lambda e: e.tensor_scalar(out=arm1[:], in0=ar[:], scalar1=-1.0, scalar2=None, op0=ALU.add), ['q_ar'], ['q_arm1'])
            DV(lambda e: e.tensor_tensor(out=den[:], in0=lr, in1=lr, op=ALU.mult), ['lam'], ['q_den'])
            DV(lambda e: e.tensor_tensor(out=t1[:], in0=li, in1=li, op=ALU.mult), ['lam'], ['q_t1'])
            DV(lambda e: e.tensor_tensor(out=den[:], in0=den[:], in1=t1[:], op=ALU.add), ['q_den', 'q_t1'], ['q_den'])
            DV(lambda e: e.reciprocal(out=den[:], in_=den[:]), ['q_den'], ['q_den'])
            DV(lambda e: e.tensor_tensor(out=t1[:], in0=arm1[:], in1=lr, op=ALU.mult), ['q_arm1', 'lam'], ['q_t1'])
            DV(lambda e: e.tensor_tensor(out=t2[:], in0=ai[:], in1=li, op=ALU.mult), ['q_ai', 'lam'], ['q_t2'])
            DV(lambda e: e.tensor_tensor(out=t1[:], in0=t1[:], in1=t2[:], op=ALU.add), ['q_t1', 'q_t2'], ['q_t1'])
            DV(lambda e: e.tensor_tensor(out=f_re[:], in0=t1[:], in1=den[:], op=ALU.mult), ['q_t1', 'q_den'], ['q_f_re'])
            DV(lambda e: e.tensor_tensor(out=t1[:], in0=ai[:], in1=lr, op=ALU.mult), ['q_ai', 'lam'], ['q_t1'])
            DV(lambda e: e.tensor_tensor(out=t2[:], in0=arm1[:], in1=li, op=ALU.mult), ['q_arm1', 'lam'], ['q_t2'])
            DV(lambda e: e.tensor_tensor(out=t1[:], in0=t1[:], in1=t2[:], op=ALU.subtract), ['q_t1', 'q_t2'], ['q_t1'])
            DV(lambda e: e.tensor_tensor(out=f_im[:], in0=t1[:], in1=den[:], op=ALU.mult), ['q_t1', 'q_den'], ['q_f_im'])
            DV(lambda e: e.tensor_tensor(out=im2[:], in0=mag[:], in1=mag[:], op=ALU.mult), ['q_mag'], ['q_im2'])
            DV(lambda e: e.reciprocal(out=im2[:], in_=im2[:]), ['q_im2'], ['q_im2'])
            DV(lambda e: e.tensor_tensor(out=avr[:], in0=ar[:], in1=im2[:], op=ALU.mult), ['q_ar', 'q_im2'], ['q_avr'])
            DV(lambda e: e.tensor_tensor(out=avi[:], in0=nai[:], in1=im2[:], op=ALU.mult), ['q_nai', 'q_im2'], ['q_avi'])
            PW = s.sb('q_PW', [128, 2, 8, 9], es=p2); AV = s.sb('q_AV', [128, 2, 8, 9], es=p2)

            def powers(P, kP, br, bi, nmax):
                s.op('pool', lambda e: e.memset(P[:, 0, :, 0], 1.0), reads=[kP], writes=[kP])
                s.op('pool', lambda e: e.memset(P[:, 1, :, 0], 0.0), reads=[kP], writes=[kP])
                for n in range(1, nmax + 1):
                    DV(lambda e: e.tensor_tensor(out=t1[:], in0=P[:, 0, :, n - 1], in1=br[:], op=ALU.mult), [kP], ['q_t1'])
                    DV(lambda e: e.tensor_tensor(out=t2[:], in0=P[:, 1, :, n - 1], in1=bi[:], op=ALU.mult), [kP], ['q_t2'])
                    DV(lambda e: e.tensor_tensor(out=P[:, 0, :, n], in0=t1[:], in1=t2[:], op=ALU.subtract), ['q_t1', 'q_t2', kP], [kP])
                    DV(lambda e: e.tensor_tensor(out=t1[:], in0=P[:, 0, :, n - 1], in1=bi[:], op=ALU.mult), [kP], ['q_t1'])
                    DV(lambda e: e.tensor_tensor(out=t2[:], in0=P[:, 1, :, n - 1], in1=br[:], op=ALU.mult), [kP], ['q_t2'])
                    DV(lambda e: e.tensor_tensor(out=P[:, 1, :, n], in0=t1[:], in1=t2[:], op=ALU.add), ['q_t1', 'q_t2', kP], [kP])
            s.op('dve', lambda e: e.tensor_copy(out=t1[:], in_=ar[:]), reads=['q_ar', 'q_ai', 'q_avr', 'q_avi'], writes=['q_t1'])
            powers(PW, 'q_PW', ar, ai, 8)
            powers(AV, 'q_AV', avr, avi, 8)
            for c in range(2):
                DV(lambda e: e.tensor_tensor(out=zz[:, c, :], in0=PW[:, c, :, 8], in1=ir8[:], op=ALU.mult), ['q_PW', 'q_ir8'], ['zz'])
            Bb = s.sb('q_Bb', [128, 2, 8, 16], es=p2)
            tb1 = s.sb('q_tb1', [128, 4, 16], es=p2); tb2 = s.sb('q_tb2', [128, 4, 16], es=p2)
            for d in range(2):
                fr = f_re[:, 4 * d:4 * d + 4].unsqueeze(2).to_broadcast([128, 4, 16]); fi = f_im[:, 4 * d:4 * d + 4].unsqueeze(2).to_broadcast([128, 4, 16])
                DV(lambda e: e.tensor_tensor(out=tb1[:], in0=bb[:, 0, :, :], in1=fr, op=ALU.mult), ['bb', 'q_f_re'], ['q_tb1'])
                DV(lambda e: e.tensor_tensor(out=tb2[:], in0=bb[:, 1, :, :], in1=fi, op=ALU.mult), ['bb', 'q_f_im'], ['q_tb2'])
                DV(lambda e: e.tensor_tensor(out=Bb[:, 0, 4 * d:4 * d + 4, :], in0=tb1[:], in1=tb2[:], op=ALU.subtract), ['q_tb1', 'q_tb2'], ['q_Bb'])
                DV(lambda e: e.tensor_tensor(out=tb1[:], in0=bb[:, 1, :, :], in1=fr, op=ALU.mult), ['bb', 'q_f_re'], ['q_tb1'])
                DV(lambda e: e.tensor_tensor(out=tb2[:], in0=bb[:, 0, :, :], in1=fi, op=ALU.mult), ['bb', 'q_f_im'], ['q_tb2'])
                DV(lambda e: e.tensor_tensor(out=Bb[:, 1, 4 * d:4 * d + 4, :], in0=tb1[:], in1=tb2[:], op=ALU.add), ['q_tb1', 'q_tb2'], ['q_Bb'])

            tw1 = s.sb('q_tw1', [128, 4, 8, 16], es=p2); tw2 = s.sb('q_tw2', [128, 4, 8, 16], es=p2)

            def cprod(out_re, out_im, kout, P, kP, sl, M, kM, d, neg_im=False):
                dsl = slice(4 * d, 4 * d + 4)
                pr = P[:, 0, dsl, sl].unsqueeze(3).to_broadcast([128, 4, 8, 16]); pi_ = P[:, 1, dsl, sl].unsqueeze(3).to_broadcast([128, 4, 8, 16])
                mr = M[:, 0, dsl, :].unsqueeze(2).to_broadcast([128, 4, 8, 16]); mi = M[:, 1, dsl, :].unsqueeze(2).to_broadcast([128, 4, 8, 16])
                o_re = out_re[:, dsl, :].rearrange("p k (n j) -> p k n j", j=16); o_im = out_im[:, dsl, :].rearrange("p k (n j) -> p k n j", j=16)
                DV(lambda e: e.tensor_tensor(out=tw1[:], in0=pr, in1=mr, op=ALU.mult), [kP, kM], ['q_tw1'])
                DV(lambda e: e.tensor_tensor(out=tw2[:], in0=pi_, in1=mi, op=ALU.mult), [kP, kM], ['q_tw2'])
                DV(lambda e: e.tensor_tensor(out=o_re, in0=tw1[:], in1=tw2[:], op=ALU.subtract), ['q_tw1', 'q_tw2'], [kout])
                DV(lambda e: e.tensor_tensor(out=tw1[:], in0=pr, in1=mi, op=ALU.mult), [kP, kM], ['q_tw1'])
                DV(lambda e: e.tensor_tensor(out=tw2[:], in0=pi_, in1=mr, op=ALU.mult), [kP, kM], ['q_tw2'])
                if neg_im:
                    DV(lambda e: e.scalar_tensor_tensor(out=o_im, in0=tw1[:], scalar=-1.0, in1=tw2[:], op0=ALU.mult, op1=ALU.subtract), ['q_tw1', 'q_tw2'], [kout])
                else:
                    DV(lambda e: e.tensor_tensor(out=o_im, in0=tw1[:], in1=tw2[:], op=ALU.add), ['q_tw1', 'q_tw2'], [kout])
            wsh = [128, 8, 128]
            WT_re = s.sb('q_WT_re', wsh, es=p2); WT_im = s.sb('q_WT_im', wsh, es=p2)
            Km_re = s.sb('q_Km_re', wsh, es=p2); Km_imn = s.sb('q_Km_imn', wsh, es=p2)
            Q_re = s.sb('q_Q_re', wsh, es=p2); Q_im = s.sb('q_Q_im', wsh, es=p2)
            V_re = s.sb('q_V_re', wsh, es=p2); V_imn = s.sb('q_V_imn', wsh, es=p2)
            fw8 = slice(0, 8); rv7 = slice(7, None, -1); f19 = slice(1, 9); rv8 = slice(8, 0, -1)
            cprod(WT_re, WT_im, 'q_WT', PW, 'q_PW', rv7, Bb, 'q_Bb', 0); cprod(WT_re, WT_im, 'q_WT', PW, 'q_PW', fw8, Bb, 'q_Bb', 1)
            cprod(Km_re, Km_imn, 'q_Km', AV, 'q_AV', fw8, Bb, 'q_Bb', 0, True); cprod(Km_re, Km_imn, 'q_Km', PW, 'q_PW', fw8, Bb, 'q_Bb', 1, True)
            cprod(Q_re, Q_im, 'q_Q', PW, 'q_PW', fw8, cc, 'cc', 0); cprod(Q_re, Q_im, 'q_Q', AV, 'q_AV', fw8, cc, 'cc', 1)
            cprod(V_re, V_imn, 'q_V', PW, 'q_PW', f19, cc, 'cc', 0, True); cprod(V_re, V_imn, 'q_V', PW, 'q_PW', rv8, cc, 'cc', 1, True)
            Qz = [[s.sb('q_Qz%d%d' % (e_, c_), wsh, es=p2) for c_ in range(2)] for e_ in range(2)]
            for e_ in range(2):
                for c_, Qs in enumerate((Q_re, Q_im)):
                    s.op('pool', lambda e: e.tensor_copy(out=Qz[e_][c_][:], in_=Qs[:]), reads=['q_Q'], writes=['q_Qz'])
                    s.op('pool', lambda e: e.memset(Qz[e_][c_][64 * (1 - e_):64 * (1 - e_) + 64, :, :], 0.0), reads=['q_Qz'], writes=['q_Qz'])
            pW = [s.ps('pW%d' % i, [128, 512], es=p2) for i in range(2)]
            npw = 0
            for d in range(2):
                for k in range(4):
                    dk = 4 * d + k
                    for e_ in range(2):
                        g = 2 * k + e_
                        pw = pW[npw % 2]; kpw = 'pW%d' % (npw % 2); npw += 1
                        s.op('pe', lambda e: e.matmul(pw[:, 0:128], lhsT=Km_re[:, dk, :], rhs=Qz[e_][0][:, dk, :], start=True, stop=False),
                             reads=['q_Km', 'q_Qz'], writes=[kpw])
                        s.op('pe', lambda e: e.matmul(pw[:, 0:128], lhsT=Km_imn[:, dk, :], rhs=Qz[e_][1][:, dk, :], start=False, stop=True),
                             reads=['q_Km', 'q_Qz'], writes=[kpw])
                        s.op('dve', lambda e: e.tensor_tensor(out=Tbf[:, 2 * g + d, :], in0=pw[:, 0:128], in1=msk[:, d, :], op=ALU.mult),
                             reads=[kpw, 'msk'], writes=['Tbf'])
                        hs = slice(64 * e_, 64 * e_ + 64)
                        s.op('act', lambda e: e.activation(out=Vz_re[hs, d, g, :], in_=V_re[hs, dk, :], func=AF.Copy), reads=['q_V'], writes=['Vz_re'])
                        s.op('act', lambda e: e.activation(out=Vz_imn[hs, d, g, :], in_=V_imn[hs, dk, :], func=AF.Copy), reads=['q_V'], writes=['Vz_imn'])
                    for c_, (WTs, Wz) in enumerate(((WT_re, Wz_re), (WT_im, Wz_im))):
                        pw = pW[npw % 2]; kpw = 'pW%d' % (npw % 2); npw += 1
                        s.op('pe', lambda e: e.transpose(out=pw[:, 0:128], in_=WTs[:, dk, :], identity=identf[:]), reads=['q_WT', 'identf2'], writes=[kpw])
                        for e_ in range(2):
                            hs = slice(64 * e_, 64 * e_ + 64)
                            s.op('act', lambda e: e.activation(out=Wz[:, d, 2 * k + e_, hs], in_=pw[:, hs], func=AF.Copy), reads=[kpw], writes=['Wz'])
            s.barrier()
        if stage == 2:
            _finish_dbg(s, dbg_d, [(Tbf[:, 5, :], 'Tbf', 128), (Tbf[:, 6, :], 'Tbf', 128), (Wz_re[:, 1, 3, :], 'Wz', 128), (Wz_im[:, 0, 2, :], 'Wz', 128),
                                   (Vz_re[:, 0, 3, :], 'Vz_re', 128), (Vz_imn[:, 1, 4, :], 'Vz_imn', 128), (r8[:], 'r8', 8), (zz[:].rearrange("p c k -> p (c k)"), 'zz', 16)])
            s.wait_all('sp', ['gutok'])
            return nc
        Xin_re = s.sb('Xin_re', [128, 2, 4, NC8], BF16); Xin_im = s.sb('Xin_im', [128, 2, 4, NC8], BF16)
        with ExitStack() as p3:
            big = [128, 4, NC8]
            S_re = s.sb('S_re', big, es=p3); S_im = s.sb('S_im', big, es=p3)
            G_re = s.sb('G_re', big, es=p3); G_im = s.sb('G_im', big, es=p3)
            RT_re = s.sb('RT_re', big, es=p3); RT_im = s.sb('RT_im', big, es=p3)
            ta = s.sb('r_ta', [128, 4, 512], es=p3); tb_ = s.sb('r_tb', [128, 4, 512], es=p3)
            w_re = s.sb('r_wre', [128, 4], es=p3); w_im = s.sb('r_wim', [128, 4], es=p3)
            wt1 = s.sb('r_wt1', [128, 4], es=p3); wt2 = s.sb('r_wt2', [128, 4], es=p3)
            pSr = [s.ps('pSr%d' % i, [128, 512], es=p3) for i in range(2)]
            pSi = [s.ps('pSi%d' % i, [128, 512], es=p3) for i in range(2)]
            DV = lambda fn, r, w: s.op('dve', fn, reads=r, writes=w)
            nps = 0
            for d in range(2):
                s.op('pool', lambda e: e.memset(RT_re[:, :, 0:1], 1.0), reads=['RT'], writes=['RT'])
                s.op('pool', lambda e: e.memset(RT_im[:, :, 0:1], 0.0), reads=['RT'], writes=['RT'])
                DV(lambda e: e.tensor_copy(out=w_re[:], in_=zz[:, 0, 4 * d:4 * d + 4]), ['zz'], ['r_w'])
                DV(lambda e: e.tensor_copy(out=w_im[:], in_=zz[:, 1, 4 * d:4 * d + 4]), ['zz'], ['r_w'])
                n = 1
                while n < NC8:
                    cnt = min(n, NC8 - n)
                    bw = lambda w: w[:].unsqueeze(2).to_broadcast([128, 4, cnt])
                    DV(lambda e: e.tensor_tensor(out=ta[:, :, 0:cnt], in0=RT_re[:, :, 0:cnt], in1=bw(w_re), op=ALU.mult), ['RT', 'r_w'], ['r_ta'])
                    DV(lambda e: e.tensor_tensor(out=tb_[:, :, 0:cnt], in0=RT_im[:, :, 0:cnt], in1=bw(w_im), op=ALU.mult), ['RT', 'r_w'], ['r_tb'])
                    DV(lambda e: e.tensor_tensor(out=RT_re[:, :, n:n + cnt], in0=ta[:, :, 0:cnt], in1=tb_[:, :, 0:cnt], op=ALU.subtract), ['r_ta', 'r_tb', 'RT'], ['RT'])
                    DV(lambda e: e.tensor_tensor(out=ta[:, :, 0:cnt], in0=RT_re[:, :, 0:cnt], in1=bw(w_im), op=ALU.mult), ['RT', 'r_w'], ['r_ta'])
                    DV(lambda e: e.tensor_tensor(out=tb_[:, :, 0:cnt], in0=RT_im[:, :, 0:cnt], in1=bw(w_re), op=ALU.mult), ['RT', 'r_w'], ['r_tb'])
                    DV(lambda e: e.tensor_tensor(out=RT_im[:, :, n:n + cnt], in0=ta[:, :, 0:cnt], in1=tb_[:, :, 0:cnt], op=ALU.add), ['r_ta', 'r_tb', 'RT'], ['RT'])
                    DV(lambda e: e.tensor_tensor(out=wt1[:], in0=w_re[:], in1=w_re[:], op=ALU.mult), ['r_w'], ['r_wt1'])
                    DV(lambda e: e.tensor_tensor(out=wt2[:], in0=w_im[:], in1=w_im[:], op=ALU.mult), ['r_w'], ['r_wt2'])
                    DV(lambda e: e.tensor_tensor(out=wt2[:], in0=wt1[:], in1=wt2[:], op=ALU.subtract), ['r_wt1', 'r_wt2'], ['r_wt2'])
                    DV(lambda e: e.tensor_tensor(out=wt1[:], in0=w_re[:], in1=w_im[:], op=ALU.mult), ['r_w', 'r_wt2'], ['r_wt1'])
                    DV(lambda e: e.tensor_scalar(out=w_im[:], in0=wt1[:], scalar1=2.0, scalar2=None, op0=ALU.mult), ['r_wt1'], ['r_w'])
                    DV(lambda e: e.tensor_copy(out=w_re[:], in_=wt2[:]), ['r_wt2'], ['r_w'])
                    n *= 2
                for k in range(4):
                    for (n0, n) in NT8:
                        i = nps % 2; nps += 1
                        for (pp, kpp, Wz) in ((pSr[i], 'pSr%d' % i, Wz_re), (pSi[i], 'pSi%d' % i, Wz_im)):
                            for e_ in range(2):
                                g = 2 * k + e_
                                s.op('pe', lambda e: e.matmul(pp[:, 0:n], lhsT=Wz[:, d, g, :], rhs=U[:, g, n0:n0 + n], start=(e_ == 0), stop=(e_ == 1)),
                                     reads=['Wz', 'U'], writes=[kpp])
                        for (pp, kpp, Sd, kS) in ((pSr[i], 'pSr%d' % i, S_re, 'S_re'), (pSi[i], 'pSi%d' % i, S_im, 'S_im')):
                            if d == 0:
                                s.op('act', lambda e: e.activation(out=Sd[:, k, n0:n0 + n], in_=pp[:, 0:n], func=AF.Copy), reads=[kpp], writes=[kS])
                            else:
                                for (ca, cb, M0) in ((0, NCX8, NCX8 - 1), (NCX8, NC8, NC8 + NCX8 - 1)):
                                    lo, hi = max(ca, n0), min(cb, n0 + n)
                                    if lo >= hi:
                                        continue
                                    stop = lo - 1 - n0
                                    src_ = pp[:, hi - 1 - n0::-1] if stop < 0 else pp[:, hi - 1 - n0:stop:-1]
                                    DV(lambda e: e.tensor_copy(out=Sd[:, k, M0 - (hi - 1):M0 - lo + 1], in_=src_), [kpp], [kS])
                fl = lambda t: t[:].rearrange("p k m -> p (k m)")
                DV(lambda e: e.tensor_tensor(out=fl(G_re), in0=fl(S_re), in1=fl(RT_re), op=ALU.mult), ['S_re', 'RT'], ['G_re'])
                s.op('dve', lambda e: e.tensor_tensor(out=fl(G_im), in0=fl(S_im), in1=fl(RT_im), op=ALU.mult), reads=['S_im', 'RT'], writes=['G_im'])
                DV(lambda e: e.tensor_tensor(out=fl(G_re), in0=fl(G_re), in1=fl(G_im), op=ALU.add), ['G_re', 'G_im'], ['G_re'])
                s.op('dve', lambda e: e.tensor_tensor(out=fl(G_im), in0=fl(S_im), in1=fl(RT_re), op=ALU.mult), reads=['S_im', 'RT', 'G_re'], writes=['G_im'])
                DV(lambda e: e.tensor_tensor(out=fl(S_im), in0=fl(S_re), in1=fl(RT_im), op=ALU.mult), ['S_re', 'RT', 'G_im'], ['S_im'])
                DV(lambda e: e.tensor_tensor(out=fl(G_im), in0=fl(G_im), in1=fl(S_im), op=ALU.subtract), ['G_im', 'S_im'], ['G_im'])
                for k in range(4):
                    dk = 4 * d + k
                    for (Gs, kG, Sd, kS) in ((G_re, 'G_re', S_re, 'S_re'), (G_im, 'G_im', S_im, 'S_im')):
                        DV(lambda e: e.tensor_tensor_scan(out=Sd[:, k, :], data0=r8[:, dk:dk + 1].to_broadcast([128, NC8]), data1=Gs[:, k, :],
                                                          initial=0.0, op0=ALU.mult, op1=ALU.add), [kG, 'r8', kS], [kS])
                DV(lambda e: e.tensor_tensor(out=fl(G_re), in0=fl(S_re), in1=fl(RT_re), op=ALU.mult), ['S_re', 'RT'], ['G_re'])
                s.op('dve', lambda e: e.tensor_tensor(out=fl(G_im), in0=fl(S_im), in1=fl(RT_im), op=ALU.mult), reads=['S_im', 'RT'], writes=['G_im'])
                DV(lambda e: e.tensor_tensor(out=fl(G_re), in0=fl(G_re), in1=fl(G_im), op=ALU.subtract), ['G_re', 'G_im'], ['G_re'])
                s.op('dve', lambda e: e.tensor_tensor(out=fl(G_im), in0=fl(S_im), in1=fl(RT_re), op=ALU.mult), reads=['S_im', 'RT', 'G_re'], writes=['G_im'])
                DV(lambda e: e.tensor_tensor(out=fl(S_im), in0=fl(S_re), in1=fl(RT_im), op=ALU.mult), ['S_re', 'RT', 'G_im'], ['S_im'])
                DV(lambda e: e.tensor_tensor(out=fl(G_im), in0=fl(G_im), in1=fl(S_im), op=ALU.add), ['G_im', 'S_im'], ['G_im'])
                for (Gs, kG, Xd, kX) in ((G_re, 'G_re', Xin_re, 'Xin_re'), (G_im, 'G_im', Xin_im, 'Xin_im')):
                    if d == 0:
                        s.op('pool', lambda e: e.memset(Xd[:, 0, :, 0:1], 0.0), reads=[kX], writes=[kX])
                        DV(lambda e: e.tensor_copy(out=Xd[:, 0, :, 1:NC8], in_=Gs[:, :, 0:NC8 - 1]), [kG, kX], [kX])
                    else:
                        s.op('pool', lambda e: e.memset(Xd[:, 1, :, NCX8 - 1:NCX8], 0.0), reads=[kX], writes=[kX])
                        DV(lambda e: e.tensor_copy(out=Xd[:, 1, :, 0:NCX8 - 1], in_=Gs[:, :, NCX8 - 2::-1]), [kG, kX], [kX])
                        DV(lambda e: e.tensor_copy(out=Xd[:, 1, :, NCX8:NC8], in_=Gs[:, :, NC8 - 2:NCX8 - 2:-1]), [kG, kX], [kX])
            s.barrier()
        if stage == 3:
            _finish_dbg(s, dbg_d, [(Xin_re[:, 0, 1, :], 'Xin_re', NC8), (Xin_im[:, 1, 2, :], 'Xin_im', NC8)])
            s.wait_all('sp', ['gutok'])
            return nc

        pYb = [s.ps('pYb%d' % i, [128, 512]) for i in range(2)]
        y5sb = [s.sb('y5sb%d' % i, [128, 512]) for i in range(2)]
        Yb = s.sb('Yb', [128, 8, NC8], BF16) if y5tok_d is not None else None
        ny = 0
        for g in range(8):
            k = g // 2
            for (n0, n) in NT8:
                i = ny % 2; ny += 1
                py = pYb[i]; kpy = 'pYb%d' % i
                nr = slice(n0, n0 + n)
                ops = [(Tbf[:, 2 * g, :], 'Tbf', U[:, g, nr], 'U'), (Tbf[:, 2 * g + 1, :], 'Tbf', U[:, g, nr], 'U')]
                for d in range(2):
                    ops.append((Vz_re[:, d, g, :], 'Vz_re', Xin_re[:, d, k, nr], 'Xin_re'))
                    ops.append((Vz_imn[:, d, g, :], 'Vz_imn', Xin_im[:, d, k, nr], 'Xin_im'))
                for j, (lh, kl, rh, kr) in enumerate(ops):
                    s.op('pe', lambda e: e.matmul(py[:, 0:n], lhsT=lh, rhs=rh, start=(j == 0), stop=(j == len(ops) - 1)), reads=[kl, kr], writes=[kpy])
                if Yb is None:
                    s.op('act', lambda e: e.activation(out=y5sb[i][:, 0:n], in_=py[:, 0:n], func=AF.Copy), reads=[kpy], writes=['y5sb%d' % i])
                    s.dma('sp', 'dm_y5%d' % i, y5_d[g, :, nr], y5sb[i][:, 0:n], reads=['y5sb%d' % i], writes=['y5'])
                else:
                    s.op('act', lambda e: e.activation(out=Yb[:, g, nr], in_=py[:, 0:n], func=AF.Copy), reads=[kpy], writes=['Yb'])
        if Yb is not None:
            fself = s.sb('fself', [128, 8, 8, 128]); fsel = s.sb('fsel', [128, 8, 8, 128], BF16)
            s.dma('sp', 'dm_c', fself[:], fsel_d[:, :, :, :], writes=['fself'])
            s.op('pool', lambda e: e.tensor_copy(out=fsel[:], in_=fself[:]), reads=['fself'], writes=['fsel'])
            y5T = [s.sb('y5T%d' % i, [128, 512]) for i in range(2)]
            ytk = [s.sb('ytk%d' % i, [128, 4, 128]) for i in range(2)]
            pUn = pYb
            pTy = s.ps('pTy', [128, 4, 128])
            for cbk in range((NC8 + 63) // 64):
                i = cbk % 2
                nb = min(64, NC8 - 64 * cbk)
                for l in range(8):
                    for g in range(8):
                        s.op('pe', lambda e: e.matmul(pUn[i][:, l * 64:l * 64 + nb], lhsT=fsel[:, g, l, :], rhs=Yb[:, g, 64 * cbk:64 * cbk + nb],
                                                      start=(g == 0), stop=(g == 7)), reads=['fsel', 'Yb'], writes=['pYb%d' % i])
                s.op('act', lambda e: e.activation(out=y5T[i][:, 0:8 * nb].rearrange("p (c l) -> p c l", l=8),
                                                   in_=pUn[i][:, :].rearrange("p (l c) -> p c l", l=8)[:, 0:nb, :], func=AF.Copy),
                     reads=['pYb%d' % i], writes=['y5T%d' % i])
                nj = (8 * nb) // 128
                for j in range(nj):
                    s.op('pe', lambda e: e.transpose(out=pTy[:, j, :], in_=y5T[i][:, j * 128:(j + 1) * 128], identity=identf[:]),
                         reads=['y5T%d' % i, 'identf2'], writes=['pTy'])
                s.op('act', lambda e: e.activation(out=ytk[i][:, 0:nj, :], in_=pTy[:, 0:nj, :], func=AF.Copy), reads=['pTy'], writes=['ytk%d' % i])
                s.dma('sp', 'dm_y5%d' % i, y5tok_d[512 * cbk:512 * cbk + 128 * nj, :].rearrange("(j p) c -> p j c", p=128), ytk[i][:, 0:nj, :],
                      reads=['ytk%d' % i], writes=['y5'])
        s.wait_all('sp', ['y5', 'gutok'])
        s.barrier()
    return nc


def build_l0a_s5(stage=99, cut=None, unfold=False):
    nc = bass.Bass("TRN2", target_bir_lowering=False)
    dr = lambda n, sh, kind="ExternalInput", dt=F32: nc.dram_tensor(n, list(sh), dt, kind=kind).ap()
    xin = dr("xin", [T, D])
    cvec_d = dr("cvec", [128, 8, 2]); adaw_d = dr("adaw", [128, 8, 2048]); adab_d = dr("adab", [128, 16])
    NW = 640
    win_d = dr("win", [128, 8, NW])
    lam_d = dr("lam", [128, 3, 8])
    bb_d = dr("bb", [128, 2, 4, 16])
    cc_d = dr("cc", [128, 2, 8, 16])
    esel_d = dr("esel", [128, 3, 8, 128])
    msk_d = dr("msk", [128, 2, 128])
    gu_d = dr("gutok", [T, 256], "ExternalOutput")
    y5_d = dr("y5", [8, 128, NC8], "ExternalOutput")
    dbg_d = dr("dbg", [128, 4096], "ExternalOutput")

    with ExitStack() as es0:
        s = Sched(nc, es0)
        kw = {}
        if unfold:
            kw = dict(y5tok_d=nc.dram_tensor("y5tok", [T, 128], F32, kind="ExternalOutput").ap(),
                      fsel_d=nc.dram_tensor("fsel", [128, 8, 8, 128], F32, kind="ExternalInput").ap())
        emit_l0a_s5(nc, s, '', xin, cvec_d, adaw_d, adab_d, win_d, lam_d, bb_d, cc_d, esel_d, msk_d, gu_d, y5_d, dbg_d, stage=stage, cut=cut, **kw)
    return nc


NROW_B = 128 + 2048
NTILE_B = NROW_B // 128


def mod_rep(s, nc, scv, aw_d, abrep_d, ncols, out, out_key, es_tmp, tag):
    screp = s.sb('screp' + tag, [128, 2, 8, 128], es=es_tmp)
    for w in range(2):
        for k in range(8):
            s.op('act', lambda e: e.activation(out=screp[:, w, k, :], in_=scv[:, k, w:w + 1].to_broadcast([128, 128]), func=AF.Copy),
                 reads=['scv'], writes=['screp' + tag])
    s.dma('sp', 'dm_rep' + tag, out[:, 0, :], abrep_d[:, :], writes=[out_key])
    s.dma('sp', 'dm_rep' + tag, out[:, 1, :], abrep_d[:, :], writes=[out_key])
    aw = s.sb('awr' + tag, [128, 8, ncols], es=es_tmp)
    for k in range(8):
        s.dma('sp' if k % 2 == 0 else 'act', 'dm_awr' + tag, aw[:, k, :], aw_d[:, k, :], writes=['awr' + tag])
    pr = [s.ps('prep%s%d' % (tag, i), [128, 512], es=es_tmp) for i in range(2)]
    n = 0
    for w in range(2):
        for c0 in range(0, ncols, 512):
            p_ = pr[n % 2]; kp = 'prep%s%d' % (tag, n % 2); n += 1
            for k in range(8):
                s.op('pe', lambda e: e.matmul(p_[:, :], lhsT=screp[:, w, k, :], rhs=aw[:, k, c0:c0 + 512], start=(k == 0), stop=(k == 7)),
                     reads=['screp' + tag, 'awr' + tag], writes=[kp])
            s.op('dve', lambda e: e.scalar_tensor_tensor(out=out[:, w, c0:c0 + 512], in0=p_[:, :], scalar=1.0, in1=out[:, w, c0:c0 + 512], op0=ALU.mult, op1=ALU.add),
                 reads=[kp, out_key], writes=[out_key])


def build_l0b(stage=99, cut=None):
    nc = bass.Bass("TRN2", target_bir_lowering=False)
    dr = lambda n, sh, kind="ExternalInput", dt=F32: nc.dram_tensor(n, list(sh), dt, kind=kind).ap()
    xres_d = dr("xres", [NROW_B, D]); ys_d = dr("ys", [NROW_B, 1024]); z_d = dr("z", [NROW_B, 1024])
    y5_d = dr("y5", [NROW_B, 512]); u_d = dr("u", [NROW_B, 512]); g5_d = dr("g5", [NROW_B, 512])
    cvec_d = dr("cvec", [128, 8, 2])
    aw0g_d = dr("aw0g", [128, 8, 1024]); ab0g_d = dr("ab0g", [128, 1024])
    adaw1_d = dr("adaw1", [128, 8, 2048]); adab1_d = dr("adab1", [128, 16])
    reps_d = dr("reps", [128, 1024 + 512 + 512 + 64 + 64])
    gluw_d = dr("gluw", [128, 4, 512]); wout_d = dr("wout", [128, 12, 1024]); w1_d = dr("w1", [128, 8, 2560])
    rope_d = dr("rope", [128, NTILE_B, 2, 32])
    h1_d = dr("h1", [NROW_B, D], "ExternalOutput")
    q_d = dr("q", [NROW_B, 1024], "ExternalOutput", BF16); k_d = dr("k", [NROW_B, 256], "ExternalOutput", BF16)
    v_d = dr("v", [NROW_B, 256], "ExternalOutput", BF16); sg_d = dr("sg", [NROW_B, 1024], "ExternalOutput", BF16)
    dbg_d = dr("dbg", [128, 4096], "ExternalOutput")

    with ExitStack() as es:
        s = Sched(nc, es)
        identb, idk = make_identity(s, 'identb', BF16)
        reps = s.sb('reps', [128, 2176]); rope = s.sb('rope', [128, NTILE_B, 2, 32])
        s.dma('sp', 'dm_c', reps[:], reps_d[:, :], writes=['reps'])
        s.dma('sp', 'dm_c', rope[:], rope_d[:, :, :, :], writes=['rope'])
        ssdn = reps[:, 0:1024]; ds5 = reps[:, 1024:1536]; glub = reps[:, 1536:2048]; qg = reps[:, 2048:2112]; kg = reps[:, 2112:2176]
        modT = phase0_mod(s, nc, cvec_d, adaw1_d, adab1_d, 16)
        sc1 = s.sb('sc1', [128, 8, 2])
        s.op('dve', lambda e: e.tensor_scalar(out=sc1[:], in0=modT[:, 8:16, :], scalar1=1.0, scalar2=None, op0=ALU.add), reads=['modT'], writes=['modv'])
        sh = modT[:, 0:8, :]
        gate0 = s.sb('gate0', [128, 2, 1024])
        with ExitStack() as t0:
            cv = s.sb('cv2', [128, 8, 2], es=t0); scv = s.sb('scv2', [128, 8, 2], es=t0)
            s.dma('sp', 'dm_c', cv[:], cvec_d[:, :, :], writes=['cv2'])
            s.op('act', lambda e: e.activation(out=scv[:], in_=cv[:], func=AF.Silu), reads=['cv2'], writes=['scv'])
            mod_rep(s, nc, scv, aw0g_d, ab0g_d, 1024, gate0, 'gate0', t0, 'g0')
            s.barrier()
        gluw = s.sb('gluw', [128, 4, 512], BF16); wout = s.sb('wout', [128, 12, 1024], BF16); w1 = s.sb('w1', [128, 8, 2560], BF16)
        with ExitStack() as t1:
            stg = [s.sb('wst%d' % i, [128, 2560], es=t1) for i in range(2)]
            n = 0
            for (wd, wsb, kw, nk, ncol) in ((gluw_d, gluw, 'gluw', 4, 512), (wout_d, wout, 'wout', 12, 1024), (w1_d, w1, 'w1', 8, 2560)):
                for k in range(nk):
                    i = n % 2; n += 1
                    s.dma('sp' if i == 0 else 'act', 'dm_wst%d' % i, stg[i][:, 0:ncol], wd[:, k, :], writes=['wst%d' % i])
                    s.op('pool', lambda e: e.tensor_copy(out=wsb[:, k, :], in_=stg[i][:, 0:ncol]), reads=['wst%d' % i], writes=[kw])
            s.barrier()

        npj = NormProj(s, nc, None, identb, idk, sc1, sh, es)
        ld = lambda n, w: [s.sb('%s%d' % (n, i), [128, w]) for i in range(2)]
        ysb = ld('ysb', 1024); zsb = ld('zsb', 1024); xrb = ld('xrb', 1024); y5b = ld('y5b', 512); ub = ld('ub', 512); g5b = ld('g5b', 512)
        tt = s.sb('tt', [128, 1024]); Fb = s.sb('Fb', [128, 1536], BF16); vv = s.sb('vv', [128, 512]); vb = s.sb('vb', [128, 512], BF16)
        sg5 = s.sb('sg5', [128, 512]); sgg = s.sb('sgg', [128, 512]); stt = s.sb('stt', [128, 8]); junkb = s.sb('junkb', [128, 1024], BF16)
        vT = s.sb('vT', [128, 4, 128], BF16); FT = s.sb('FT', [128, 12, 128], BF16); h1 = [s.sb('h1_%d' % i, [128, 1024]) for i in range(2)]
        a1T = s.sb('a1T', [128, 8, 128], BF16)
        qf = s.sb('qf', [128, 512]); qs = s.sb('qs', [128, 512]); qst = s.sb('qst', [128, 4, 8]); qr = s.sb('qr', [128, 512])
        qo = [s.sb('qo%d' % i, [128, 1024], BF16) for i in range(2)]; ko = [s.sb('ko%d' % i, [128, 256], BF16) for i in range(2)]
        vo = [s.sb('vo%d' % i, [128, 256], BF16) for i in range(2)]; sgo = [s.sb('sgo%d' % i, [128, 1024], BF16) for i in range(2)]
        pTv = s.ps('pTv', [128, 8, 128], BF16); pF = [s.ps('pF%d' % i, [128, 8, 128], BF16) for i in range(2)]
        pA = [s.ps('pA%d' % i, [128, 512]) for i in range(3)]
        npa = [0]

        def nextpa():
            i = npa[0] % 3; npa[0] += 1
            return pA[i], 'pA%d' % i

        def headnorm(src_ps, kps, nh, gain, extra, dst, kdst, ti):
            w = nh * 64
            s.op('act', lambda e: e.activation(out=qs[:, 0:w], in_=src_ps, func=AF.Square), reads=[kps], writes=['qs'])
            s.op('dve', lambda e: e.tensor_reduce(out=qst[:, 0, 0:nh], in_=qs[:, 0:w].rearrange("p (h q) -> p h q", q=64), axis=AX.X, op=ALU.add),
                 reads=['qs'], writes=['qst'])
            s.op('dve', lambda e: e.tensor_scalar(out=qst[:, 1, 0:nh], in0=qst[:, 0, 0:nh], scalar1=1.0 / 64, scalar2=EPS, op0=ALU.mult, op1=ALU.add),
                 reads=['qst'], writes=['qst'])
            s.op('act', lambda e: e.activation(out=qst[:, 2, 0:nh], in_=qst[:, 1, 0:nh], func=AF.Sqrt), reads=['qst'], writes=['qst'])
            s.op('dve', lambda e: e.reciprocal(out=qst[:, 3, 0:nh], in_=qst[:, 2, 0:nh]), reads=['qst'], writes=['qst'])
            s.op('dve', lambda e: e.tensor_tensor(out=qf[:, 0:w].rearrange("p (h q) -> p h q", q=64), in0=src_ps.rearrange("p (h q) -> p h q", q=64),
                                                  in1=qst[:, 3, 0:nh].unsqueeze(2).to_broadcast([128, nh, 64]), op=ALU.mult), reads=[kps, 'qst'], writes=['qf'])
            s.op('dve', lambda e: e.scalar_tensor_tensor(out=qf[:, 0:w].rearrange("p (h q) -> p h q", q=64), in0=qf[:, 0:w].rearrange("p (h q) -> p h q", q=64),
                                                         scalar=extra, in1=gain.unsqueeze(1).to_broadcast([128, nh, 64]), op0=ALU.mult, op1=ALU.mult),
                 reads=['qf', 'reps'], writes=['qf'])
            x5 = qf[:, 0:w].rearrange("p (h a t r) -> p h a t r", a=2, t=2, r=16)
            o5 = dst.rearrange("p (h a t r) -> p h a t r", a=2, t=2, r=16)
            t5 = qr[:, 0:w].rearrange("p (h a t r) -> p h a t r", a=2, t=2, r=16)
            cs = rope[:, ti, 0, :].rearrange("p (a r) -> p a r", a=2).unsqueeze(1).to_broadcast([128, nh, 2, 16])
            sn = rope[:, ti, 1, :].rearrange("p (a r) -> p a r", a=2).unsqueeze(1).to_broadcast([128, nh, 2, 16])
            x1, x2 = x5[:, :, :, 0, :], x5[:, :, :, 1, :]
            s.op('dve', lambda e: e.tensor_tensor(out=t5[:, :, :, 0, :], in0=x1, in1=cs, op=ALU.mult), reads=['qf', 'rope'], writes=['qr'])
            s.op(PENGB, lambda e: e.tensor_tensor(out=t5[:, :, :, 1, :], in0=x2, in1=sn, op=ALU.mult), reads=['qf', 'rope'], writes=['qr1'])
            s.op('dve', lambda e: e.tensor_tensor(out=o5[:, :, :, 0, :], in0=t5[:, :, :, 0, :], in1=t5[:, :, :, 1, :], op=ALU.subtract), reads=['qr', 'qr1'], writes=[kdst])
            s.op('dve', lambda e: e.tensor_tensor(out=t5[:, :, :, 0, :], in0=x2, in1=cs, op=ALU.mult), reads=['qf', 'rope'], writes=['qr'])
            s.op(PENGB, lambda e: e.tensor_tensor(out=t5[:, :, :, 1, :], in0=x1, in1=sn, op=ALU.mult), reads=['qf', 'rope'], writes=['qr1'])
            s.op('dve', lambda e: e.tensor_tensor(out=o5[:, :, :, 1, :], in0=t5[:, :, :, 0, :], in1=t5[:, :, :, 1, :], op=ALU.add), reads=['qr', 'qr1'], writes=[kdst])

        for ti in range(NTILE_B):
            i = ti % 2
            which = 1 if ti == 0 else 0
            rows = slice(ti * 128, (ti + 1) * 128)
            for (dd_, tl, nm) in ((ys_d, ysb, 'ysb'), (z_d, zsb, 'zsb'), (xres_d, xrb, 'xrb'), (y5_d, y5b, 'y5b'), (u_d, ub, 'ub'), (g5_d, g5b, 'g5b')):
                s.dma('sp', 'dm_%s%d' % (nm, i), tl[i][:], dd_[rows, :], writes=['%s%d' % (nm, i)])
            s.op('act', lambda e: e.activation(out=tt[:], in_=zsb[i][:], func=AF.Silu), reads=['zsb%d' % i], writes=['tt'])
            s.op('dve', lambda e: e.tensor_tensor(out=tt[:], in0=tt[:], in1=ysb[i][:], op=ALU.mult), reads=['tt', 'ysb%d' % i], writes=['tt'])
            s.op('act', lambda e: e.activation(out=junkb[:], in_=tt[:], func=AF.Square, accum_out=stt[:, 0:1]), reads=['tt'], writes=['junkb', 'stt'])
            s.op('dve', lambda e: e.tensor_scalar(out=stt[:, 1:2], in0=stt[:, 0:1], scalar1=1.0 / 1024, scalar2=EPS, op0=ALU.mult, op1=ALU.add), reads=['stt'], writes=['stt'])
            s.op('act', lambda e: e.activation(out=stt[:, 2:3], in_=stt[:, 1:2], func=AF.Sqrt), reads=['stt'], writes=['stt'])
            s.op('dve', lambda e: e.reciprocal(out=stt[:, 3:4], in_=stt[:, 2:3]), reads=['stt'], writes=['stt'])
            s.op('dve', lambda e: e.scalar_tensor_tensor(out=Fb[:, 0:1024], in0=tt[:], scalar=stt[:, 3:4], in1=ssdn, op0=ALU.mult, op1=ALU.mult),
                 reads=['tt', 'stt', 'reps'], writes=['Fb'])
            s.op(PENGB, lambda e: e.tensor_tensor(out=vv[:], in0=ub[i][:], in1=ds5, op=ALU.mult), reads=['ub%d' % i, 'reps'], writes=['vv'])
            s.op(PENGB, lambda e: e.tensor_tensor(out=vv[:], in0=vv[:], in1=y5b[i][:], op=ALU.add), reads=['vv', 'y5b%d' % i], writes=['vv'])
            s.op('act', lambda e: e.activation(out=vv[:], in_=vv[:], func=AF.Gelu_apprx_tanh), reads=['vv'], writes=['vv'])
            s.op(PENGB, lambda e: e.tensor_copy(out=vb[:], in_=vv[:]), reads=['vv'], writes=['vb'])
            for k in range(4):
                s.op('pe', lambda e: e.transpose(out=pTv[:, k, :], in_=vb[:, k * 128:(k + 1) * 128], identity=identb[:]), reads=['vb', idk], writes=['pTv'])
            s.op('act', lambda e: e.activation(out=vT[:], in_=pTv[:, 0:4, :], func=AF.Copy), reads=['pTv'], writes=['vT'])
            pg, kpg = nextpa()
            for k in range(4):
                s.op('pe', lambda e: e.matmul(pg[:, :], lhsT=vT[:, k, :], rhs=gluw[:, k, :], start=(k == 0), stop=(k == 3)), reads=['vT', 'gluw'], writes=[kpg])
            s.op('dve', lambda e: e.scalar_tensor_tensor(out=sg5[:], in0=pg[:, :], scalar=1.0, in1=glub, op0=ALU.mult, op1=ALU.add), reads=[kpg, 'reps'], writes=['sg5'])
            s.op('act', lambda e: e.activation(out=sg5[:], in_=sg5[:], func=AF.Sigmoid), reads=['sg5'], writes=['sg5'])
            s.op('act', lambda e: e.activation(out=sgg[:], in_=g5b[i][:], func=AF.Silu), reads=['g5b%d' % i], writes=['sgg'])
            s.op(PENGB, lambda e: e.tensor_tensor(out=sg5[:], in0=sg5[:], in1=sgg[:], op=ALU.mult), reads=['sg5', 'sgg'], writes=['sg5'])
            s.op(PENGB, lambda e: e.tensor_tensor(out=Fb[:, 1024:1536], in0=sg5[:], in1=vv[:], op=ALU.mult), reads=['sg5', 'vv'], writes=['Fb'])
            for k in range(12):
                pf = pF[k // 8]
                s.op('pe', lambda e: e.transpose(out=pf[:, k % 8, :], in_=Fb[:, k * 128:(k + 1) * 128], identity=identb[:]), reads=['Fb', idk], writes=['pF%d' % (k // 8)])
            s.op('act', lambda e: e.activation(out=FT[:, 0:8, :], in_=pF[0][:, :, :], func=AF.Copy), reads=['pF0'], writes=['FT'])
            s.op('act', lambda e: e.activation(out=FT[:, 8:12, :], in_=pF[1][:, 0:4, :], func=AF.Copy), reads=['pF1'], writes=['FT'])
            for half in range(2):
                po, kpo = nextpa()
                cs_ = slice(half * 512, (half + 1) * 512)
                for k in range(12):
                    s.op('pe', lambda e: e.matmul(po[:, :], lhsT=FT[:, k, :], rhs=wout[:, k, cs_], start=(k == 0), stop=(k == 11)), reads=['FT', 'wout'], writes=[kpo])
                s.op('dve', lambda e: e.tensor_tensor(out=h1[i][:, cs_], in0=po[:, :], in1=gate0[:, which, cs_], op=ALU.mult), reads=[kpo, 'gate0'], writes=['h1_%d' % i])
                s.op(PENGB, lambda e: e.tensor_tensor(out=h1[i][:, cs_], in0=h1[i][:, cs_], in1=xrb[i][:, cs_], op=ALU.add), reads=['h1_%d' % i, 'xrb%d' % i], writes=['h1_%d' % i])
            s.dma('act', 'dm_h1%d' % i, h1_d[rows, :], h1[i][:], reads=['h1_%d' % i], writes=['h1o'])
            npj.sub(0, which, a1T, 'a1T', 0, src=(h1[i], 'h1_%d' % i))
            for nt_ in range(5):
                pq, kpq = nextpa()
                for k in range(8):
                    s.op('pe', lambda e: e.matmul(pq[:, :], lhsT=a1T[:, k, :], rhs=w1[:, k, nt_ * 512:(nt_ + 1) * 512], start=(k == 0), stop=(k == 7)),
                         reads=['a1T', 'w1'], writes=[kpq])
                if nt_ < 2:
                    headnorm(pq[:, :], kpq, 8, qg, 0.125, qo[i][:, nt_ * 512:(nt_ + 1) * 512], 'qo%d' % i, ti)
                elif nt_ == 2:
                    headnorm(pq[:, 0:256], kpq, 4, kg, 1.0, ko[i][:, :], 'ko%d' % i, ti)
                    s.op('act', lambda e: e.activation(out=vo[i][:], in_=pq[:, 256:512], func=AF.Copy), reads=[kpq], writes=['vo%d' % i])
                else:
                    s.op('act', lambda e: e.activation(out=sgo[i][:, (nt_ - 3) * 512:(nt_ - 2) * 512], in_=pq[:, :], func=AF.Silu), reads=[kpq], writes=['sgo%d' % i])
            s.dma('act', 'dm_q%d' % i, q_d[rows, :], qo[i][:], reads=['qo%d' % i], writes=['qout'])
            s.dma('act', 'dm_q%d' % i, k_d[rows, :], ko[i][:], reads=['ko%d' % i], writes=['kout'])
            s.dma('act', 'dm_q%d' % i, v_d[rows, :], vo[i][:], reads=['vo%d' % i], writes=['vout'])
            s.dma('act', 'dm_q%d' % i, sg_d[rows, :], sgo[i][:], reads=['sgo%d' % i], writes=['sgout'])
        s.wait_all('sp', ['h1o', 'qout', 'kout', 'vout', 'sgout'])
        s.barrier()
    return nc


def build_l1b(stage=99, nq_tiles=4):
    nc = bass.Bass("TRN2", target_bir_lowering=False)
    dr = lambda n, sh, kind="ExternalInput", dt=F32: nc.dram_tensor(n, list(sh), dt, kind=kind).ap()
    qz_d = dr("qz", [4, 128, 16, 512], dt=BF16)
    kt_d = dr("kt", [128, 2, T], dt=BF16)
    v_d = dr("vv", [128, NCH, 4, 65], dt=BF16)
    sg_d = dr("sg", [2048, 1024], dt=BF16)
    h1_d = dr("h1", [2048, D])
    wo_d = dr("wo", [128, 8, 1024])
    cvec_d = dr("cvec", [128, 8, 2]); aw1g_d = dr("aw1g", [128, 8, 1024]); ab1g_d = dr("ab1g", [128, 1024])
    fg_d = dr("fg", [128, 1024])
    out_d = dr("out", [2048, D], "ExternalOutput")

    with ExitStack() as es:
        s = Sched(nc, es)
        identb, idk = make_identity(s, 'identb', BF16)
        identf = s.sb('identb_f_alias', [1, 1])
        KT = s.sb('KT', [128, 2, T], BF16); V = s.sb('V', [128, NCH, 4, 65], BF16)
        s.dma('sp', 'dm_kt', KT[:, 0, :], kt_d[:, 0, :], writes=['KT'])
        s.dma('act', 'dm_kt', KT[:, 1, :], kt_d[:, 1, :], writes=['KT'])
        for c4 in range(0, NCH, 11):
            s.dma('sp', 'dm_v', V[:, c4:c4 + 11, :, :], v_d[:, c4:c4 + 11, :, :], writes=['V'])
        fg = s.sb('fg', [128, 1024])
        s.dma('sp', 'dm_c', fg[:], fg_d[:, :], writes=['fg'])
        gate1 = s.sb('gate1', [128, 2, 1024])
        with ExitStack() as t0:
            cv = s.sb('cv2', [128, 8, 2], es=t0); scv = s.sb('scv2', [128, 8, 2], es=t0)
            s.dma('sp', 'dm_c', cv[:], cvec_d[:, :, :], writes=['cv2'])
            s.op('act', lambda e: e.activation(out=scv[:], in_=cv[:], func=AF.Silu), reads=['cv2'], writes=['scv'])
            mod_rep(s, nc, scv, aw1g_d, ab1g_d, 1024, gate1, 'gate1', t0, 'g1')
            s.barrier()
        wo = s.sb('wo', [128, 8, 1024], BF16)
        with ExitStack() as t1:
            stg = [s.sb('wst%d' % i, [128, 1024], es=t1) for i in range(2)]
            for k in range(8):
                i = k % 2
                s.dma('sp' if i == 0 else 'act', 'dm_wst%d' % i, stg[i][:], wo_d[:, k, :], writes=['wst%d' % i])
                s.op('pool', lambda e: e.tensor_copy(out=wo[:, k, :], in_=stg[i][:]), reads=['wst%d' % i], writes=['wo'])
            s.barrier()
        idf = None
        qz = [s.sb('qz%d' % i, [128, 16, 512], BF16) for i in range(2)]
        og = s.sb('og', [128, 4, 1024])
        PT = [s.sb('PT%d' % i, [128, 512], BF16) for i in range(3)]
        OTs = s.sb('OTs', [65, 512]); rec = s.sb('rec', [128, 4])
        sgt = [s.sb('sgt%d' % i, [128, 1024], BF16) for i in range(2)]; h1t = [s.sb('h1t%d' % i, [128, 1024]) for i in range(2)]
        ogg = s.sb('ogg', [128, 1024], BF16); ogT = s.sb('ogT', [128, 8, 128], BF16)
        h2 = s.sb('h2', [128, 1024]); osb = [s.sb('osb%d' % i, [128, 1024]) for i in range(2)]
        junk = s.sb('junkc', [128, 1024], BF16); st = s.sb('stc', [128, 8])
        pS = [s.ps('pS%d' % i, [128, 512]) for i in range(3)]
        pO = [s.ps('pO%d' % i, [128, 512]) for i in range(2)]
        pOT = s.ps('pOT', [128, 4, 65])
        pTg = s.ps('pTg', [128, 8, 128], BF16)
        pA = s.ps('pAo', [128, 512])
        idf32 = s.sb('idf32', [128, 128])
        s.op('pool', lambda e: e.memset(idf32[:], 1.0), writes=['idf32'])
        s.op('pool', lambda e: e.affine_select(out=idf32[:], in_=idf32[:], pattern=[[-1, 128]], compare_op=ALU.is_equal,
                                               fill=0.0, base=0, channel_multiplier=1), reads=['idf32'], writes=['idf32'])
        s.dma('sp', 'dm_qz0', qz[0][:], qz_d[0, :, :, :], writes=['qz0'])
        nit = 0
        nho = 0
        for qt in range(nq_tiles):
            qi = qt % 2
            if qt + 1 < nq_tiles:
                s.dma('sp', 'dm_qz%d' % ((qt + 1) % 2), qz[(qt + 1) % 2][:], qz_d[qt + 1, :, :, :], writes=['qz%d' % ((qt + 1) % 2)])
            for h in range(16):
                kh = h // 4; pair, e_ = kh // 2, kh % 2
                po = pO[nho % 2]; kpo = 'pO%d' % (nho % 2); nho += 1

                def qk(kc, it):
                    ps_ = pS[it % 3]
                    s.op('pe', lambda e: e.matmul(ps_[:, :], lhsT=KT[:, pair, kc * 128:(kc + 1) * 128], rhs=qz[qi][:, h, :], start=True, stop=True),
                         reads=['KT', 'qz%d' % qi], writes=['pS%d' % (it % 3)])
                qk(0, nit)
                for kc in range(NCH):
                    it = nit + kc
                    if kc + 1 < NCH:
                        qk(kc + 1, it + 1)
                    s.op('act', lambda e: e.activation(out=PT[it % 3][:], in_=pS[it % 3][:, :], func=AF.Exp), reads=['pS%d' % (it % 3)], writes=['PT%d' % (it % 3)])
                    s.op('pe', lambda e: e.matmul(po[0:65, :], lhsT=V[:, kc, kh, :], rhs=PT[it % 3][:], start=(kc == 0), stop=(kc == NCH - 1)),
                         reads=['V', 'PT%d' % (it % 3)], writes=[kpo])
                nit += NCH
                s.op('act', lambda e: e.activation(out=OTs[:, :], in_=po[0:65, :], func=AF.Copy), reads=[kpo], writes=['OTs'])
                for j in range(4):
                    s.op('pe', lambda e: e.transpose(out=pOT[:, j, :], in_=OTs[:, j * 128:(j + 1) * 128], identity=idf32[0:65, 0:65]),
                         reads=['OTs', 'idf32'], writes=['pOT'])
                s.op('dve', lambda e: e.reciprocal(out=rec[:], in_=pOT[:, :, 64]), reads=['pOT'], writes=['rec'])
                s.op('dve', lambda e: e.tensor_tensor(out=og[:, :, h * 64:(h + 1) * 64], in0=pOT[:, :, 0:64], in1=rec[:].unsqueeze(2).to_broadcast([128, 4, 64]), op=ALU.mult),
                     reads=['pOT', 'rec'], writes=['og'])
            for j in range(4):
                i = j % 2
                rows = slice(qt * 512 + j * 128, qt * 512 + (j + 1) * 128)
                s.dma('sp', 'dm_sg%d' % i, sgt[i][:], sg_d[rows, :], writes=['sgt%d' % i])
                s.dma('sp', 'dm_h1%d' % i, h1t[i][:], h1_d[rows, :], writes=['h1t%d' % i])
                s.op('dve', lambda e: e.tensor_tensor(out=ogg[:], in0=og[:, j, :], in1=sgt[i][:], op=ALU.mult), reads=['og', 'sgt%d' % i], writes=['ogg'])
                for k in range(8):
                    s.op('pe', lambda e: e.transpose(out=pTg[:, k, :], in_=ogg[:, k * 128:(k + 1) * 128], identity=identb[:]), reads=['ogg', idk], writes=['pTg'])
                s.op('act', lambda e: e.activation(out=ogT[:], in_=pTg[:], func=AF.Copy), reads=['pTg'], writes=['ogT'])
                for half in range(2):
                    cs_ = slice(half * 512, (half + 1) * 512)
                    for k in range(8):
                        s.op('pe', lambda e: e.matmul(pA[:, :], lhsT=ogT[:, k, :], rhs=wo[:, k, cs_], start=(k == 0), stop=(k == 7)), reads=['ogT', 'wo'], writes=['pAo'])
                    s.op('dve', lambda e: e.tensor_tensor(out=h2[:, cs_], in0=pA[:, :], in1=gate1[:, 0, cs_], op=ALU.mult), reads=['pAo', 'gate1'], writes=['h2'])
                s.op('pool', lambda e: e.tensor_tensor(out=h2[:], in0=h2[:], in1=h1t[i][:], op=ALU.add), reads=['h2', 'h1t%d' % i], writes=['h2'])
                s.op('act', lambda e: e.activation(out=junk[:], in_=h2[:], func=AF.Square, accum_out=st[:, 0:1]), reads=['h2'], writes=['junkc', 'stc'])
                s.op('dve', lambda e: e.tensor_scalar(out=st[:, 1:2], in0=st[:, 0:1], scalar1=1.0 / D, scalar2=EPS, op0=ALU.mult, op1=ALU.add), reads=['stc'], writes=['stc'])
                s.op('act', lambda e: e.activation(out=st[:, 2:3], in_=st[:, 1:2], func=AF.Sqrt), reads=['stc'], writes=['stc'])
                s.op('dve', lambda e: e.reciprocal(out=st[:, 3:4], in_=st[:, 2:3]), reads=['stc'], writes=['stc'])
                s.op('dve', lambda e: e.scalar_tensor_tensor(out=osb[i][:], in0=h2[:], scalar=st[:, 3:4], in1=fg[:], op0=ALU.mult, op1=ALU.mult),
                     reads=['h2', 'stc', 'fg'], writes=['osb%d' % i])
                s.dma('act', 'dm_o%d' % i, out_d[rows, :], osb[i][:], reads=['osb%d' % i], writes=['out'])
        s.wait_all('sp', ['out'])
        s.barrier()
    return nc


def emit_l0b_f(nc, s, P, xin, yssd_all, ztok_all, gutok_all, y5tok_all, cvec_d, aw0g_d, ab0g_d, adaw1_d, adab1_d, reps_d,
               gluw_d, wout_d, w1_d, rope_d, h1_all, sg_all, v_all, kt_all, qz_all):
    NTILE_B = NCH
    with ExitStack() as es:
        s.es = es
        s.prefix = P
        s.bulk_on = False
        identb, idk = make_identity(s, 'identb', BF16)
        reps = s.sb('reps', [128, 2176]); rope = s.sb('rope', [128, NTILE_B, 2, 32])
        s.dma('sp', 'dm_c', reps[:], reps_d[:, :], writes=['reps'])
        s.dma('sp', 'dm_c', rope[:], rope_d[:, :, :, :], writes=['rope'])
        ssdn = reps[:, 0:1024]; ds5 = reps[:, 1024:1536]; glub = reps[:, 1536:2048]; qg = reps[:, 2048:2112]; kg = reps[:, 2112:2176]
        modT = phase0_mod(s, nc, cvec_d, adaw1_d, adab1_d, 16)
        sc1 = s.sb('sc1', [128, 8, 2])
        s.op('dve', lambda e: e.tensor_scalar(out=sc1[:], in0=modT[:, 8:16, :], scalar1=1.0, scalar2=None, op0=ALU.add), reads=['modT'], writes=['modv'])
        sh = modT[:, 0:8, :]
        gate0 = s.sb('gate0', [128, 2, 1024])
        with ExitStack() as t0:
            cv = s.sb('cv2', [128, 8, 2], es=t0); scv = s.sb('scv2', [128, 8, 2], es=t0)
            s.dma('sp', 'dm_c', cv[:], cvec_d[:, :, :], writes=['cv2'])
            s.op('act', lambda e: e.activation(out=scv[:], in_=cv[:], func=AF.Silu), reads=['cv2'], writes=['scv'])
            mod_rep(s, nc, scv, aw0g_d, ab0g_d, 1024, gate0, 'gate0', t0, 'g0')
            s.barrier()
        gluw = s.sb('gluw', [128, 4, 512], BF16); wout = s.sb('wout', [128, 12, 1024], BF16); w1 = s.sb('w1', [128, 8, 2560], BF16)
        with ExitStack() as t1:
            stg = [s.sb('wst%d' % i, [128, 2560], es=t1) for i in range(2)]
            n = 0
            for (wd, wsb, kw, nk, ncol) in ((gluw_d, gluw, 'gluw', 4, 512), (wout_d, wout, 'wout', 12, 1024), (w1_d, w1, 'w1', 8, 2560)):
                for k in range(nk):
                    i = n % 2; n += 1
                    s.dma('sp' if i == 0 else 'act', 'dm_wst%d' % i, stg[i][:, 0:ncol], wd[:, k, :], writes=['wst%d' % i])
                    s.op('pool', lambda e: e.tensor_copy(out=wsb[:, k, :], in_=stg[i][:, 0:ncol]), reads=['wst%d' % i], writes=[kw])
            s.barrier()

        npj = NormProj(s, nc, None, identb, idk, sc1, sh, es)
        ld = lambda n, w: [s.sb('%s%d' % (n, i), [128, w]) for i in range(2)]
        ysb = ld('ysb', 1024); zsb = ld('zsb', 1024); xrb = ld('xrb', 1024); y5b = ld('y5b', 512); ub = ld('ub', 512); g5b = ld('g5b', 512)
        tt = s.sb('tt', [128, 1024]); Fb = s.sb('Fb', [128, 1536], BF16); vv = s.sb('vv', [128, 512]); vb = s.sb('vb', [128, 512], BF16)
        sg5 = s.sb('sg5', [128, 512]); sgg = s.sb('sgg', [128, 512]); stt = s.sb('stt', [128, 8]); junkb = s.sb('junkb', [128, 1024], BF16)
        vT = s.sb('vT', [128, 4, 128], BF16); FT = s.sb('FT', [128, 12, 128], BF16); h1 = [s.sb('h1_%d' % i, [128, 1024]) for i in range(2)]
        a1T = s.sb('a1T', [128, 8, 128], BF16)
        qf = s.sb('qf', [128, 512]); qs = s.sb('qs', [128, 512]); qst = s.sb('qst', [128, 4, 8]); qr = s.sb('qr', [128, 512])
        qo = [s.sb('qo%d' % i, [128, 1024], BF16) for i in range(2)]; ko = [s.sb('ko%d' % i, [128, 256], BF16) for i in range(2)]
        vo = [s.sb('vo%d' % i, [128, 256], BF16) for i in range(2)]; sgo = [s.sb('sgo%d' % i, [128, 1024], BF16) for i in range(2)]
        qzt = [s.sb('qzt%d' % i, [128, 16, 128], BF16) for i in range(2)]; ktt = [s.sb('ktt%d' % i, [128, 2, 128], BF16) for i in range(2)]
        for i_ in range(2):
            s.op('pool', lambda e: e.memset(qzt[i_][:], 0.0), writes=['qzt%d' % i_])
        pTv = s.ps('pTv', [128, 8, 128], BF16); pF = [s.ps('pF%d' % i, [128, 8, 128], BF16) for i in range(2)]
        pA = [s.ps('pA%d' % i, [128, 512]) for i in range(3)]
        npa = [0]

        def nextpa():
            i = npa[0] % 3; npa[0] += 1
            return pA[i], 'pA%d' % i

        def headnorm(src_ps, kps, nh, gain, extra, dst, kdst, ti):
            w = nh * 64
            s.op('act', lambda e: e.activation(out=qs[:, 0:w], in_=src_ps, func=AF.Square), reads=[kps], writes=['qs'])
            s.op('dve', lambda e: e.tensor_reduce(out=qst[:, 0, 0:nh], in_=qs[:, 0:w].rearrange("p (h q) -> p h q", q=64), axis=AX.X, op=ALU.add),
                 reads=['qs'], writes=['qst'])
            s.op('dve', lambda e: e.tensor_scalar(out=qst[:, 1, 0:nh], in0=qst[:, 0, 0:nh], scalar1=1.0 / 64, scalar2=EPS, op0=ALU.mult, op1=ALU.add),
                 reads=['qst'], writes=['qst'])
            s.op('act', lambda e: e.activation(out=qst[:, 2, 0:nh], in_=qst[:, 1, 0:nh], func=AF.Sqrt), reads=['qst'], writes=['qst'])
            s.op('dve', lambda e: e.reciprocal(out=qst[:, 3, 0:nh], in_=qst[:, 2, 0:nh]), reads=['qst'], writes=['qst'])
            s.op('dve', lambda e: e.tensor_tensor(out=qf[:, 0:w].rearrange("p (h q) -> p h q", q=64), in0=src_ps.rearrange("p (h q) -> p h q", q=64),
                                                  in1=qst[:, 3, 0:nh].unsqueeze(2).to_broadcast([128, nh, 64]), op=ALU.mult), reads=[kps, 'qst'], writes=['qf'])
            s.op('dve', lambda e: e.scalar_tensor_tensor(out=qf[:, 0:w].rearrange("p (h q) -> p h q", q=64), in0=qf[:, 0:w].rearrange("p (h q) -> p h q", q=64),
                                                         scalar=extra, in1=gain.unsqueeze(1).to_broadcast([128, nh, 64]), op0=ALU.mult, op1=ALU.mult),
                 reads=['qf', 'reps'], writes=['qf'])
            x5 = qf[:, 0:w].rearrange("p (h a t r) -> p h a t r", a=2, t=2, r=16)
            o5 = dst.rearrange("p (h a t r) -> p h a t r", a=2, t=2, r=16)
            t5 = qr[:, 0:w].rearrange("p (h a t r) -> p h a t r", a=2, t=2, r=16)
            cs = rope[:, ti, 0, :].rearrange("p (a r) -> p a r", a=2).unsqueeze(1).to_broadcast([128, nh, 2, 16])
            sn = rope[:, ti, 1, :].rearrange("p (a r) -> p a r", a=2).unsqueeze(1).to_broadcast([128, nh, 2, 16])
            x1, x2 = x5[:, :, :, 0, :], x5[:, :, :, 1, :]
            s.op('dve', lambda e: e.tensor_tensor(out=t5[:, :, :, 0, :], in0=x1, in1=cs, op=ALU.mult), reads=['qf', 'rope'], writes=['qr'])
            s.op(PENGB, lambda e: e.tensor_tensor(out=t5[:, :, :, 1, :], in0=x2, in1=sn, op=ALU.mult), reads=['qf', 'rope'], writes=['qr1'])
            s.op('dve', lambda e: e.tensor_tensor(out=o5[:, :, :, 0, :], in0=t5[:, :, :, 0, :], in1=t5[:, :, :, 1, :], op=ALU.subtract), reads=['qr', 'qr1'], writes=[kdst])
            s.op('dve', lambda e: e.tensor_tensor(out=t5[:, :, :, 0, :], in0=x2, in1=cs, op=ALU.mult), reads=['qf', 'rope'], writes=['qr'])
            s.op(PENGB, lambda e: e.tensor_tensor(out=t5[:, :, :, 1, :], in0=x1, in1=sn, op=ALU.mult), reads=['qf', 'rope'], writes=['qr1'])
            s.op('dve', lambda e: e.tensor_tensor(out=o5[:, :, :, 1, :], in0=t5[:, :, :, 0, :], in1=t5[:, :, :, 1, :], op=ALU.add), reads=['qr', 'qr1'], writes=[kdst])

        for ti in range(NTILE_B):
            i = ti % 2
            which = 1 if ti < 2 else 0
            rows = slice(ti * 128, (ti + 1) * 128)
            s.dma('sp', 'dm_xrb%d' % i, xrb[i][:], xin[rows, :], writes=['xrb%d' % i])
            s.dma('sp', 'dm_ysb%d' % i, ysb[i][:].rearrange("p (q c) -> p q c", q=4), yssd_all[:, rows, :].rearrange("q p c -> p q c"), writes=['ysb%d' % i])
            s.dma('sp', 'dm_zsb%d' % i, zsb[i][:].rearrange("p (q c) -> p q c", q=4), ztok_all[:, rows, 0:256].rearrange("q p c -> p q c"), writes=['zsb%d' % i])
            s.dma('sp', 'dm_y5b%d' % i, y5b[i][:].rearrange("p (q c) -> p q c", q=4), y5tok_all[:, rows, :].rearrange("q p c -> p q c"), writes=['y5b%d' % i])
            s.dma('sp', 'dm_ub%d' % i, ub[i][:].rearrange("p (q c) -> p q c", q=4), gutok_all[:, rows, 128:256].rearrange("q p c -> p q c"), writes=['ub%d' % i])
            s.dma('sp', 'dm_g5b%d' % i, g5b[i][:].rearrange("p (q c) -> p q c", q=4), gutok_all[:, rows, 0:128].rearrange("q p c -> p q c"), writes=['g5b%d' % i])
            s.op('act', lambda e: e.activation(out=tt[:], in_=zsb[i][:], func=AF.Silu), reads=['zsb%d' % i], writes=['tt'])
            s.op('dve', lambda e: e.tensor_tensor(out=tt[:], in0=tt[:], in1=ysb[i][:], op=ALU.mult), reads=['tt', 'ysb%d' % i], writes=['tt'])
            s.op('act', lambda e: e.activation(out=junkb[:], in_=tt[:], func=AF.Square, accum_out=stt[:, 0:1]), reads=['tt'], writes=['junkb', 'stt'])
            s.op('dve', lambda e: e.tensor_scalar(out=stt[:, 1:2], in0=stt[:, 0:1], scalar1=1.0 / 1024, scalar2=EPS, op0=ALU.mult, op1=ALU.add), reads=['stt'], writes=['stt'])
            s.op('act', lambda e: e.activation(out=stt[:, 2:3], in_=stt[:, 1:2], func=AF.Sqrt), reads=['stt'], writes=['stt'])
            s.op('dve', lambda e: e.reciprocal(out=stt[:, 3:4], in_=stt[:, 2:3]), reads=['stt'], writes=['stt'])
            s.op('dve', lambda e: e.scalar_tensor_tensor(out=Fb[:, 0:1024], in0=tt[:], scalar=stt[:, 3:4], in1=ssdn, op0=ALU.mult, op1=ALU.mult),
                 reads=['tt', 'stt', 'reps'], writes=['Fb'])
            s.op(PENGB, lambda e: e.tensor_tensor(out=vv[:], in0=ub[i][:], in1=ds5, op=ALU.mult), reads=['ub%d' % i, 'reps'], writes=['vv'])
            s.op(PENGB, lambda e: e.tensor_tensor(out=vv[:], in0=vv[:], in1=y5b[i][:], op=ALU.add), reads=['vv', 'y5b%d' % i], writes=['vv'])
            s.op('act', lambda e: e.activation(out=vv[:], in_=vv[:], func=AF.Gelu_apprx_tanh), reads=['vv'], writes=['vv'])
            s.op(PENGB, lambda e: e.tensor_copy(out=vb[:], in_=vv[:]), reads=['vv'], writes=['vb'])
            for k in range(4):
                s.op('pe', lambda e: e.transpose(out=pTv[:, k, :], in_=vb[:, k * 128:(k + 1) * 128], identity=identb[:]), reads=['vb', idk], writes=['pTv'])
            s.op('act', lambda e: e.activation(out=vT[:], in_=pTv[:, 0:4, :], func=AF.Copy), reads=['pTv'], writes=['vT'])
            pg, kpg = nextpa()
            for k in range(4):
                s.op('pe', lambda e: e.matmul(pg[:, :], lhsT=vT[:, k, :], rhs=gluw[:, k, :], start=(k == 0), stop=(k == 3)), reads=['vT', 'gluw'], writes=[kpg])
            s.op('dve', lambda e: e.scalar_tensor_tensor(out=sg5[:], in0=pg[:, :], scalar=1.0, in1=glub, op0=ALU.mult, op1=ALU.add), reads=[kpg, 'reps'], writes=['sg5'])
            s.op('act', lambda e: e.activation(out=sg5[:], in_=sg5[:], func=AF.Sigmoid), reads=['sg5'], writes=['sg5'])
            s.op('act', lambda e: e.activation(out=sgg[:], in_=g5b[i][:], func=AF.Silu), reads=['g5b%d' % i], writes=['sgg'])
            s.op(PENGB, lambda e: e.tensor_tensor(out=sg5[:], in0=sg5[:], in1=sgg[:], op=ALU.mult), reads=['sg5', 'sgg'], writes=['sg5'])
            s.op(PENGB, lambda e: e.tensor_tensor(out=Fb[:, 1024:1536], in0=sg5[:], in1=vv[:], op=ALU.mult), reads=['sg5', 'vv'], writes=['Fb'])
            for k in range(12):
                pf = pF[k // 8]
                s.op('pe', lambda e: e.transpose(out=pf[:, k % 8, :], in_=Fb[:, k * 128:(k + 1) * 128], identity=identb[:]), reads=['Fb', idk], writes=['pF%d' % (k // 8)])
            s.op('act', lambda e: e.activation(out=FT[:, 0:8, :], in_=pF[0][:, :, :], func=AF.Copy), reads=['pF0'], writes=['FT'])
            s.op('act', lambda e: e.activation(out=FT[:, 8:12, :], in_=pF[1][:, 0:4, :], func=AF.Copy), reads=['pF1'], writes=['FT'])
            for half in range(2):
                po, kpo = nextpa()
                cs_ = slice(half * 512, (half + 1) * 512)
                for k in range(12):
                    s.op('pe', lambda e: e.matmul(po[:, :], lhsT=FT[:, k, :], rhs=wout[:, k, cs_], start=(k == 0), stop=(k == 11)), reads=['FT', 'wout'], writes=[kpo])
                s.op('dve', lambda e: e.tensor_tensor(out=h1[i][:, cs_], in0=po[:, :], in1=gate0[:, which, cs_], op=ALU.mult), reads=[kpo, 'gate0'], writes=['h1_%d' % i])
                s.op(PENGB, lambda e: e.tensor_tensor(out=h1[i][:, cs_], in0=h1[i][:, cs_], in1=xrb[i][:, cs_], op=ALU.add), reads=['h1_%d' % i, 'xrb%d' % i], writes=['h1_%d' % i])
            s.dma('act', 'dm_h1%d' % i, h1_all[rows, :], h1[i][:], reads=['h1_%d' % i], writes=['h1o'])
            npj.sub(0, which, a1T, 'a1T', 0, src=(h1[i], 'h1_%d' % i))
            for nt_ in range(5):
                pq, kpq = nextpa()
                for k in range(8):
                    s.op('pe', lambda e: e.matmul(pq[:, :], lhsT=a1T[:, k, :], rhs=w1[:, k, nt_ * 512:(nt_ + 1) * 512], start=(k == 0), stop=(k == 7)),
                         reads=['a1T', 'w1'], writes=[kpq])
                if nt_ < 2:
                    headnorm(pq[:, :], kpq, 8, qg, 0.125, qo[i][:, nt_ * 512:(nt_ + 1) * 512], 'qo%d' % i, ti)
                elif nt_ == 2:
                    headnorm(pq[:, 0:256], kpq, 4, kg, 1.0, ko[i][:, :], 'ko%d' % i, ti)
                    s.op('act', lambda e: e.activation(out=vo[i][:], in_=pq[:, 256:512], func=AF.Copy), reads=[kpq], writes=['vo%d' % i])
                else:
                    s.op('act', lambda e: e.activation(out=sgo[i][:, (nt_ - 3) * 512:(nt_ - 2) * 512], in_=pq[:, :], func=AF.Silu), reads=[kpq], writes=['sgo%d' % i])
            for j in range(8):
                s.op('pe', lambda e: e.transpose(out=pF[0][:, j, :], in_=qo[i][:, j * 128:(j + 1) * 128], identity=identb[:]), reads=['qo%d' % i, idk], writes=['pF0'])
            s.op('act', lambda e: e.activation(out=qzt[i][0:64, 0:16:2, :], in_=pF[0][0:64, :, :], func=AF.Copy), reads=['pF0'], writes=['qzt%d' % i])
            s.op('act', lambda e: e.activation(out=qzt[i][64:128, 1:16:2, :], in_=pF[0][64:128, :, :], func=AF.Copy), reads=['pF0'], writes=['qzt%d' % i])
            for j in range(2):
                s.op('pe', lambda e: e.transpose(out=pTv[:, j, :], in_=ko[i][:, j * 128:(j + 1) * 128], identity=identb[:]), reads=['ko%d' % i, idk], writes=['pTv'])
            s.op('act', lambda e: e.activation(out=ktt[i][:], in_=pTv[:, 0:2, :], func=AF.Copy), reads=['pTv'], writes=['ktt%d' % i])
            s.dma('act', 'dm_q%d' % i, qz_all[ti, :, :, :], qzt[i][:], reads=['qzt%d' % i], writes=['qout'])
            s.dma('act', 'dm_q%d' % i, kt_all[:, :, rows], ktt[i][:], reads=['ktt%d' % i], writes=['kout'])
            s.dma('act', 'dm_q%d' % i, v_all[rows, :], vo[i][:], reads=['vo%d' % i], writes=['vout'])
            s.dma('act', 'dm_q%d' % i, sg_all[rows, :], sgo[i][:], reads=['sgo%d' % i], writes=['sgout'])
        s.wait_all('sp', ['h1o', 'qout', 'kout', 'vout', 'sgout'])
        s.barrier()
        s.bulk_on = False
    return


def emit_l1b_f(nc, s, P, qz_all, kt_d, v_all, sg_all, h1_all, wo_d, cvec_d, aw1g_d, ab1g_d, fg_d, out_d, nq_tiles=4):
    PERM = [0, 4, 1, 5, 2, 6, 3, 7, 8, 12, 9, 13, 10, 14, 11, 15]
    sx = nc.sync.partition_id() % 4
    tile0 = 2 + 16 * sx
    with ExitStack() as es:
        s.es = es
        s.prefix = P
        s.bulk_on = False
        identb, idk = make_identity(s, 'identb', BF16)
        identf = s.sb('identb_f_alias', [1, 1])
        KT = s.sb('KT', [128, 2, T], BF16); V = s.sb('V', [128, NCH, 4, 65], BF16)
        s.dma('sp', 'dm_kt', KT[:, 0, :], kt_d[:, 0, :], writes=['KT'])
        s.dma('act', 'dm_kt', KT[:, 1, :], kt_d[:, 1, :], writes=['KT'])
        s.op('pool', lambda e: e.memset(V[:, :, :, 64:65], 1.0), writes=['V'])
        for c4 in range(0, NCH, 11):
            for kh_ in range(4):
                s.dma('sp' if kh_ % 2 == 0 else 'act', 'dm_v', V[:, c4:c4 + 11, kh_, 0:64],
                      v_all[c4 * 128:(c4 + 11) * 128, kh_ * 64:(kh_ + 1) * 64].rearrange("(c p) d -> p c d", p=128), reads=['V'], writes=['V'])
        fg = s.sb('fg', [128, 1024])
        s.dma('sp', 'dm_c', fg[:], fg_d[:, :], writes=['fg'])
        gate1 = s.sb('gate1', [128, 2, 1024])
        with ExitStack() as t0:
            cv = s.sb('cv2', [128, 8, 2], es=t0); scv = s.sb('scv2', [128, 8, 2], es=t0)
            s.dma('sp', 'dm_c', cv[:], cvec_d[:, :, :], writes=['cv2'])
            s.op('act', lambda e: e.activation(out=scv[:], in_=cv[:], func=AF.Silu), reads=['cv2'], writes=['scv'])
            mod_rep(s, nc, scv, aw1g_d, ab1g_d, 1024, gate1, 'gate1', t0, 'g1')
            s.barrier()
        wo = s.sb('wo', [128, 8, 1024], BF16)
        with ExitStack() as t1:
            stg = [s.sb('wst%d' % i, [128, 1024], es=t1) for i in range(2)]
            for k in range(8):
                i = k % 2
                s.dma('sp' if i == 0 else 'act', 'dm_wst%d' % i, stg[i][:], wo_d[:, k, :], writes=['wst%d' % i])
                s.op('pool', lambda e: e.tensor_copy(out=wo[:, k, :], in_=stg[i][:]), reads=['wst%d' % i], writes=['wo'])
            s.barrier()
        idf = None
        qz = [s.sb('qz%d' % i, [128, 16, 512], BF16) for i in range(2)]
        og = s.sb('og', [128, 4, 1024])
        PT = [s.sb('PT%d' % i, [128, 1024], BF16) for i in range(2)]
        OTs = s.sb('OTs', [65, 512]); rec = s.sb('rec', [128, 4])
        sgt = [s.sb('sgt%d' % i, [128, 1024], BF16) for i in range(2)]; h1t = [s.sb('h1t%d' % i, [128, 1024]) for i in range(2)]
        ogg = s.sb('ogg', [128, 1024], BF16); ogT = s.sb('ogT', [128, 8, 128], BF16)
        h2 = s.sb('h2', [128, 1024]); osb = [s.sb('osb%d' % i, [128, 1024]) for i in range(2)]
        junk = s.sb('junkc', [128, 1024], BF16); st = s.sb('stc', [128, 8])
        pS = [s.ps('pS%d' % i, [128, 1024]) for i in range(2)]
        pO = [s.ps('pO%d' % i, [128, 512]) for i in range(1)]
        pOT = s.ps('pOT', [128, 4, 65])
        pTg = s.ps('pTg', [128, 8, 128], BF16)
        pA = s.ps('pAo', [128, 512])
        idf32 = s.sb('idf32', [128, 128])
        s.op('pool', lambda e: e.memset(idf32[:], 1.0), writes=['idf32'])
        s.op('pool', lambda e: e.affine_select(out=idf32[:], in_=idf32[:], pattern=[[-1, 128]], compare_op=ALU.is_equal,
                                               fill=0.0, base=0, channel_multiplier=1), reads=['idf32'], writes=['idf32'])
        def dyn_dma(sem, out, in_, wkey, eng='sp'):
            s._deps(eng, [], [wkey])
            if sem not in s.sems:
                s._mk(sem); s.dma_sems.add(sem)
            ins = s.E[eng].dma_start(out=out, in_=in_)
            s.cnt[sem] += 16; ins.then_inc(s.sems[sem], 16); s.nins += 1
            s._done((sem, s.cnt[sem]), [], [wkey])

        qzv = qz_all[bass.ds(tile0, 16), :, :, :]
        sgv = sg_all.rearrange("(t p) c -> t p c", p=128)[bass.ds(2 + 16 * (nc.scalar.partition_id() % 4), 16), :, :]
        h1v = h1_all.rearrange("(t p) c -> t p c", p=128)[bass.ds(2 + 16 * (nc.gpsimd.partition_id() % 4), 16), :, :]

        def load_q(qt_, buf):
            dyn_dma('dm_qz%d' % buf, qz[buf][:].rearrange("p h (j t) -> p h j t", j=4),
                    qzv[4 * qt_:4 * qt_ + 4, :, :, :].rearrange("j p h t -> p h j t"), 'qz%d' % buf)
        load_q(0, 0)
        nit = 0
        nho = 0
        primed = [False]
        for qt in range(nq_tiles):
            qi = qt % 2
            if qt + 1 < nq_tiles:
                load_q(qt + 1, (qt + 1) % 2)
            for h in range(16):
                kh = PERM[h] // 4; pair, e_ = kh // 2, kh % 2
                po = pO[0]; kpo = 'pO0'; nho += 1
                NP = NCH // 2

                def qk2(pp, it, h_=h, qi_=qi):
                    ps_ = pS[it % 2]
                    pair_ = (PERM[h_] // 4) // 2
                    for u in range(2):
                        kc = 2 * pp + u
                        s.op('pe', lambda e: e.matmul(ps_[:, u * 512:(u + 1) * 512], lhsT=KT[:, pair_, kc * 128:(kc + 1) * 128], rhs=qz[qi_][:, h_, :], start=True, stop=True),
                             reads=['KT', 'qz%d' % qi_], writes=['pS%d' % (it % 2)])
                if not primed[0]:
                    qk2(0, nit)
                primed[0] = False
                for pp in range(NP):
                    it = nit + pp
                    s.op('act', lambda e: e.activation(out=PT[it % 2][:], in_=pS[it % 2][:, :], func=AF.Exp), reads=['pS%d' % (it % 2)], writes=['PT%d' % (it % 2)])
                    if pp + 1 < NP:
                        qk2(pp + 1, it + 1)
                    for u in range(2):
                        kc = 2 * pp + u
                        s.op('pe', lambda e: e.matmul(po[0:65, :], lhsT=V[:, kc, kh, :], rhs=PT[it % 2][:, u * 512:(u + 1) * 512], start=(kc == 0), stop=(kc == NCH - 1)),
                             reads=['V', 'PT%d' % (it % 2)], writes=[kpo])
                nit += NP
                if h + 1 < 16:
                    qk2(0, nit, h + 1, qi)
                    primed[0] = True
                s.op('dve', lambda e: e.tensor_copy(out=OTs[:, :], in_=po[0:65, :]), reads=[kpo], writes=['OTs'])
                for j in range(4):
                    s.op('pe', lambda e: e.transpose(out=pOT[:, j, :], in_=OTs[:, j * 128:(j + 1) * 128], identity=idf32[0:65, 0:65]),
                         reads=['OTs', 'idf32'], writes=['pOT'])
                s.op('dve', lambda e: e.reciprocal(out=rec[:], in_=pOT[:, :, 64]), reads=['pOT'], writes=['rec'])
                s.op('dve', lambda e: e.tensor_tensor(out=og[:, :, h * 64:(h + 1) * 64], in0=pOT[:, :, 0:64], in1=rec[:].unsqueeze(2).to_broadcast([128, 4, 64]), op=ALU.mult),
                     reads=['pOT', 'rec'], writes=['og'])
            for j in range(4):
                i = j % 2
                rows = slice(qt * 512 + j * 128, qt * 512 + (j + 1) * 128)
                dyn_dma('dm_sg%d' % i, sgt[i][:], sgv[4 * qt + j, :, :], 'sgt%d' % i, 'act')
                dyn_dma('dm_h1%d' % i, h1t[i][:].rearrange("p (a c) -> p a c", a=8), h1v[4 * qt + j, :, :].rearrange("p (a c) -> p a c", a=8), 'h1t%d' % i, 'pool')
                s.op('dve', lambda e: e.tensor_tensor(out=ogg[:], in0=og[:, j, :], in1=sgt[i][:], op=ALU.mult), reads=['og', 'sgt%d' % i], writes=['ogg'])
                for k in range(8):
                    s.op('pe', lambda e: e.transpose(out=pTg[:, k, :], in_=ogg[:, k * 128:(k + 1) * 128], identity=identb[:]), reads=['ogg', idk], writes=['pTg'])
                s.op('act', lambda e: e.activation(out=ogT[:], in_=pTg[:], func=AF.Copy), reads=['pTg'], writes=['ogT'])
                for half in range(2):
                    cs_ = slice(half * 512, (half + 1) * 512)
                    for k in range(8):
                        s.op('pe', lambda e: e.matmul(pA[:, :], lhsT=ogT[:, k, :], rhs=wo[:, k, cs_], start=(k == 0), stop=(k == 7)), reads=['ogT', 'wo'], writes=['pAo'])
                    s.op('dve', lambda e: e.tensor_tensor(out=h2[:, cs_], in0=pA[:, :], in1=gate1[:, 0, cs_], op=ALU.mult), reads=['pAo', 'gate1'], writes=['h2'])
                s.op('pool', lambda e: e.tensor_tensor(out=h2[:], in0=h2[:], in1=h1t[i][:], op=ALU.add), reads=['h2', 'h1t%d' % i], writes=['h2'])
                s.op('act', lambda e: e.activation(out=junk[:], in_=h2[:], func=AF.Square, accum_out=st[:, 0:1]), reads=['h2'], writes=['junkc', 'stc'])
                s.op('dve', lambda e: e.tensor_scalar(out=st[:, 1:2], in0=st[:, 0:1], scalar1=1.0 / D, scalar2=EPS, op0=ALU.mult, op1=ALU.add), reads=['stc'], writes=['stc'])
                s.op('act', lambda e: e.activation(out=st[:, 2:3], in_=st[:, 1:2], func=AF.Sqrt), reads=['stc'], writes=['stc'])
                s.op('dve', lambda e: e.reciprocal(out=st[:, 3:4], in_=st[:, 2:3]), reads=['stc'], writes=['stc'])
                s.op('dve', lambda e: e.scalar_tensor_tensor(out=osb[i][:], in0=h2[:], scalar=st[:, 3:4], in1=fg[:], op0=ALU.mult, op1=ALU.mult),
                     reads=['h2', 'stc', 'fg'], writes=['osb%d' % i])
                s.dma('act', 'dm_o%d' % i, out_d[rows, :], osb[i][:], reads=['osb%d' % i], writes=['out'])
        s.wait_all('sp', ['out'])
        s.barrier()
        s.bulk_on = False
    return


def build_fused():
    nc = bass.Bass("TRN2", target_bir_lowering=False)
    dr = lambda n, sh, kind="ExternalInput", dt=F32: nc.dram_tensor(n, list(sh), dt, kind=kind).ap()
    sc = lambda n, sh, dt=F32: nc.dram_tensor(n, list(sh), dt).ap()
    xin = dr("xin", [T, D]); cvec = dr("cvec", [128, 8, 2])
    adaw0 = dr("adaw0", [128, 8, 2048]); adab0 = dr("adab0", [128, 16])
    NW1 = SSD_NCM * 128 + SSD_NTM
    a1 = [dict(win=dr("a1_win%d" % q, [128, 8, NW1]), convw=dr("a1_convw%d" % q, [128, 4, 5]), convb=dr("a1_convb%d" % q, [128, 4]),
               dtb=dr("a1_dtb%d" % q, [128, 8]), alog=dr("a1_alog%d" % q, [128, 8]), dssd=dr("a1_dssd%d" % q, [128, 4])) for q in range(4)]
    cst = dr("cst", [128, 6, 128])
    a2 = [dict(win=dr("a2_win%d" % q, [128, 8, 640]), lam=dr("a2_lam%d" % q, [128, 3, 8]), bb=dr("a2_bb%d" % q, [128, 2, 4, 16]),
               cc=dr("a2_cc%d" % q, [128, 2, 8, 16])) for q in range(4)]
    esel = dr("esel", [128, 3, 8, 128]); msk = dr("msk", [128, 2, 128]); fsel = dr("fsel", [128, 8, 8, 128])
    aw0g = dr("aw0g", [128, 8, 1024]); ab0g = dr("ab0g", [128, 1024]); adaw1 = dr("adaw1", [128, 8, 2048]); adab1 = dr("adab1", [128, 16])
    reps = dr("reps", [128, 2176]); gluw = dr("gluw", [128, 4, 512]); wout = dr("wout", [128, 12, 1024]); w1 = dr("w1", [128, 8, 2560])
    rope = dr("rope", [128, NCH, 2, 32])
    wo = dr("wo", [128, 8, 1024]); aw1g = dr("aw1g", [128, 8, 1024]); ab1g = dr("ab1g", [128, 1024]); fg = dr("fg", [128, 1024])
    out_d = dr("out", [2048, D], "ExternalOutput")
    ztok_all = sc("ztok_all", [4, T, SSD_NTM]); yssd_all = sc("yssd_all", [4, T, 256]); gutok_all = sc("gutok_all", [4, T, 256])
    y5tok_all = sc("y5tok_all", [4, T, 128]); h1_all = sc("h1_all", [T, D]); sg_all = sc("sg_all", [T, 1024], BF16)
    v_all = sc("v_all", [T, 256], BF16); kt_all = sc("kt_all", [128, 2, T], BF16); qz_all = sc("qz_all", [NCH, 128, 16, 128], BF16)
    dbg = sc("dbg_scratch", [128, 4096])
    aT_all = sc("aT_all", [128, 8, T], BF16)
    with ExitStack() as es0:
        s = Sched(nc, es0)
        emit_norm0(nc, s, 'n0_', xin, cvec, adaw0, adab0, aT_all)
        for q in range(4):
            w = a1[q]
            emit_l0a_ssd(nc, s, 'a1%d_' % q, xin, cvec, adaw0, adab0, w['win'], w['convw'], w['convb'], w['dtb'], w['alog'], w['dssd'], cst,
                         ztok_all[q], yssd_all[q], dbg, aT_all=aT_all)
            w = a2[q]
            emit_l0a_s5(nc, s, 'a2%d_' % q, xin, cvec, adaw0, adab0, w['win'], w['lam'], w['bb'], w['cc'], esel, msk, gutok_all[q], None, dbg,
                        y5tok_d=y5tok_all[q], fsel_d=fsel, aT_all=aT_all)
        emit_l0b_f(nc, s, 'b_', xin, yssd_all, ztok_all, gutok_all, y5tok_all, cvec, aw0g, ab0g, adaw1, adab1, reps, gluw, wout, w1, rope,
                   h1_all, sg_all, v_all, kt_all, qz_all)
        emit_l1b_f(nc, s, 'c_', qz_all, kt_all, v_all, sg_all, h1_all, wo, cvec, aw1g, ab1g, fg, out_d)
        s.force = True
        s.wait_all('sp', ['out'])
        s.barrier()
        print("fused program: %d instructions, %d waits" % (s.nins, s.nwait))
    return nc


HEAD_PERM = [0, 4, 1, 5, 2, 6, 3, 7, 8, 12, 9, 13, 10, 14, 11, 15]


def fused_inputs(I, b):
    d = {}
    s0 = l0a_ssd_inputs(I, b, 0)
    d['xin'] = s0['xin']; d['cvec'] = s0['cvec']; d['adaw0'] = s0['adaw']; d['adab0'] = s0['adab']; d['cst'] = s0['cst']
    for q in range(4):
        a = l0a_ssd_inputs(I, b, q)
        for k in ('win', 'convw', 'convb', 'dtb', 'alog', 'dssd'):
            d['a1_%s%d' % (k, q)] = a[k]
        a = l0a_s5_inputs(I, b, q)
        for k in ('win', 'lam', 'bb', 'cc'):
            d['a2_%s%d' % (k, q)] = a[k]
        if q == 0:
            d['esel'] = a['esel']; d['msk'] = a['msk']
    d['fsel'] = fsel_const()
    d['aw0g'] = kp(I['ada_w'][0][:, 2048:3072]); d['ab0g'] = rep(I['ada_b'][0][2048:3072])
    d['adaw1'] = kp(I['ada_w'][1][:, :2048]); d['adab1'] = colv(I['ada_b'][1][:2048])
    d['reps'] = rep(np.concatenate([I['ev_ssd_norm'][0], I['ev_d_s5'][0], I['ev_glu_b'][0], I['od_q_gain'][0], I['od_k_gain'][0]]))
    d['gluw'] = np.ascontiguousarray(I['ev_glu_w'][0].reshape(4, 128, 512).transpose(1, 0, 2))
    d['wout'] = np.ascontiguousarray(I['ev_w_out'][0].reshape(12, 128, 1024).transpose(1, 0, 2))
    W1 = I['od_w_in'][0]
    hp = np.concatenate([np.arange(64) + 64 * h for h in HEAD_PERM])
    W1p = np.concatenate([W1[:, 0:1024][:, hp], W1[:, 1024:1536], W1[:, 1536:2560][:, hp]], axis=1)
    d['w1'] = kp(W1p)
    cos, sin = rope_tables()
    rp = np.zeros((T, 2, 32), np.float32); rp[:NCTX, 0] = 1.0; rp[NCTX:, 0] = cos; rp[NCTX:, 1] = sin
    d['rope'] = np.ascontiguousarray(rp.reshape(NCH, 128, 2, 32).transpose(1, 0, 2, 3))
    d['wo'] = kp(I['od_w_out'][0][hp, :])
    d['aw1g'] = kp(I['ada_w'][1][:, 2048:3072]); d['ab1g'] = rep(I['ada_b'][1][2048:3072]); d['fg'] = rep(I['final_gain'])
    return d


P = 128
def kp(w):
    K, N = w.shape
    return np.ascontiguousarray(w.reshape(K // P, P, N).transpose(1, 0, 2))
def colv(v):
    return np.ascontiguousarray(v.reshape(-1, P).T)
def rep(v):
    return np.ascontiguousarray(np.tile(np.asarray(v, np.float32).reshape(1, -1), (P, 1)))
def consts_ssd():
    s = np.arange(P)[:, None]; l = np.arange(P)[None, :]
    trif = (s <= l).astype(np.float32); trib = (s >= l).astype(np.float32)
    mbf = np.where(l >= s, 0.0, -30000.0).astype(np.float32)
    mbb = np.where(l <= s, 0.0, -30000.0).astype(np.float32)
    sellast = np.zeros((P, P), np.float32); sellast[P - 1, :] = 1
    selfirst = np.zeros((P, P), np.float32); selfirst[0, :] = 1
    return np.ascontiguousarray(np.stack([trif, trib, mbf, mbb, sellast, selfirst], axis=1))
def l0a_ssd_inputs(I, b, q):
    f = np.float32
    xin = np.ascontiguousarray(np.concatenate([I['ctx'][b], I['x'][b]], axis=0))
    cvec = np.ascontiguousarray(np.stack([colv(I['c'][b]), colv(I['c_ctx'])], axis=2))
    adaw = kp(I['ada_w'][0][:, :2048]); adab = colv(I['ada_b'][0][:2048])
    W = I['ev_w_in'][0]
    zc = W[:, 256 * q:256 * (q + 1)]
    xc = W[:, 1024 + 256 * q:1024 + 256 * (q + 1)]
    Bc = W[:, 2048 + 128 * q:2048 + 128 * (q + 1)]
    Cc = W[:, 2560 + 128 * q:2560 + 128 * (q + 1)]
    dtc = np.concatenate([W[:, 3072 + 4 * q:3072 + 4 * q + 4], W[:, 3072 + 16 + 4 * q:3072 + 16 + 4 * q + 4]], axis=1)
    win = kp(np.concatenate([xc, Bc, Cc, zc, dtc], axis=1))
    cw = I['ev_conv_w'][0]; cb = I['ev_conv_b'][0]
    idx = np.concatenate([np.arange(256 * q, 256 * (q + 1)), 1024 + np.arange(128 * q, 128 * (q + 1)), 1536 + np.arange(128 * q, 128 * (q + 1))])
    convw = np.ascontiguousarray(cw[idx].reshape(4, P, 5).transpose(1, 0, 2)); convb = np.ascontiguousarray(cb[idx].reshape(4, P).T)
    dtb = rep(np.concatenate([I['ev_dt_bias'][0][0][4 * q:4 * q + 4], I['ev_dt_bias'][0][1][4 * q:4 * q + 4]]))
    alog = rep(np.concatenate([I['ev_a_log'][0][0][4 * q:4 * q + 4], I['ev_a_log'][0][1][4 * q:4 * q + 4]]))
    dssd = rep(I['ev_d_ssd'][0][4 * q:4 * q + 4])
    return dict(xin=xin, cvec=cvec, adaw=adaw, adab=adab, win=win, convw=convw, convb=convb, dtb=dtb, alog=alog, dssd=dssd, cst=consts_ssd())

def l0a_s5_inputs(I, b, q):
    xin = np.ascontiguousarray(np.concatenate([I['ctx'][b], I['x'][b]], axis=0))
    cvec = np.ascontiguousarray(np.stack([colv(I['c'][b]), colv(I['c_ctx'])], axis=2))
    adaw = kp(I['ada_w'][0][:, :2048]); adab = colv(I['ada_b'][0][:2048])
    W = I['ev_w_in'][0]
    ucols = W[:, 3104 + 128 * q:3104 + 128 * (q + 1)]
    gcols = W[:, 3616 + 128 * q:3616 + 128 * (q + 1)]
    upad = np.zeros((1024, 384), np.float32)
    for g in range(8):
        hh, r = g // 3, g % 3
        upad[:, hh * 128 + 32 * r:hh * 128 + 32 * r + 16] = ucols[:, 16 * g:16 * g + 16]
    win = kp(np.concatenate([upad, gcols, ucols], axis=1))
    G0 = 8 * q
    def ep(fn):
        return np.ascontiguousarray(np.concatenate([fn(0), fn(1)], axis=0))
    lam = np.zeros((128, 3, 8), np.float32)
    for d in range(2):
        for k in range(4):
            for e in range(2):
                g = G0 + 2 * k + e
                lam[64 * e:64 * e + 64, 0, d * 4 + k] = I['ev_lam_re'][0][d, g]
                lam[64 * e:64 * e + 64, 1, d * 4 + k] = I['ev_lam_im'][0][d, g]
                lam[64 * e:64 * e + 64, 2, d * 4 + k] = I['ev_log_step'][0][d, g]
    bb = np.zeros((128, 2, 4, 16), np.float32); cc = np.zeros((128, 2, 8, 16), np.float32)
    for k in range(4):
        for e in range(2):
            g = G0 + 2 * k + e
            bb[64 * e:64 * e + 64, 0, k] = I['ev_b_re'][0][g]; bb[64 * e:64 * e + 64, 1, k] = I['ev_b_im'][0][g]
            for d in range(2):
                cc[64 * e:64 * e + 64, 0, d * 4 + k] = I['ev_c_re'][0][d, g].T; cc[64 * e:64 * e + 64, 1, d * 4 + k] = I['ev_c_im'][0][d, g].T
    esel = np.zeros((128, 3, 8, 128), np.float32)
    for r in range(3):
        for j in range(16):
            for sx in range(8):
                esel[32 * r + j, r, sx, sx * 16 + j] = 1.0
    msk = np.zeros((128, 2, 128), np.float32)
    sidx = np.arange(128) // 16
    msk[:, 0, :] = (sidx[None, :] >= sidx[:, None]); msk[:, 1, :] = (sidx[None, :] <= sidx[:, None])
    return dict(xin=xin, cvec=cvec, adaw=adaw, adab=adab, win=win, lam=lam, bb=bb, cc=cc, esel=esel, msk=msk)

def fsel_const():
    f = np.zeros((128, 8, 8, 128), np.float32)
    for l in range(8):
        for g in range(8):
            for i in range(16):
                f[l * 16 + i, g, l, g * 16 + i] = 1.0
    return f


def rope_tables():
    rows = 8192 // 64
    row = np.repeat(np.arange(rows), 64).astype(np.float32); col = np.tile(np.arange(64), rows).astype(np.float32)
    inv = (10000.0 ** (-np.arange(16, dtype=np.float32) / 16)).astype(np.float32)
    ang = np.concatenate([row[:, None] * inv, col[:, None] * inv], axis=-1).astype(np.float32)
    return np.cos(ang).astype(np.float32), np.sin(ang).astype(np.float32)

def l0b_inputs(I, b, sidx, ys, z, y5, u, g5):
    def rows(a):
        w = a.shape[1]
        out = np.zeros((128 + 2048, w), np.float32)
        out[0:64] = a[64 * sidx:64 * sidx + 64]
        out[128:] = a[256 + 2048 * sidx:256 + 2048 * (sidx + 1)]
        return out
    xall = np.concatenate([I['ctx'][b], I['x'][b]], 0)
    cvec = np.ascontiguousarray(np.stack([colv(I['c'][b]), colv(I['c_ctx'])], axis=2))
    reps = rep(np.concatenate([I['ev_ssd_norm'][0], I['ev_d_s5'][0], I['ev_glu_b'][0], I['od_q_gain'][0], I['od_k_gain'][0]]))
    cos, sin = rope_tables()
    rp = np.zeros((128 + 2048, 2, 32), np.float32); rp[:128, 0] = 1.0
    rp[128:, 0] = cos[2048 * sidx:2048 * (sidx + 1)]; rp[128:, 1] = sin[2048 * sidx:2048 * (sidx + 1)]
    rope = np.ascontiguousarray(rp.reshape(17, 128, 2, 32).transpose(1, 0, 2, 3))
    return dict(xres=rows(xall), ys=rows(ys), z=rows(z), y5=rows(y5), u=rows(u), g5=rows(g5), cvec=cvec,
                aw0g=kp(I['ada_w'][0][:, 2048:3072]), ab0g=rep(I['ada_b'][0][2048:3072]),
                adaw1=kp(I['ada_w'][1][:, :2048]), adab1=colv(I['ada_b'][1][:2048]), reps=reps,
                gluw=np.ascontiguousarray(I['ev_glu_w'][0].reshape(4, 128, 512).transpose(1, 0, 2)),
                wout=np.ascontiguousarray(I['ev_w_out'][0].reshape(12, 128, 1024).transpose(1, 0, 2)),
                w1=kp(I['od_w_in'][0]), rope=rope)

def l1b_inputs(I, b, sidx, q_rows, k_all, v_all, sg_rows, h1_rows):
    qz = np.zeros((4, 128, 16, 512), BF)
    qq = q_rows.reshape(4, 512, 16, 64)
    for h in range(16):
        e = (h // 4) % 2
        qz[:, 64 * e:64 * e + 64, h, :] = qq[:, :, h, :].transpose(0, 2, 1)
    kt = np.zeros((128, 2, 8448), BF)
    kk = k_all.reshape(8448, 4, 64)
    for kh in range(4):
        kt[64 * (kh % 2):64 * (kh % 2) + 64, kh // 2, :] = kk[:, kh, :].T
    vv = np.ones((128, 66, 4, 65), BF)
    vv[:, :, :, 0:64] = v_all.reshape(66, 128, 4, 64).transpose(1, 0, 2, 3)
    cvec = np.ascontiguousarray(np.stack([colv(I['c'][b]), colv(I['c_ctx'])], axis=2))
    return dict(qz=qz, kt=kt, vv=vv, sg=np.ascontiguousarray(sg_rows), h1=np.ascontiguousarray(h1_rows), wo=kp(I['od_w_out'][0]), cvec=cvec,
                aw1g=kp(I['ada_w'][1][:, 2048:3072]), ab1g=rep(I['ada_b'][1][2048:3072]), fg=rep(I['final_gain']))


_PROGS = {}


def _prog(name, fn):
    if name not in _PROGS:
        _PROGS[name] = fn()
    return _PROGS[name]


def _run(nc, in_maps):
    res = run_bass_kernel_spmd(nc, in_maps, core_ids=list(range(8)))
    return res.results


def kernel(**inputs):
    I = {k: np.asarray(v) for k, v in inputs.items()}
    per_b = [fused_inputs(I, b) for b in range(2)]
    res = _run(_prog('fused', build_fused), [per_b[ci // 4] for ci in range(8)])
    outs = np.zeros((2, NLAT, D), np.float32)
    for ci in range(8):
        outs[ci // 4, 2048 * (ci % 4):2048 * (ci % 4 + 1)] = res[ci]['out']
    return outs


def kernel_unfused(**inputs):
    I = {k: np.asarray(v) for k, v in inputs.items()}
    cores = [(b, q) for b in range(2) for q in range(4)]
    rA1 = _run(_prog('a1', build_l0a_ssd), [l0a_ssd_inputs(I, b, q) for (b, q) in cores])
    rA2 = _run(_prog('a2', build_l0a_s5), [l0a_s5_inputs(I, b, q) for (b, q) in cores])
    per_b = []
    for b in range(2):
        ys = np.concatenate([rA1[4 * b + q]['yssd'] for q in range(4)], axis=1)
        z = np.concatenate([rA1[4 * b + q]['ztok'][:, :256] for q in range(4)], axis=1)
        g5 = np.concatenate([rA2[4 * b + q]['gutok'][:, :128] for q in range(4)], axis=1)
        u = np.concatenate([rA2[4 * b + q]['gutok'][:, 128:] for q in range(4)], axis=1)
        y5 = np.concatenate([rA2[4 * b + q]['y5'].reshape(8, 8, 16, NC8).transpose(3, 1, 0, 2).reshape(T, 128) for q in range(4)], axis=1)
        per_b.append((ys, z, y5, u, g5))
    rB = _run(_prog('b', build_l0b), [l0b_inputs(I, b, sx, *per_b[b]) for (b, sx) in cores])
    outs = np.zeros((2, NLAT, D), np.float32)
    inC = []
    for b in range(2):
        k_all = np.concatenate([rB[4 * b + sx]['k'][:64] for sx in range(4)] + [rB[4 * b + sx]['k'][128:] for sx in range(4)], axis=0)
        v_all = np.concatenate([rB[4 * b + sx]['v'][:64] for sx in range(4)] + [rB[4 * b + sx]['v'][128:] for sx in range(4)], axis=0)
        for sx in range(4):
            r = rB[4 * b + sx]
            inC.append(l1b_inputs(I, b, sx, r['q'][128:], k_all, v_all, r['sg'][128:], r['h1'][128:]))
    rC = _run(_prog('c', build_l1b), inC)
    for ci, (b, sx) in enumerate(cores):
        outs[b, 2048 * sx:2048 * (sx + 1)] = rC[ci]['out']
    return outs
```
